# Optimizing a Trainium2 kernel written in Bass

```python
import math
import jax, jax.numpy as jnp
from jax import lax
import numpy as np

D_MODEL = 2048
BATCH = 1
SEQ = 8192
DEPTH = 4

N_A_LAYERS = DEPTH // 2
N_B_LAYERS = DEPTH - N_A_LAYERS
EPS = 1e-6
D_FF = ((8 * D_MODEL // 3 + 255) // 256) * 256
HEAD_DIM_A = 128
N_HEADS_A = D_MODEL // HEAD_DIM_A
DILATED_BRANCHES = ((128, 1), (512, 4), (2048, 16))
N_HEADS_B = D_MODEL // 128
QK_NOPE_DIM = 128
QK_ROPE_DIM = 64
V_HEAD_DIM = 128
KV_LORA_RANK = D_MODEL // 4
Q_LORA_RANK = D_MODEL // 4
ROPE_THETA = 10000.0
Q_BLOCK = 128

kernel_name = "yoco_dilated_swa_mla_macaron"


def rms_norm(x, g):
    xf = x.astype(jnp.float32)
    y = xf * lax.rsqrt(jnp.mean(xf * xf, axis=-1, keepdims=True) + EPS)
    return (y * g.astype(jnp.float32)).astype(x.dtype)


def swiglu(h, w_gate, w_up, w_down):
    return (jax.nn.silu(h @ w_gate) * (h @ w_up)) @ w_down


def alibi_slopes(n_heads):
    return jnp.asarray(2.0 ** (-8.0 * (np.arange(n_heads) + 1) / n_heads), dtype=jnp.float32)


def rope_tables(seq):
    inv = 1.0 / (ROPE_THETA ** (jnp.arange(0, QK_ROPE_DIM, 2, dtype=jnp.float32) / QK_ROPE_DIM))
    ang = jnp.arange(seq, dtype=jnp.float32)[:, None] * inv[None, :]
    return jnp.cos(ang), jnp.sin(ang)


def apply_rope(t, cos, sin):
    tf = t.astype(jnp.float32)
    t1, t2 = jnp.split(tf, 2, axis=-1)
    return jnp.concatenate([t1 * cos - t2 * sin, t1 * sin + t2 * cos], axis=-1).astype(t.dtype)


def dilated_branch(q, k, v, window, dilation, slopes):
    B, S, H, Dh = q.shape
    n = window // dilation
    span = n * dilation
    Sp = -(-S // span) * span
    nb = Sp // span

    def to_blocks(t):
        t = jnp.pad(t, ((0, 0), (0, Sp - S), (0, 0), (0, 0)))
        t = t.reshape(B, Sp // dilation, dilation, H, Dh).transpose(0, 2, 1, 3, 4)
        return t.reshape(B, dilation, nb, n, H, Dh)

    def with_prev(t):
        prev = jnp.pad(t[:, :, :-1], ((0, 0), (0, 0), (1, 0), (0, 0), (0, 0), (0, 0)))
        return jnp.concatenate([prev, t], axis=3)

    qb = to_blocks(q)
    kw = with_prev(to_blocks(k))
    vw = with_prev(to_blocks(v))
    s = jnp.einsum('brcihd,brcjhd->brchij', qb, kw, preferred_element_type=jnp.float32) * (Dh ** -0.5)
    i = jnp.arange(n)[:, None]
    j = jnp.arange(2 * n)[None, :]
    steps = n + i - j
    band = (steps >= 0) & (steps <= n)
    valid = band[None] & ((jnp.arange(nb)[:, None, None] > 0) | (j >= n)[None])
    bias = -slopes[:, None, None] * (dilation * steps).astype(jnp.float32)[None]
    s = jnp.where(valid[None, None, :, None], s + bias[None, None, None], -jnp.inf)
    m = jnp.max(s, axis=-1, keepdims=True)
    p = jnp.exp(s - m)
    l = jnp.sum(p, axis=-1, keepdims=True)
    o = jnp.einsum('brchij,brcjhd->brcihd', (p / l).astype(v.dtype), vw)
    lse = (m + jnp.log(l))[..., 0]
    o = o.reshape(B, dilation, Sp // dilation, H, Dh).transpose(0, 2, 1, 3, 4).reshape(B, Sp, H, Dh)[:, :S]
    lse = lse.transpose(0, 1, 2, 4, 3).reshape(B, dilation, Sp // dilation, H)
    lse = lse.transpose(0, 2, 1, 3).reshape(B, Sp, H)[:, :S]
    return o, lse


def dilated_attention(h, w_qkv, w_o, slopes):
    B, S, _ = h.shape
    qkv = (h @ w_qkv).reshape(B, S, 3, N_HEADS_A, HEAD_DIM_A)
    q, k, v = qkv[:, :, 0], qkv[:, :, 1], qkv[:, :, 2]
    outs, lses = [], []
    for window, dilation in DILATED_BRANCHES:
        o, lse = dilated_branch(q, k, v, window, dilation, slopes)
        outs.append(o)
        lses.append(lse)
    wts = jax.nn.softmax(jnp.stack(lses, axis=0), axis=0)
    o = jnp.einsum('gbsh,gbshd->bshd', wts.astype(q.dtype), jnp.stack(outs, axis=0))
    return o.reshape(B, S, N_HEADS_A * HEAD_DIM_A) @ w_o


def mla_shared_kv(x, kv_norm, b_wdkv, b_ckv_norm, b_wkr, b_wuk, b_wuv, cos, sin):
    h = rms_norm(x, kv_norm)
    c_kv = rms_norm(h @ b_wdkv, b_ckv_norm)
    k_nope = jnp.einsum('bsc,chd->bshd', c_kv, b_wuk)
    v = jnp.einsum('bsc,chd->bshd', c_kv, b_wuv)
    k_rope = apply_rope(h @ b_wkr, cos, sin)
    return k_nope, k_rope, v


def mla_attention(h, k_nope, k_rope, v, w_dq, cq_norm, w_uq, w_o, cos, sin):
    B, S, _ = h.shape
    c_q = rms_norm(h @ w_dq, cq_norm)
    q = jnp.einsum('bsc,chd->bshd', c_q, w_uq)
    q_nope = q[..., :QK_NOPE_DIM]
    q_rope = apply_rope(q[..., QK_NOPE_DIM:], cos[:, None, :], sin[:, None, :])
    nb = S // Q_BLOCK
    qn_b = q_nope.reshape(B, nb, Q_BLOCK, N_HEADS_B, QK_NOPE_DIM).transpose(1, 0, 2, 3, 4)
    qr_b = q_rope.reshape(B, nb, Q_BLOCK, N_HEADS_B, QK_ROPE_DIM).transpose(1, 0, 2, 3, 4)
    starts = jnp.arange(nb, dtype=jnp.int32) * Q_BLOCK
    scale = (QK_NOPE_DIM + QK_ROPE_DIM) ** -0.5
    kpos = jnp.arange(S, dtype=jnp.int32)

    def attend(args):
        qn, qr, start = args
        s = (jnp.einsum('bihd,bjhd->bhij', qn, k_nope, preferred_element_type=jnp.float32)
             + jnp.einsum('bihr,bjr->bhij', qr, k_rope, preferred_element_type=jnp.float32)) * scale
        qpos = start + jnp.arange(Q_BLOCK, dtype=jnp.int32)
        s = jnp.where(kpos[None, :] <= qpos[:, None], s, -jnp.inf)
        p = jax.nn.softmax(s, axis=-1)
        return jnp.einsum('bhij,bjhd->bihd', p.astype(v.dtype), v)

    o = lax.map(attend, (qn_b, qr_b, starts))
    o = o.transpose(1, 0, 2, 3, 4).reshape(B, S, N_HEADS_B * V_HEAD_DIM)
    return o @ w_o


def setup_inputs(seed: int = 0) -> dict:
    key = jax.random.key(seed)
    ks = jax.random.split(key, 24)
    f32 = jnp.float32

    def w(k, shape, fan_in):
        return jax.random.normal(k, shape, f32) * (fan_in ** -0.5)

    def gain(k, shape):
        return 1.0 + 0.01 * jax.random.normal(k, shape, f32)

    D, F = D_MODEL, D_FF
    return {
        "x": jax.random.normal(ks[0], (BATCH, SEQ, D), f32),
        "ffn_norm1": gain(ks[1], (DEPTH, D)),
        "ffn1_wg": w(ks[2], (DEPTH, D, F), D),
        "ffn1_wu": w(ks[3], (DEPTH, D, F), D),
        "ffn1_wd": w(ks[4], (DEPTH, F, D), F),
        "mix_norm": gain(ks[5], (DEPTH, D)),
        "ffn_norm2": gain(ks[6], (DEPTH, D)),
        "ffn2_wg": w(ks[7], (DEPTH, D, F), D),
        "ffn2_wu": w(ks[8], (DEPTH, D, F), D),
        "ffn2_wd": w(ks[9], (DEPTH, F, D), F),
        "a_wqkv": w(ks[10], (N_A_LAYERS, D, 3 * N_HEADS_A * HEAD_DIM_A), D),
        "a_wo": w(ks[11], (N_A_LAYERS, N_HEADS_A * HEAD_DIM_A, D), N_HEADS_A * HEAD_DIM_A),
        "kv_norm": gain(ks[12], (D,)),
        "b_wdkv": w(ks[13], (D, KV_LORA_RANK), D),
        "b_ckv_norm": gain(ks[14], (KV_LORA_RANK,)),
        "b_wkr": w(ks[15], (D, QK_ROPE_DIM), D),
        "b_wuk": w(ks[16], (KV_LORA_RANK, N_HEADS_B, QK_NOPE_DIM), KV_LORA_RANK),
        "b_wuv": w(ks[17], (KV_LORA_RANK, N_HEADS_B, V_HEAD_DIM), KV_LORA_RANK),
        "b_wdq": w(ks[18], (N_B_LAYERS, D, Q_LORA_RANK), D),
        "b_cq_norm": gain(ks[19], (N_B_LAYERS, Q_LORA_RANK)),
        "b_wuq": w(ks[20], (N_B_LAYERS, Q_LORA_RANK, N_HEADS_B, QK_NOPE_DIM + QK_ROPE_DIM), Q_LORA_RANK),
        "b_wo": w(ks[21], (N_B_LAYERS, N_HEADS_B * V_HEAD_DIM, D), N_HEADS_B * V_HEAD_DIM),
        "final_norm": gain(ks[22], (D,)),
    }


def reference(x, ffn_norm1, ffn1_wg, ffn1_wu, ffn1_wd, mix_norm, ffn_norm2, ffn2_wg, ffn2_wu, ffn2_wd,
              a_wqkv, a_wo, kv_norm, b_wdkv, b_ckv_norm, b_wkr, b_wuk, b_wuv,
              b_wdq, b_cq_norm, b_wuq, b_wo, final_norm):
    S = x.shape[1]
    slopes = alibi_slopes(N_HEADS_A)
    cos, sin = rope_tables(S)
    k_nope = k_rope = v_shared = None
    for layer in range(DEPTH):
        if layer == N_A_LAYERS:
            k_nope, k_rope, v_shared = mla_shared_kv(x, kv_norm, b_wdkv, b_ckv_norm, b_wkr, b_wuk, b_wuv, cos, sin)
        x = x + 0.5 * swiglu(rms_norm(x, ffn_norm1[layer]), ffn1_wg[layer], ffn1_wu[layer], ffn1_wd[layer])
        h = rms_norm(x, mix_norm[layer])
        if layer < N_A_LAYERS:
            x = x + dilated_attention(h, a_wqkv[layer], a_wo[layer], slopes)
        else:
            jb = layer - N_A_LAYERS
            x = x + mla_attention(h, k_nope, k_rope, v_shared, b_wdq[jb], b_cq_norm[jb], b_wuq[jb], b_wo[jb], cos, sin)
        x = x + 0.5 * swiglu(rms_norm(x, ffn_norm2[layer]), ffn2_wg[layer], ffn2_wu[layer], ffn2_wd[layer])
    return rms_norm(x, final_norm)
```

```python
import numpy as np
import ml_dtypes
import concourse.bass as bass
import concourse.mybir as mybir
from concourse.bass_utils import run_bass_kernel_spmd

F32 = mybir.dt.float32
BF16 = mybir.dt.bfloat16
AF = mybir.ActivationFunctionType
ALU = mybir.AluOpType

NCORES = 8
D = 2048
S = 8192
T = S // NCORES
KC = D // 128
DFF = 5632
FC = DFF // 128
FH = FC // 2
EPS = 1e-6
DEPTH = 4


class Sem:
    def __init__(self, h):
        self.h = h
        self.n = 0


class Prog:
    def __init__(self, nc):
        self.nc = nc
        self.q = {"pe": [], "act": [], "dve": [], "pool": [], "sp": []}

    def op(self, eng, name, *args, inc=None, incv=None, **kw):
        if inc is not None:
            v = incv if incv is not None else 1
            inc.n += v
            h = inc.h

            def f(e, name=name, args=args, kw=kw, h=h, v=v):
                getattr(e, name)(*args, **kw).then_inc(h, v)
        else:
            def f(e, name=name, args=args, kw=kw):
                getattr(e, name)(*args, **kw)
        self.q[eng].append(f)

    def dma(self, eng, out, in_, sem):
        self.op(eng, "dma_start", out=out, in_=in_, inc=sem, incv=16)

    def wait(self, eng, sem, val):
        if val <= 0:
            return
        h = sem.h
        self.q[eng].append(lambda e, h=h, val=val: e.wait_ge(h, val))

    def run(self):
        with self.nc.Block() as block:
            block.tensor(lambda e: [f(e) for f in self.q["pe"]])
            block.scalar(lambda e: [f(e) for f in self.q["act"]])
            block.vector(lambda e: [f(e) for f in self.q["dve"]])
            block.gpsimd(lambda e: [f(e) for f in self.q["pool"]])
            block.sync(lambda e: [f(e) for f in self.q["sp"]])


class Ctx:
    def __init__(self, nc, stack):
        self.nc = nc
        self.stack = stack
        self.k = 0

    def sbuf(self, shape, dt, name=None):
        self.k += 1
        return self.stack.enter_context(self.nc.sbuf_tensor(name or f"sb{self.k}", list(shape), dt))

    def psum(self, shape, dt=F32, name=None):
        self.k += 1
        return self.stack.enter_context(self.nc.psum_tensor(name or f"ps{self.k}", list(shape), dt))

    def sem(self, name=None):
        self.k += 1
        return Sem(self.stack.enter_context(self.nc.semaphore(name or f"sem{self.k}")))


class TokenPhase:
    def __init__(self, nc, P, C):
        self.nc, self.P, self.C = nc, P, C
        self.xT = C.sbuf([128, KC, T], F32, "xT")
        self.hT = C.sbuf([128, KC, T], BF16, "hT")
        self.aT = C.sbuf([128, FH, T], BF16, "aT")
        self.scr = C.sbuf([128, 4, 512], F32, "scr")
        self.ones = C.sbuf([128, 128], F32, "ones")
        self.rstd = C.sbuf([128, 512], F32, "rstd")
        self.sg = [C.sbuf([128, 512], F32, f"sg{i}") for i in range(2)]
        self.wg = [C.sbuf([128, KC, 128], BF16, f"wg{i}") for i in range(2)]
        self.wu = [C.sbuf([128, KC, 128], BF16, f"wu{i}") for i in range(2)]
        self.wd = [C.sbuf([128, FH, 128], BF16, f"wd{i}") for i in range(2)]
        self.bank = [C.psum([128, 512], F32, f"bk{i}") for i in range(8)]
        self.s_init = C.sem("s_init")
        self.s_x = C.sem("s_x")
        self.s_sq = C.sem("s_sq")
        self.s_st = C.sem("s_st")
        self.s_sqrt = C.sem("s_sqrt")
        self.s_rs = C.sem("s_rs")
        self.s_h = C.sem("s_h")
        self.s_wgu = C.sem("s_wgu")
        self.s_gu = C.sem("s_gu")
        self.s_sl = C.sem("s_sl")
        self.s_a = C.sem("s_a")
        self.s_wd = C.sem("s_wd")
        self.s_dn = C.sem("s_dn")
        self.s_y = C.sem("s_y")
        self.s_out = C.sem("s_out")
        self.n_wgu = self.n_wd = self.n_gu = self.n_dn = self.n_hn = self.n_sq = 0
        P.op("dve", "memset", self.ones[:, :], 1.0, inc=self.s_init)
        P.wait("pe", self.s_init, 1)

    def load(self, dst, src):
        self.P.dma("sp", dst, src, self.s_x)

    def load_x(self, x_dram):
        xv = x_dram.rearrange("(k p) t -> p k t", p=128)
        for k in range(0, KC, 4):
            self.load(self.xT[:, k:k + 4, :], xv[:, k:k + 4, :])

    def store_x(self, y_dram):
        P = self.P
        yv = y_dram.rearrange("(k p) t -> p k t", p=128)
        P.wait("sp", self.s_y, self.s_y.n)
        for k in range(0, KC, 4):
            P.dma("sp", yv[:, k:k + 4, :], self.xT[:, k:k + 4, :], self.s_out)

    def finish(self):
        self.P.wait("sp", self.s_out, self.s_out.n)

    def norm(self, gcol, out_tile, src=None, nk=KC):
        P = self.P
        src = self.xT if src is None else src
        for e in ("act", "dve"):
            P.wait(e, self.s_x, self.s_x.n)
            P.wait(e, self.s_y, self.s_y.n)
        for th in range(2):
            n = self.n_hn
            self.n_hn += 1
            tsl = slice(th * 512, (th + 1) * 512)
            P.wait("pe", self.s_sqrt, n)
            for k in range(nk):
                q = self.n_sq
                self.n_sq += 1
                P.wait("act", self.s_st, q - 3)
                P.op("act", "activation", out=self.scr[:, q % 4, :], in_=src[:, k, tsl],
                     func=AF.Square, inc=self.s_sq)
                P.wait("pe", self.s_sq, q + 1)
                P.op("pe", "matmul", self.bank[7][:, :], self.ones[:, :], self.scr[:, q % 4, :],
                     start=(k == 0), stop=(k == nk - 1), inc=self.s_st)
            P.wait("act", self.s_st, self.n_sq)
            P.wait("act", self.s_h, self.s_h.n)
            P.op("act", "activation", out=self.rstd[:, :], in_=self.bank[7][:, :], func=AF.Sqrt,
                 scale=1.0 / (128 * nk), bias=self.epsb[:, 0:1], inc=self.s_sqrt)
            P.wait("dve", self.s_sqrt, n + 1)
            P.op("dve", "reciprocal", self.rstd[:, :], self.rstd[:, :], inc=self.s_rs)
            P.wait("dve", self.s_rs, n + 1)
            for k in range(nk):
                P.op("dve", "scalar_tensor_tensor", out=out_tile[:, k, tsl], in0=src[:, k, tsl],
                     scalar=gcol[:, k:k + 1], op0=ALU.mult, in1=self.rstd[:, :], op1=ALU.mult,
                     inc=self.s_h)

    def ffn(self, wg_d, wu_d, wd_d):
        P = self.P
        h_ready = self.s_h.n
        P.wait("dve", self.s_out, self.s_out.n)
        for fh in range(2):
            gu0 = self.n_gu
            P.wait("pe", self.s_y, self.s_y.n)
            P.wait("pe", self.s_h, h_ready)
            for j in range(FH):
                fc = fh * FH + j
                c = self.n_wgu
                self.n_wgu += 1
                b = c % 2
                P.wait("pool", self.s_gu, 2 * (c - 1))
                P.dma("pool", self.wg[b][:, :, :], wg_d[fc], self.s_wgu)
                P.dma("pool", self.wu[b][:, :, :], wu_d[fc], self.s_wgu)
                P.wait("pe", self.s_wgu, 32 * (c + 1))
                for th in range(2):
                    n = self.n_gu
                    self.n_gu += 1
                    pr = n % 4
                    gps, ups = self.bank[2 * pr], self.bank[2 * pr + 1]
                    tsl = slice(th * 512, (th + 1) * 512)
                    if n - gu0 >= 4:
                        P.wait("pe", self.s_a, n - 3)
                    for k in range(KC):
                        P.op("pe", "matmul", gps[:, :], self.wg[b][:, k, :], self.hT[:, k, tsl],
                             start=(k == 0), stop=(k == KC - 1))
                    for k in range(KC):
                        last = k == KC - 1
                        P.op("pe", "matmul", ups[:, :], self.wu[b][:, k, :], self.hT[:, k, tsl],
                             start=(k == 0), stop=last, inc=self.s_gu if last else None)
                    P.wait("act", self.s_gu, n + 1)
                    P.wait("act", self.s_a, n - 1)
                    P.op("act", "activation", out=self.sg[n % 2][:, :], in_=gps[:, :], func=AF.Silu,
                         inc=self.s_sl)
                    P.wait("dve", self.s_sl, n + 1)
                    P.op("dve", "tensor_tensor", out=self.aT[:, j, tsl], in0=ups[:, :],
                         in1=self.sg[n % 2][:, :], op=ALU.mult, inc=self.s_a)
            P.wait("pe", self.s_a, self.n_gu)
            self.proj_units(self.aT, FH, [wd_d[fh, dc] for dc in range(KC)], 128,
                            lambda dc, th, tsl, bk, inc: P.op(
                                "dve", "scalar_tensor_tensor", out=self.xT[:, dc, tsl], in0=bk[:, :], scalar=0.5,
                                op0=ALU.mult, in1=self.xT[:, dc, tsl], op1=ALU.add, inc=inc))

    def proj_units(self, src, nk, w_list, ncols, evac):
        P = self.P
        P.wait("pe", self.s_sqrt, self.s_sqrt.n)
        P.wait("pe", self.s_h, self.s_h.n)
        P.wait("pe", self.s_x, self.s_x.n)
        for oc, w_ap in enumerate(w_list):
            c = self.n_wd
            self.n_wd += 1
            b = c % 2
            P.wait("pool", self.s_dn, 2 * (c - 1))
            P.dma("pool", self.wd[b][:, 0:nk, 0:ncols], w_ap, self.s_wd)
            P.wait("pe", self.s_wd, 16 * (c + 1))
            for th in range(2):
                m = self.n_dn
                self.n_dn += 1
                bk = self.bank[m % 8]
                tsl = slice(th * 512, (th + 1) * 512)
                P.wait("pe", self.s_y, m - 7)
                for j in range(nk):
                    last = j == nk - 1
                    P.op("pe", "matmul", bk[0:ncols, :], self.wd[b][:, j, 0:ncols], src[:, j, tsl],
                         start=(j == 0), stop=last, inc=(self.s_dn if last else None))
                P.wait("dve", self.s_dn, m + 1)
                evac(oc, th, tsl, bk, self.s_y)

    def add_proj(self, src, nk, w_list):
        P = self.P
        self.proj_units(src, nk, w_list, 128,
                        lambda oc, th, tsl, bk, inc: P.op(
                            "dve", "tensor_tensor", out=self.xT[:, oc, tsl], in0=bk[:, :],
                            in1=self.xT[:, oc, tsl], op=ALU.add, inc=inc))

    def proj_to(self, src, nk, w_list, dst):
        P = self.P
        self.proj_units(src, nk, w_list, 128,
                        lambda oc, th, tsl, bk, inc: P.op(
                            "dve", "tensor_copy", out=dst[:, oc, tsl], in_=bk[:, :], inc=inc))


def tile_wgu(w):
    return np.ascontiguousarray(w.reshape(KC, 128, FC, 128).transpose(2, 1, 0, 3))


def tile_wd(w):
    return np.ascontiguousarray(w.reshape(2, FH, 128, KC, 128).transpose(0, 3, 2, 1, 4))


def gcol_of(g):
    return np.ascontiguousarray(g.reshape(KC, 128).T)


def build_ffn_program():
    from contextlib import ExitStack
    nc = bass.Bass("TRN2", target_bir_lowering=False)
    x_d = nc.dram_tensor("xT_in", [D, T], F32, kind="ExternalInput").ap()
    g_d = nc.dram_tensor("gcol", [128, KC], F32, kind="ExternalInput").ap()
    wg_d = nc.dram_tensor("wg", [FC, 128, KC, 128], F32, kind="ExternalInput").ap()
    wu_d = nc.dram_tensor("wu", [FC, 128, KC, 128], F32, kind="ExternalInput").ap()
    wd_d = nc.dram_tensor("wd", [2, KC, 128, FH, 128], F32, kind="ExternalInput").ap()
    y_d = nc.dram_tensor("xT_out", [D, T], F32, kind="ExternalOutput").ap()
    with ExitStack() as st:
        C = Ctx(nc, st)
        P = Prog(nc)
        tp = TokenPhase(nc, P, C)
        gcol = C.sbuf([128, KC], F32, "gcol_sb")
        tp.epsb = C.sbuf([128, 1], F32, "epsb")
        P.op("dve", "memset", tp.epsb[:, :], EPS, inc=tp.s_init)
        P.wait("act", tp.s_init, 2)
        tp.load_x(x_d)
        tp.load(gcol[:, :], g_d[:, :])
        tp.norm(gcol, tp.hT)
        tp.ffn(wg_d, wu_d, wd_d)
        tp.store_x(y_d)
        tp.finish()
        P.run()
    return nc


DILS = (1, 4, 16)


def ss(start, n, step):
    return slice(start, start + (n - 1) * step + 1, step)

BLK = 2048
NB = S // BLK


def alibi_masks(heads):
    k = np.arange(128)[:, None].astype(np.float64)
    j = np.arange(128)[None, :].astype(np.float64)
    out = np.zeros((3, len(heads), 128, 256), np.float32)
    for di, d in enumerate(DILS):
        for hi, h in enumerate(heads):
            slope = 2.0 ** (-8.0 * (h + 1) / 16)
            lo = np.where(j >= k, np.exp(-slope * d * np.maximum(j - k, 0.0)), 0.0)
            hi_ = np.where(j <= k, np.exp(-slope * d * np.maximum(128 + j - k, 0.0)), 0.0)
            out[di, hi, :, 0:128] = hi_
            out[di, hi, :, 128:256] = lo
    return out


def build_attn_a_program():
    from contextlib import ExitStack
    nc = bass.Bass("TRN2", target_bir_lowering=False)
    NT = S // 512
    h_d = nc.dram_tensor("hT_all", [NT, 128, KC, 512], BF16, kind="ExternalInput").ap()
    wq_d = nc.dram_tensor("wq", [128, KC, 256], F32, kind="ExternalInput").ap()
    wk_d = nc.dram_tensor("wk", [128, KC, 256], F32, kind="ExternalInput").ap()
    wv_d = nc.dram_tensor("wv", [128, KC, 256], F32, kind="ExternalInput").ap()
    em_d = nc.dram_tensor("emask", [3, 2, 128, 256], F32, kind="ExternalInput").ap()
    o_d = nc.dram_tensor("oT", [256, S], BF16, kind="ExternalOutput").ap()
    vd = nc.dram_tensor("v_scratch", [S, 256], BF16).ap()
    scale = 128.0 ** -0.5
    with ExitStack() as st:
        C = Ctx(nc, st)
        P = Prog(nc)
        wq = C.sbuf([128, KC, 256], BF16, "wq_sb")
        wk = C.sbuf([128, KC, 256], BF16, "wk_sb")
        wv = C.sbuf([128, KC, 256], BF16, "wv_sb")
        em = C.sbuf([128, 3, 2, 256], F32, "em_sb")
        QT = C.sbuf([128, 2, S], BF16, "QT")
        KT = C.sbuf([128, 2, S], BF16, "KT")
        hbuf = [C.sbuf([128, KC, 512], BF16, f"hbuf{i}") for i in range(2)]
        vst = [C.sbuf([128, 2, 256], BF16, f"vst{i}") for i in range(2)]
        vn = [C.sbuf([128, 16, 256], BF16, f"vn{i}") for i in range(2)]
        v4 = [C.sbuf([128, 4, 4, 256], BF16, f"v4{i}") for i in range(2)]
        v16 = [C.sbuf([128, 16, 256], BF16, f"v16{i}") for i in range(2)]
        vn.append(hbuf[0][:, 0:8, :].rearrange("p a (b c) -> p (a b) c", c=256))
        v4.append(hbuf[0][:, 8:16, :].rearrange("p (r a) (b c) -> p r (a b) c", r=4, c=256))
        v16.append(hbuf[1][:, 0:8, :].rearrange("p a (b c) -> p (a b) c", c=256))
        NV = 3
        oacc = C.sbuf([128, BLK], F32, "oacc")
        lacc = C.sbuf([128, BLK], F32, "lacc")
        NP = 3
        pbuf = [C.sbuf([128, 512], F32, f"pbuf{i}") for i in range(NP)]
        ptb = [C.sbuf([128, 512], BF16, f"ptb{i}") for i in range(NP)]
        onesb = C.sbuf([128, 128], BF16, "onesb")
        oout = [C.sbuf([128, BLK], BF16, "oout0")]
        bank = [C.psum([128, 512], F32, f"bk{i}") for i in range(8)]
        s_w, s_hb, s_pj, s_evA, s_evD, s_vst, s_vl = (C.sem(n) for n in
                                                      ("s_w", "s_hb", "s_pj", "s_evA", "s_evD", "s_vst", "s_vl"))
        s_init, s_qk, s_ex, s_ptA, s_ptB, s_pv, s_eA, s_eD, s_fin, s_out = (
            C.sem(n) for n in ("s_init", "s_qk", "s_ex", "s_ptA", "s_ptB", "s_pv", "s_eA", "s_eD", "s_fin", "s_out"))
        P.op("dve", "memset", onesb[:, :], 1.0, inc=s_init)
        P.dma("pool", wq[:, :, :], wq_d, s_w)
        P.dma("pool", wk[:, :, :], wk_d, s_w)
        P.dma("pool", wv[:, :, :], wv_d, s_w)
        P.dma("pool", em[:, :, :, :], em_d.rearrange("d h p c -> p d h c"), s_w)
        vdt = vd.rearrange("(n p) c -> p n c", p=128)
        P.wait("pe", s_w, 64)
        P.wait("pe", s_init, 1)
        P.wait("pool", s_w, 64)
        units = []
        n_evA = n_evD = 0
        for tt in range(NT):
            hb = hbuf[tt % 2]
            if tt >= 2:
                P.wait("sp", s_pj, 6 * (tt - 1))
            P.dma("sp", hb[:, 0:KC // 2, :], h_d[tt, :, 0:KC // 2, :], s_hb)
            P.dma("sp", hb[:, KC // 2:KC, :], h_d[tt, :, KC // 2:KC, :], s_hb)
            P.wait("pe", s_hb, 32 * (tt + 1))
            for ui in range(6):
                u = len(units)
                bk = bank[u % 8]
                if u >= 8:
                    P.wait("pe", units[u - 8][0], units[u - 8][1])
                if ui < 4:
                    w = wq if ui < 2 else wk
                    hd = ui % 2
                    for k in range(KC):
                        P.op("pe", "matmul", bk[:, :], w[:, k, hd * 128:(hd + 1) * 128], hb[:, k, :],
                             start=(k == 0), stop=(k == KC - 1), inc=s_pj if k == KC - 1 else None)
                    dst = (QT if ui < 2 else KT)[:, hd, tt * 512:(tt + 1) * 512]
                    P.wait("act", s_pj, u + 1)
                    P.op("act", "activation", out=dst, in_=bk[:, :], func=AF.Copy, inc=s_evA)
                    n_evA += 1
                    units.append((s_evA, n_evA))
                else:
                    sp_ = ui - 4
                    for si in range(2):
                        sub = sp_ * 2 + si
                        for k in range(KC):
                            last = (k == KC - 1) and si == 1
                            P.op("pe", "matmul", bk[:, si * 256:(si + 1) * 256], hb[:, k, sub * 128:(sub + 1) * 128],
                                 wv[:, k, :], start=(k == 0), stop=(k == KC - 1), inc=s_pj if last else None)
                    vb = vst[n_evD % 2]
                    P.wait("dve", s_pj, u + 1)
                    P.wait("dve", s_vst, 16 * (n_evD - 1))
                    P.op("dve", "tensor_copy", out=vb[:, :, :], in_=bk[:, :].rearrange("p (s c) -> p s c", c=256),
                         inc=s_evD)
                    n_evD += 1
                    units.append((s_evD, n_evD))
                    P.wait("pool", s_evD, n_evD)
                    n0 = tt * 4 + sp_ * 2
                    P.dma("pool", vdt[:, n0:n0 + 2, :], vb[:, :, :], s_vst)
        v4d = vd.rearrange("(blk i r) c -> i r blk c", i=128, r=4)
        v16d = vd.rearrange("(blk i r) c -> i r blk c", i=128, r=16)
        P.wait("sp", s_vst, s_vst.n)
        P.wait("pe", s_evA, n_evA)
        groups = []
        for B in range(NB):
            b = B % NV
            pb = (B - 1) % NV
            for hd in range(2):
                hs = slice(hd * 128, (hd + 1) * 128)
                tiles = []
                for ml in range(16):
                    prev = vn[b][:, ml - 1, hs] if ml > 0 else (vn[pb][:, 15, hs] if B > 0 else None)
                    tiles.append((0, 1, B * BLK + ml * 128, vn[b][:, ml, hs], prev, ml * 128))
                for r in range(4):
                    for ml in range(4):
                        prev = v4[b][:, r, ml - 1, hs] if ml > 0 else (v4[pb][:, r, 3, hs] if B > 0 else None)
                        tiles.append((1, 4, B * BLK + r + 4 * 128 * ml, v4[b][:, r, ml, hs], prev, r + 4 * 128 * ml))
                for r in range(16):
                    prev = v16[pb][:, r, hs] if B > 0 else None
                    tiles.append((2, 16, B * BLK + r, v16[b][:, r, hs], prev, r))
                for gi in range(0, len(tiles), 2):
                    groups.append(dict(B=B, hd=hd, tiles=tiles[gi:gi + 2], first=(gi == 0),
                                       last=(gi == len(tiles) - 2)))
        G = len(groups)
        ev_done = []
        st_ = dict(n_eA=0, n_eD=0, n_fin=0, last_d1=(None, 0))
        vl_loaded = set()

        def ensure_v(B):
            if B in vl_loaded or B >= NB:
                return
            vl_loaded.add(B)
            b = B % NV
            if B >= 2:
                P.wait("sp", s_pv, blk_end[B - 2])
                P.wait("sp", s_pj, s_pj.n)
            for q4 in range(4):
                P.dma("sp", vn[b][:, q4 * 4:(q4 + 1) * 4, :], vdt[:, B * 16 + q4 * 4:B * 16 + (q4 + 1) * 4, :], s_vl)
                P.dma("sp", v4[b][:, q4, :, :], v4d[:, q4, 4 * B:4 * B + 4, :], s_vl)
                P.dma("sp", v16[b][:, q4 * 4:(q4 + 1) * 4, :], v16d[:, q4 * 4:(q4 + 1) * 4, B, :], s_vl)

        blk_end = {}
        for B in range(NB):
            blk_end[B] = sum(1 for g_ in groups if g_["B"] <= B)

        def emit_qk(g):
            gr = groups[g]
            hd = gr["hd"]
            di, d = gr["tiles"][0][0], gr["tiles"][0][1]
            sb_ = bank[g % NP]
            P.wait("pe", s_ex, g - NP + 1)
            for ti in range(2):
                _, _, t0, vcur, vprev, _ = gr["tiles"][ti]
                qap = QT[:, hd, ss(t0, 128, d)]
                if vprev is not None:
                    P.op("pe", "matmul", sb_[:, ti * 256:ti * 256 + 128], KT[:, hd, ss(t0 - 128 * d, 128, d)], qap,
                         start=True, stop=True)
                P.op("pe", "matmul", sb_[:, ti * 256 + 128:ti * 256 + 256], KT[:, hd, ss(t0, 128, d)], qap,
                     start=True, stop=True, inc=s_qk if ti == 1 else None)
            P.wait("act", s_qk, g + 1)
            P.wait("act", s_ptA, g - NP + 1)
            P.wait("act", s_ptB, g - NP + 1)
            P.op("act", "activation", out=pbuf[g % NP][:, :], in_=sb_[:, :], func=AF.Exp, scale=scale, inc=s_ex)
            for eng, sem_, c0 in (("dve", s_ptA, 0), ("pool", s_ptB, 256)):
                P.wait(eng, s_ex, g + 1)
                P.wait(eng, s_pv, g - NP + 1)
                P.op(eng, "tensor_tensor", out=ptb[g % NP][:, c0:c0 + 256], in0=pbuf[g % NP][:, c0:c0 + 256],
                     in1=em[:, di, hd, :], op=ALU.mult, inc=sem_)

        def emit_pv(g):
            gr = groups[g]
            hd, B = gr["hd"], gr["B"]
            di, d = gr["tiles"][0][0], gr["tiles"][0][1]
            ob, lb = bank[NP + 2 * (g % 2)], bank[NP + 1 + 2 * (g % 2)]
            pt = ptb[g % NP]
            if gr["first"] and hd == 0:
                P.wait("pe", s_vl, 16 * 12 * (B + 1))
            P.wait("pe", s_ptA, g + 1)
            P.wait("pe", s_ptB, g + 1)
            if g >= 2:
                P.wait("pe", ev_done[g - 2][0], ev_done[g - 2][1])
            for kind in (0, 1):
                dstb = ob if kind == 0 else lb
                for ti in range(2):
                    _, _, t0, vcur, vprev, _ = gr["tiles"][ti]
                    oc = slice(ti * 128, (ti + 1) * 128)
                    last = kind == 1 and ti == 1
                    if vprev is not None:
                        P.op("pe", "matmul", dstb[:, oc], vprev if kind == 0 else onesb[:, :],
                             pt[:, ti * 256:ti * 256 + 128], start=True, stop=False)
                    P.op("pe", "matmul", dstb[:, oc], vcur if kind == 0 else onesb[:, :],
                         pt[:, ti * 256 + 128:ti * 256 + 256], start=(vprev is None), stop=True,
                         inc=s_pv if last else None)
            if di == 0:
                P.wait("act", s_pv, g + 1)
                if gr["first"]:
                    P.wait("act", s_fin, st_["n_fin"])
                for ti in range(2):
                    loc = gr["tiles"][ti][5]
                    oc = slice(ti * 128, (ti + 1) * 128)
                    dsl = ss(loc, 128, d)
                    P.op("act", "activation", out=oacc[:, dsl], in_=ob[:, oc], func=AF.Copy)
                    P.op("act", "activation", out=lacc[:, dsl], in_=lb[:, oc], func=AF.Copy,
                         inc=s_eA if ti == 1 else None)
                st_["n_eA"] += 1
                ev_done.append((s_eA, st_["n_eA"]))
                st_["last_d1"] = (s_eA, st_["n_eA"])
            else:
                P.wait("dve", s_pv, g + 1)
                P.wait("dve", st_["last_d1"][0], st_["last_d1"][1])
                for ti in range(2):
                    loc = gr["tiles"][ti][5]
                    oc = slice(ti * 128, (ti + 1) * 128)
                    dsl = ss(loc, 128, d)
                    P.op("dve", "tensor_tensor", out=oacc[:, dsl], in0=ob[:, oc], in1=oacc[:, dsl], op=ALU.add)
                    P.op("dve", "tensor_tensor", out=lacc[:, dsl], in0=lb[:, oc], in1=lacc[:, dsl], op=ALU.add,
                         inc=s_eD if ti == 1 else None)
                st_["n_eD"] += 1
                ev_done.append((s_eD, st_["n_eD"]))
            if gr["last"]:
                nf = st_["n_fin"]
                ob_ = oout[0]
                P.wait("dve", s_out, 16 * nf)
                P.op("dve", "reciprocal", lacc[:, :], lacc[:, :])
                P.op("dve", "tensor_tensor", out=ob_[:, :], in0=oacc[:, :], in1=lacc[:, :], op=ALU.mult, inc=s_fin)
                st_["n_fin"] += 1
                P.wait("sp", s_fin, st_["n_fin"])
                P.dma("sp", o_d[hd * 128:(hd + 1) * 128, B * BLK:(B + 1) * BLK], ob_[:, :], s_out)
                if hd == 1:
                    ensure_v(B + 2)

        ensure_v(0)
        ensure_v(1)
        emit_qk(0)
        for g in range(G):
            if g + 1 < G:
                emit_qk(g + 1)
            emit_pv(g)
        P.wait("sp", s_out, s_out.n)
        P.run()
    return nc


def tile_hT(hT_all):
    return np.ascontiguousarray(hT_all.reshape(KC, 128, S // 512, 512).transpose(2, 1, 0, 3))


LC = 4


def build_mla_program():
    from contextlib import ExitStack
    nc = bass.Bass("TRN2", target_bir_lowering=False)
    cq_d = nc.dram_tensor("cqT_all", [512, S], BF16, kind="ExternalInput").ap()
    ckv_d = nc.dram_tensor("ckvT_all", [512, S], BF16, kind="ExternalInput").ap()
    kr_d = nc.dram_tensor("krT_all", [64, S], BF16, kind="ExternalInput").ap()
    wqn_d = nc.dram_tensor("wuq_n", [128, LC, 256], F32, kind="ExternalInput").ap()
    wqr_d = nc.dram_tensor("wuq_r", [128, LC, 128], F32, kind="ExternalInput").ap()
    wqr2_d = nc.dram_tensor("wuq_r2", [128, LC, 128], F32, kind="ExternalInput").ap()
    wuk_d = nc.dram_tensor("wuk", [128, LC, 256], F32, kind="ExternalInput").ap()
    wuv_d = nc.dram_tensor("wuv", [128, LC, 256], F32, kind="ExternalInput").ap()
    cs_d = nc.dram_tensor("cs", [2, 128, S], F32, kind="ExternalInput").ap()
    cst_d = nc.dram_tensor("cst", [2, 128, 128], BF16, kind="ExternalInput").ap()
    o_d = nc.dram_tensor("oT", [256, S], BF16, kind="ExternalOutput").ap()
    scale = 192.0 ** -0.5
    with ExitStack() as st:
        C = Ctx(nc, st)
        P = Prog(nc)
        wqn = C.sbuf([128, LC, 256], BF16, "wqn")
        wqr = C.sbuf([128, LC, 128], BF16, "wqr")
        wqr2 = C.sbuf([128, LC, 128], BF16, "wqr2")
        wuk = C.sbuf([128, LC, 256], BF16, "wuk_sb")
        wuv = C.sbuf([128, LC, 256], BF16, "wuv_sb")
        cst = C.sbuf([128, 2, 128], BF16, "cst_sb")
        KnT = C.sbuf([128, 2, S], BF16, "KnT")
        krT = C.sbuf([128, S], BF16, "krT2")
        Vs = C.sbuf([128, S // 128, 256], BF16, "Vs")
        QnT = C.sbuf([128, 2, S], BF16, "QnT")
        QrT = C.sbuf([128, S], BF16, "QrT")
        ckb = [C.sbuf([128, LC, 512], BF16, f"ckb{i}") for i in range(2)]
        cqb = [C.sbuf([128, LC, 512], BF16, f"cqb{i}") for i in range(2)]
        csb = [C.sbuf([128, 2, 512], F32, f"csb{i}") for i in range(2)]
        tmp1 = C.sbuf([128, 512], F32, "tmp1")
        tmp2 = C.sbuf([128, 512], F32, "tmp2")
        ptb = [C.sbuf([128, 512], BF16, f"ptb{i}") for i in range(3)]
        rl = C.sbuf([128, 512], F32, "rl")
        oout = [C.sbuf([128, 512], BF16, f"oout{i}") for i in range(2)]
        onesb = C.sbuf([128, 128], BF16, "onesb")
        bank = [C.psum([128, 512], F32, f"bk{i}") for i in range(8)]
        s_w, s_in, s_pj, s_evA, s_evD, s_init = (C.sem(n) for n in ("s_w", "s_in", "s_pj", "s_evA", "s_evD", "s_init"))
        s_qk, s_ex, s_pv, s_fin, s_out, s_acc = (C.sem(n) for n in ("s_qk", "s_ex", "s_pv", "s_fin", "s_out", "s_acc"))
        P.op("dve", "memset", onesb[:, :], 1.0, inc=s_init)
        for dst, src in ((wqn, wqn_d), (wqr, wqr_d), (wqr2, wqr2_d), (wuk, wuk_d), (wuv, wuv_d)):
            P.dma("pool", dst[:, :, :], src, s_w)
        P.dma("pool", cst[:, :, :], cst_d.rearrange("a p c -> p a c"), s_w)
        P.dma("pool", krT[0:64, :], kr_d, s_w)
        P.dma("pool", krT[64:128, :], kr_d, s_w)
        NW = 8 * 16
        ident, tri = cst[:, 0, :], cst[:, 1, :]
        ckv_v = ckv_d.rearrange("(k p) t -> p k t", p=128)
        cq_v = cq_d.rearrange("(k p) t -> p k t", p=128)
        cs_v = cs_d.rearrange("a p t -> p a t")
        NT = S // 512
        P.wait("pe", s_w, NW)
        P.wait("pe", s_init, 1)
        units = []
        n_evA = n_evD = 0
        for tt in range(NT):
            b = tt % 2
            tsl = slice(tt * 512, (tt + 1) * 512)
            if tt >= 2:
                P.wait("sp", s_pj, 8 * (tt - 1))
                P.wait("sp", s_evD, evd_at[tt - 2])
            P.dma("sp", ckb[b][:, :, :], ckv_v[:, :, tsl], s_in)
            P.dma("sp", cqb[b][:, :, :], cq_v[:, :, tsl], s_in)
            P.dma("sp", csb[b][:, :, :], cs_v[:, :, tsl], s_in)
            P.wait("pe", s_in, 48 * (tt + 1))
            if tt == 0:
                evd_at = {}
            for ui in range(8):
                u = len(units)
                bk = bank[u % 8]
                if u >= 8:
                    P.wait("pe", units[u - 8][0], units[u - 8][1])
                if ui in (0, 1, 4, 5):
                    hd = ui % 2
                    w, src, dst = (wuk, ckb[b], KnT) if ui < 2 else (wqn, cqb[b], QnT)
                    for k in range(LC):
                        P.op("pe", "matmul", bk[:, :], w[:, k, hd * 128:(hd + 1) * 128], src[:, k, :],
                             start=(k == 0), stop=(k == LC - 1), inc=s_pj if k == LC - 1 else None)
                    P.wait("act", s_pj, u + 1)
                    P.op("act", "activation", out=dst[:, hd, tsl], in_=bk[:, :], func=AF.Copy, inc=s_evA)
                    n_evA += 1
                    units.append((s_evA, n_evA))
                elif ui in (2, 3):
                    sp_ = ui - 2
                    for si in range(2):
                        sub = sp_ * 2 + si
                        for k in range(LC):
                            last = (k == LC - 1) and si == 1
                            P.op("pe", "matmul", bk[:, si * 256:(si + 1) * 256], ckb[b][:, k, sub * 128:(sub + 1) * 128],
                                 wuv[:, k, :], start=(k == 0), stop=(k == LC - 1), inc=s_pj if last else None)
                    P.wait("dve", s_pj, u + 1)
                    n0 = tt * 4 + sp_ * 2
                    P.op("dve", "tensor_copy", out=Vs[:, n0:n0 + 2, :], in_=bk[:, :].rearrange("p (s c) -> p s c", c=256),
                         inc=s_evD)
                    n_evD += 1
                    units.append((s_evD, n_evD))
                else:
                    w = wqr if ui == 6 else wqr2
                    for k in range(LC):
                        P.op("pe", "matmul", bk[:, :], w[:, k, :], cqb[b][:, k, :],
                             start=(k == 0), stop=(k == LC - 1), inc=s_pj if k == LC - 1 else None)
                    P.wait("dve", s_pj, u + 1)
                    if ui == 6:
                        P.op("dve", "tensor_tensor", out=tmp1[:, :], in0=bk[:, :], in1=csb[b][:, 0, :], op=ALU.mult,
                             inc=s_evD)
                        n_evD += 1
                    else:
                        P.op("dve", "tensor_tensor", out=tmp2[:, :], in0=bk[:, :], in1=csb[b][:, 1, :], op=ALU.mult,
                             inc=s_evD)
                        n_evD += 1
                        P.wait("dve", s_evD, n_evD)
                        P.op("dve", "tensor_tensor", out=QrT[:, tsl], in0=tmp1[:, :], in1=tmp2[:, :], op=ALU.add,
                             inc=s_evD)
                        n_evD += 1
                    units.append((s_evD, n_evD if ui == 7 else n_evD))
            evd_at[tt] = n_evD
        P.wait("pe", s_evA, n_evA)
        P.wait("pe", s_evD, n_evD)
        steps = []
        for hd in range(2):
            for qt in range(NT):
                nk = 4 * qt + 4
                for kt in range(nk):
                    steps.append((hd, qt, kt, nk))
        n_acc = 0

        def emit_qk(g):
            hd, qt, kt, nk = steps[g]
            i = kt - 4 * qt
            c0 = 128 * i if i > 0 else 0
            sb_ = bank[g % 2]
            q0 = qt * 512
            ksl = slice(kt * 128, (kt + 1) * 128)
            P.wait("pe", s_ex, g - 1)
            P.op("pe", "matmul", sb_[:, c0:512], KnT[:, hd, ksl], QnT[:, hd, q0 + c0:q0 + 512], start=True, stop=False)
            P.op("pe", "matmul", sb_[:, c0:512], krT[hd * 64:(hd + 1) * 64, ksl],
                 QrT[hd * 64:(hd + 1) * 64, q0 + c0:q0 + 512], start=False, stop=(i < 0),
                 inc=s_qk if i < 0 else None)
            if i >= 0:
                P.op("pe", "matmul", sb_[:, c0:c0 + 128], ident, tri, start=False, stop=True, inc=s_qk)
            P.wait("act", s_qk, g + 1)
            P.wait("act", s_pv, g - 2)
            P.op("act", "activation", out=ptb[g % 3][:, c0:512], in_=sb_[:, c0:512], func=AF.Exp, scale=scale, inc=s_ex)

        def emit_pv(g):
            nonlocal n_acc
            hd, qt, kt, nk = steps[g]
            i = kt - 4 * qt
            c0 = 128 * i if i > 0 else 0
            a = n_acc
            ob, lb = bank[2 + 2 * (a % 2)], bank[3 + 2 * (a % 2)]
            P.wait("pe", s_ex, g + 1)
            if kt == 0:
                P.wait("pe", s_fin, a - 1)
            P.op("pe", "matmul", ob[:, c0:512], Vs[:, kt, hd * 128:(hd + 1) * 128], ptb[g % 3][:, c0:512],
                 start=(kt == 0), stop=(kt == nk - 1))
            P.op("pe", "matmul", lb[:, c0:512], onesb[:, :], ptb[g % 3][:, c0:512],
                 start=(kt == 0), stop=(kt == nk - 1), inc=s_pv)
            if kt == nk - 1:
                q0 = qt * 512
                P.wait("dve", s_pv, g + 1)
                P.wait("dve", s_out, 16 * (a - 1))
                P.op("dve", "reciprocal", rl[:, :], lb[:, :], inc=s_acc)
                P.wait("dve", s_acc, a + 1)
                P.op("dve", "tensor_tensor", out=oout[a % 2][:, :], in0=ob[:, :], in1=rl[:, :], op=ALU.mult, inc=s_fin)
                P.wait("sp", s_fin, a + 1)
                P.dma("sp", o_d[hd * 128:(hd + 1) * 128, q0:q0 + 512], oout[a % 2][:, :], s_out)
                n_acc += 1

        G = len(steps)
        emit_qk(0)
        for g in range(G):
            if g + 1 < G:
                emit_qk(g + 1)
            emit_pv(g)
        P.wait("sp", s_out, s_out.n)
        P.run()
    return nc


def rope_tables_ext(pos):
    inv = (1.0 / (10000.0 ** (np.arange(0, 64, 2, dtype=np.float32) / np.float32(64)))).astype(np.float32)
    ang = pos.astype(np.float32)[None, :] * inv[:, None]
    cos, sin = np.cos(ang).astype(np.float32), np.sin(ang).astype(np.float32)
    return np.concatenate([cos, cos], 0), np.concatenate([-sin, sin], 0)


def mla_consts():
    k = np.arange(128)[:, None]
    j = np.arange(128)[None, :]
    tri = np.where(k <= j, 0.0, -30000.0).astype(np.float32)
    return np.stack([np.eye(128, dtype=np.float32), tri]).astype(ml_dtypes.bfloat16)


def til(w, ncols=128):
    din, dout = w.shape
    return np.ascontiguousarray(w.reshape(din // 128, 128, dout // ncols, ncols).transpose(2, 1, 0, 3))


def build_token_program(wo=False, ffn2=False, latent=False, ffn1=False, mix=None, final=False):
    from contextlib import ExitStack
    nc = bass.Bass("TRN2", target_bir_lowering=False)

    def din(name, shape, dt=F32):
        return nc.dram_tensor(name, list(shape), dt, kind="ExternalInput").ap()

    def dout(name, shape, dt=F32):
        return nc.dram_tensor(name, list(shape), dt, kind="ExternalOutput").ap()

    x_d = din("xT_in", [D, T])
    y_d = dout("xT_out", [D, T])
    gnames = []
    if ffn2:
        gnames.append("g_ffn2")
    if latent:
        gnames += ["g_kv"]
    if ffn1:
        gnames.append("g_ffn1")
    if mix:
        gnames.append("g_mix")
    if final:
        gnames.append("g_final")
    g_d = {n: din(n, [128, KC]) for n in gnames}
    if wo:
        o_d = din("oT_in", [D, T], BF16)
        wo_d = din("wo", [KC, 128, KC, 128])
    if ffn2:
        w2 = (din("f2_wg", [FC, 128, KC, 128]), din("f2_wu", [FC, 128, KC, 128]), din("f2_wd", [2, KC, 128, FH, 128]))
    if ffn1:
        w1 = (din("f1_wg", [FC, 128, KC, 128]), din("f1_wu", [FC, 128, KC, 128]), din("f1_wd", [2, KC, 128, FH, 128]))
    if latent:
        wdkv_d = din("wdkv", [LC, 128, KC, 128])
        gckv_d = din("g_ckv", [128, LC])
        wkr_d = din("wkr", [2, 128, KC, 64])
        cs_d = din("cs_tok", [2, 64, T])
        ckv_o = dout("ckvT_out", [512, T], BF16)
        kr_o = dout("krT_out", [64, T], BF16)
    if mix == "cq":
        wdq_d = din("wdq", [LC, 128, KC, 128])
        gcq_d = din("g_cq", [128, LC])
        cq_o = dout("cqT_out", [512, T], BF16)
    if mix == "h":
        h_o = dout("hT_out", [D, T], BF16)

    with ExitStack() as st:
        C = Ctx(nc, st)
        P = Prog(nc)
        tp = TokenPhase(nc, P, C)
        tp.epsb = C.sbuf([128, 1], F32, "epsb")
        P.op("dve", "memset", tp.epsb[:, :], EPS, inc=tp.s_init)
        P.wait("act", tp.s_init, 2)
        g_sb = {n: C.sbuf([128, KC], F32, n + "_sb") for n in gnames}
        tp.load_x(x_d)
        for n in gnames:
            tp.load(g_sb[n][:, :], g_d[n][:, :])
        if latent or mix == "cq":
            lat = C.sbuf([128, LC, T], F32, "lat")
            latb = tp.aT[:, 0:LC, :]
            s_misc = C.sem("s_misc")
        if latent:
            gckv = C.sbuf([128, LC], F32, "gckv_sb")
            cs_sb = C.sbuf([64, 2, T], F32, "cs_sb")
            krb = tp.aT[0:64, LC + 1, :]
            tp.load(gckv[:, :], gckv_d[:, :])
            tp.load(cs_sb[:, :, :], cs_d.rearrange("a p t -> p a t"))
        if mix == "cq":
            gcq = C.sbuf([128, LC], F32, "gcq_sb")
            tp.load(gcq[:, :], gcq_d[:, :])
        n_lat_out = 0
        if wo:
            tp.load(tp.aT[:, 0:KC, :], o_d.rearrange("(k p) t -> p k t", p=128))
            for e in ("dve",):
                P.wait(e, tp.s_x, tp.s_x.n)
            tp.add_proj(tp.aT, KC, [wo_d[dc] for dc in range(KC)])
        if ffn2:
            tp.norm(g_sb["g_ffn2"], tp.hT)
            tp.ffn(*w2)
        if latent:
            tp.norm(g_sb["g_kv"], tp.hT)
            tp.proj_to(tp.hT, KC, [wdkv_d[oc] for oc in range(LC)], lat)
            tp.norm(gckv, latb, src=lat, nk=LC)
            P.wait("sp", tp.s_h, tp.s_h.n)
            P.dma("sp", ckv_o.rearrange("(k p) t -> p k t", p=128), latb, tp.s_out)
            n_lat_out = tp.s_out.n
            tA = lat[0:64, 0, :]
            tB = lat[0:64, 1, :]
            P.wait("dve", tp.s_h, tp.s_h.n)

            def kr_evac(v, th, tsl, bk, inc):
                P.op("dve", "tensor_tensor", out=(tA if v == 0 else tB)[:, tsl], in0=bk[0:64, :],
                     in1=cs_sb[:, v, tsl], op=ALU.mult, inc=inc)
            tp.proj_units(tp.hT, KC, [wkr_d[0], wkr_d[1]], 64, kr_evac)
            P.wait("dve", tp.s_y, tp.s_y.n)
            P.op("dve", "tensor_tensor", out=krb, in0=tA, in1=tB, op=ALU.add, inc=s_misc)
            P.wait("sp", s_misc, s_misc.n)
            P.dma("sp", kr_o, krb, tp.s_out)
        if ffn1:
            tp.norm(g_sb["g_ffn1"], tp.hT)
            tp.ffn(*w1)
        if mix:
            tp.norm(g_sb["g_mix"], tp.hT)
        if mix == "h":
            P.wait("sp", tp.s_h, tp.s_h.n)
            hv = h_o.rearrange("(k p) t -> p k t", p=128)
            for k in range(0, KC, 4):
                P.dma("sp", hv[:, k:k + 4, :], tp.hT[:, k:k + 4, :], tp.s_out)
        if mix == "cq":
            if latent:
                P.wait("dve", tp.s_out, n_lat_out)
                P.wait("dve", s_misc, s_misc.n)
            tp.proj_to(tp.hT, KC, [wdq_d[oc] for oc in range(LC)], lat)
            tp.norm(gcq, latb, src=lat, nk=LC)
            P.wait("sp", tp.s_h, tp.s_h.n)
            P.dma("sp", cq_o.rearrange("(k p) t -> p k t", p=128), latb, tp.s_out)
        if final:
            tp.norm(g_sb["g_final"], tp.xT)
            P.wait("sp", tp.s_h, tp.s_h.n)
        tp.store_x(y_d)
        tp.finish()
        P.run()
    return nc


_PROGS = {}


def _prog(key, fn, **kw):
    if key not in _PROGS:
        _PROGS[key] = fn(**kw)
    return _PROGS[key]


def _launch(nc, maps):
    res = run_bass_kernel_spmd(nc, maps, core_ids=list(range(NCORES)))
    return res.results


def _f32(a):
    return np.ascontiguousarray(np.asarray(a, dtype=np.float32))


def kernel(x, ffn_norm1, ffn1_wg, ffn1_wu, ffn1_wd, mix_norm, ffn_norm2, ffn2_wg, ffn2_wu, ffn2_wd,
           a_wqkv, a_wo, kv_norm, b_wdkv, b_ckv_norm, b_wkr, b_wuk, b_wuv,
           b_wdq, b_cq_norm, b_wuq, b_wo, final_norm):
    x = _f32(x)
    xT = np.ascontiguousarray(x[0].T)
    xs = [np.ascontiguousarray(xT[:, c * T:(c + 1) * T]) for c in range(NCORES)]
    swap = (np.arange(64) + 32) % 64

    def ffn_w(l, which):
        wg, wu, wd = (ffn1_wg, ffn1_wu, ffn1_wd) if which == 1 else (ffn2_wg, ffn2_wu, ffn2_wd)
        p = "f1_" if which == 1 else "f2_"
        return {p + "wg": tile_wgu(_f32(wg[l])), p + "wu": tile_wgu(_f32(wu[l])), p + "wd": tile_wd(_f32(wd[l]))}

    def gc(g):
        return gcol_of(_f32(g))

    def gath_tok(res, name):
        return np.ascontiguousarray(np.concatenate([np.asarray(r[name]) for r in res], axis=1))

    def tok_shards(full):
        return [np.ascontiguousarray(full[:, c * T:(c + 1) * T]) for c in range(NCORES)]

    com = dict(ffn_w(0, 1), g_ffn1=gc(ffn_norm1[0]), g_mix=gc(mix_norm[0]))
    nc = _prog("T_first", build_token_program, ffn1=True, mix="h")
    res = _launch(nc, [dict(com, xT_in=xs[c]) for c in range(NCORES)])
    xs = [np.asarray(r["xT_out"]) for r in res]
    hT_all = gath_tok(res, "hT_out")
    oT = None
    for l in range(DEPTH):
        if l < 2:
            wqkv = _f32(a_wqkv[l])

            def hw(w, c):
                return np.ascontiguousarray(w[:, c * 256:(c + 1) * 256].reshape(KC, 128, 256).transpose(1, 0, 2))
            nc = _prog("A", build_attn_a_program)
            hT_t = tile_hT(hT_all)
            maps = [{"hT_all": hT_t, "wq": hw(wqkv[:, 0:2048], c), "wk": hw(wqkv[:, 2048:4096], c),
                     "wv": hw(wqkv[:, 4096:6144], c), "emask": alibi_masks([2 * c, 2 * c + 1])}
                    for c in range(NCORES)]
            res = _launch(nc, maps)
            wo_l = _f32(a_wo[l])
        else:
            jb = l - 2
            wuq, wuk, wuv = _f32(b_wuq[jb]), _f32(b_wuk), _f32(b_wuv)
            ce, se = rope_tables_ext(np.arange(S))
            cs = np.stack([np.concatenate([ce, ce], 0), np.concatenate([se, se], 0)])

            def t4(w):
                return np.ascontiguousarray(w.reshape(LC, 128, -1).transpose(1, 0, 2))
            nc = _prog("M", build_mla_program)
            maps = []
            for c in range(NCORES):
                hs = [2 * c, 2 * c + 1]
                maps.append({"cqT_all": cqT_all, "ckvT_all": ckvT_all, "krT_all": krT_all,
                             "wuq_n": t4(np.concatenate([wuq[:, h, :128] for h in hs], 1)),
                             "wuq_r": t4(np.concatenate([wuq[:, h, 128:] for h in hs], 1)),
                             "wuq_r2": t4(np.concatenate([wuq[:, h, 128:][:, swap] for h in hs], 1)),
                             "wuk": t4(np.concatenate([wuk[:, h] for h in hs], 1)),
                             "wuv": t4(np.concatenate([wuv[:, h] for h in hs], 1)),
                             "cs": cs, "cst": mla_consts()})
            res = _launch(nc, maps)
            wo_l = _f32(b_wo[jb])
        oT_full = np.ascontiguousarray(np.concatenate([np.asarray(r["oT"]) for r in res], axis=0))
        oTs = tok_shards(oT_full)
        com = dict(ffn_w(l, 2), wo=til(wo_l), g_ffn2=gc(ffn_norm2[l]))
        if l == DEPTH - 1:
            nc = _prog("T_last", build_token_program, wo=True, ffn2=True, final=True)
            com["g_final"] = gc(final_norm)
            res = _launch(nc, [dict(com, xT_in=xs[c], oT_in=oTs[c]) for c in range(NCORES)])
            outT = np.concatenate([np.asarray(r["xT_out"]) for r in res], axis=1)
            return np.ascontiguousarray(outT.T)[None].astype(np.float32)
        com.update(ffn_w(l + 1, 1))
        com["g_ffn1"] = gc(ffn_norm1[l + 1])
        com["g_mix"] = gc(mix_norm[l + 1])
        per_core = [dict(xT_in=xs[c], oT_in=oTs[c]) for c in range(NCORES)]
        if l + 1 < 2:
            nc = _prog("T_mid_h", build_token_program, wo=True, ffn2=True, ffn1=True, mix="h")
        else:
            jb = l + 1 - 2
            com["wdq"] = til(_f32(b_wdq[jb]))
            com["g_cq"] = np.ascontiguousarray(_f32(b_cq_norm[jb]).reshape(LC, 128).T)
            if l + 1 == 2:
                nc = _prog("T_mid_lat", build_token_program, wo=True, ffn2=True, latent=True, ffn1=True, mix="cq")
                com["g_kv"] = gc(kv_norm)
                com["wdkv"] = til(_f32(b_wdkv))
                com["g_ckv"] = np.ascontiguousarray(_f32(b_ckv_norm).reshape(LC, 128).T)
                wkr = _f32(b_wkr)
                com["wkr"] = np.stack([til(wkr, 64)[0], til(np.ascontiguousarray(wkr[:, swap]), 64)[0]])
                for c in range(NCORES):
                    ce, se = rope_tables_ext(np.arange(c * T, (c + 1) * T))
                    per_core[c]["cs_tok"] = np.stack([ce, se])
            else:
                nc = _prog("T_mid_cq", build_token_program, wo=True, ffn2=True, ffn1=True, mix="cq")
        res = _launch(nc, [dict(com, **per_core[c]) for c in range(NCORES)])
        xs = [np.asarray(r["xT_out"]) for r in res]
        if l + 1 < 2:
            hT_all = gath_tok(res, "hT_out")
        else:
            cqT_all = gath_tok(res, "cqT_out")
            if l + 1 == 2:
                ckvT_all = gath_tok(res, "ckvT_out")
                krT_all = gath_tok(res, "krT_out")
```

```python
import numpy as np
import ml_dtypes
import concourse.bass as bass
import concourse.mybir as mybir
from concourse.bass_utils import run_bass_kernel_spmd

F32 = mybir.dt.float32
BF16 = mybir.dt.bfloat16
AF = mybir.ActivationFunctionType
ALU = mybir.AluOpType

NCORES = 8
D = 2048
S = 8192
T = S // NCORES
KC = D // 128
DFF = 5632
FC = DFF // 128
FH = FC // 2
EPS = 1e-6
DEPTH = 4


class Sem:
    def __init__(self, h):
        self.h = h
        self.n = 0


class Prog:
    def __init__(self, nc):
        self.nc = nc
        self.q = {"pe": [], "act": [], "dve": [], "pool": [], "sp": []}

    def op(self, eng, name, *args, inc=None, incv=None, **kw):
        if inc is not None:
            v = incv if incv is not None else 1
            inc.n += v
            h = inc.h

            def f(e, name=name, args=args, kw=kw, h=h, v=v):
                getattr(e, name)(*args, **kw).then_inc(h, v)
        else:
            def f(e, name=name, args=args, kw=kw):
                getattr(e, name)(*args, **kw)
        self.q[eng].append(f)

    def dma(self, eng, out, in_, sem):
        self.op(eng, "dma_start", out=out, in_=in_, inc=sem, incv=16)

    def wait(self, eng, sem, val):
        if val <= 0:
            return
        h = sem.h
        self.q[eng].append(lambda e, h=h, val=val: e.wait_ge(h, val))

    def run(self):
        with self.nc.Block() as block:
            block.tensor(lambda e: [f(e) for f in self.q["pe"]])
            block.scalar(lambda e: [f(e) for f in self.q["act"]])
            block.vector(lambda e: [f(e) for f in self.q["dve"]])
            block.gpsimd(lambda e: [f(e) for f in self.q["pool"]])
            block.sync(lambda e: [f(e) for f in self.q["sp"]])


class Ctx:
    def __init__(self, nc, stack):
        self.nc = nc
        self.stack = stack
        self.k = 0

    def sbuf(self, shape, dt, name=None):
        self.k += 1
        return self.stack.enter_context(self.nc.sbuf_tensor(name or f"sb{self.k}", list(shape), dt))

    def psum(self, shape, dt=F32, name=None):
        self.k += 1
        return self.stack.enter_context(self.nc.psum_tensor(name or f"ps{self.k}", list(shape), dt))

    def sem(self, name=None):
        self.k += 1
        return Sem(self.stack.enter_context(self.nc.semaphore(name or f"sem{self.k}")))


class TokenPhase:
    def __init__(self, nc, P, C):
        self.nc, self.P, self.C = nc, P, C
        self.xT = C.sbuf([128, KC, T], F32, "xT")
        self.hT = C.sbuf([128, KC, T], BF16, "hT")
        self.aT = C.sbuf([128, FH, T], BF16, "aT")
        self.scr = C.sbuf([128, 4, 512], F32, "scr")
        self.ones = C.sbuf([128, 128], F32, "ones")
        self.rstd = C.sbuf([128, 512], F32, "rstd")
        self.sg = [C.sbuf([128, 512], F32, f"sg{i}") for i in range(2)]
        self.wg = [C.sbuf([128, KC, 128], BF16, f"wg{i}") for i in range(2)]
        self.wu = [C.sbuf([128, KC, 128], BF16, f"wu{i}") for i in range(2)]
        self.wd = [C.sbuf([128, FH, 128], BF16, f"wd{i}") for i in range(2)]
        self.bank = [C.psum([128, 512], F32, f"bk{i}") for i in range(8)]
        self.s_init = C.sem("s_init")
        self.s_x = C.sem("s_x")
        self.s_sq = C.sem("s_sq")
        self.s_st = C.sem("s_st")
        self.s_sqrt = C.sem("s_sqrt")
        self.s_rs = C.sem("s_rs")
        self.s_h = C.sem("s_h")
        self.s_wgu = C.sem("s_wgu")
        self.s_gu = C.sem("s_gu")
        self.s_sl = C.sem("s_sl")
        self.s_a = C.sem("s_a")
        self.s_wd = C.sem("s_wd")
        self.s_dn = C.sem("s_dn")
        self.s_y = C.sem("s_y")
        self.s_out = C.sem("s_out")
        self.n_wgu = self.n_wd = self.n_gu = self.n_dn = self.n_hn = self.n_sq = 0
        P.op("dve", "memset", self.ones[:, :], 1.0, inc=self.s_init)
        P.wait("pe", self.s_init, 1)

    def load(self, dst, src):
        self.P.dma("sp", dst, src, self.s_x)

    def load_x(self, x_dram):
        xv = x_dram.rearrange("(k p) t -> p k t", p=128)
        for i, k in enumerate(range(0, KC, 2)):
            self.P.dma("sp" if i % 2 == 0 else "act", self.xT[:, k:k + 2, :], xv[:, k:k + 2, :], self.s_x)

    def store_x(self, y_dram):
        P = self.P
        yv = y_dram.rearrange("(k p) t -> p k t", p=128)
        P.wait("sp", self.s_y, self.s_y.n)
        P.wait("act", self.s_y, self.s_y.n)
        P.wait("act", self.s_h, self.s_h.n)
        for i, k in enumerate(range(0, KC, 2)):
            P.dma("sp" if i % 2 == 0 else "act", yv[:, k:k + 2, :], self.xT[:, k:k + 2, :], self.s_out)

    def finish(self):
        self.P.wait("sp", self.s_out, self.s_out.n)

    def norm(self, gcol, out_tile, src=None, nk=KC):
        P = self.P
        src = self.xT if src is None else src
        for e in ("act", "dve"):
            P.wait(e, self.s_x, self.s_x.n)
            P.wait(e, self.s_y, self.s_y.n)
        for th in range(2):
            n = self.n_hn
            self.n_hn += 1
            tsl = slice(th * 512, (th + 1) * 512)
            P.wait("pe", self.s_sqrt, n)
            for k in range(nk):
                q = self.n_sq
                self.n_sq += 1
                P.wait("act", self.s_st, q - 3)
                P.op("act", "activation", out=self.scr[:, q % 4, :], in_=src[:, k, tsl],
                     func=AF.Square, inc=self.s_sq)
                P.wait("pe", self.s_sq, q + 1)
                P.op("pe", "matmul", self.bank[7][:, :], self.ones[:, :], self.scr[:, q % 4, :],
                     start=(k == 0), stop=(k == nk - 1), inc=self.s_st)
            P.wait("act", self.s_st, self.n_sq)
            P.wait("act", self.s_h, self.s_h.n)
            P.op("act", "activation", out=self.rstd[:, :], in_=self.bank[7][:, :], func=AF.Sqrt,
                 scale=1.0 / (128 * nk), bias=self.epsb[:, 0:1], inc=self.s_sqrt)
            P.wait("dve", self.s_sqrt, n + 1)
            P.op("dve", "reciprocal", self.rstd[:, :], self.rstd[:, :], inc=self.s_rs)
            P.wait("dve", self.s_rs, n + 1)
            for k in range(nk):
                P.op("dve", "scalar_tensor_tensor", out=out_tile[:, k, tsl], in0=src[:, k, tsl],
                     scalar=gcol[:, k:k + 1], op0=ALU.mult, in1=self.rstd[:, :], op1=ALU.mult,
                     inc=self.s_h)

    def ffn(self, wg_d, wu_d, wd_d):
        P = self.P
        h_ready = self.s_h.n
        P.wait("dve", self.s_out, self.s_out.n)
        for fh in range(2):
            gu0 = self.n_gu
            P.wait("pe", self.s_y, self.s_y.n)
            P.wait("pe", self.s_h, h_ready)
            for j in range(FH):
                fc = fh * FH + j
                c = self.n_wgu
                self.n_wgu += 1
                b = c % 2
                P.wait("pool", self.s_gu, 2 * (c - 1))
                P.dma("pool", self.wg[b][:, :, :], wg_d[fc], self.s_wgu)
                P.dma("pool", self.wu[b][:, :, :], wu_d[fc], self.s_wgu)
                P.wait("pe", self.s_wgu, 32 * (c + 1))
                for th in range(2):
                    n = self.n_gu
                    self.n_gu += 1
                    pr = n % 4
                    gps, ups = self.bank[2 * pr], self.bank[2 * pr + 1]
                    tsl = slice(th * 512, (th + 1) * 512)
                    if n - gu0 >= 4:
                        P.wait("pe", self.s_a, n - 3)
                    for k in range(KC):
                        P.op("pe", "matmul", gps[:, :], self.wg[b][:, k, :], self.hT[:, k, tsl],
                             start=(k == 0), stop=(k == KC - 1))
                    for k in range(KC):
                        last = k == KC - 1
                        P.op("pe", "matmul", ups[:, :], self.wu[b][:, k, :], self.hT[:, k, tsl],
                             start=(k == 0), stop=last, inc=self.s_gu if last else None)
                    P.wait("act", self.s_gu, n + 1)
                    P.wait("act", self.s_a, n - 1)
                    P.op("act", "activation", out=self.sg[n % 2][:, :], in_=gps[:, :], func=AF.Silu,
                         inc=self.s_sl)
                    P.wait("dve", self.s_sl, n + 1)
                    P.op("dve", "tensor_tensor", out=self.aT[:, j, tsl], in0=ups[:, :],
                         in1=self.sg[n % 2][:, :], op=ALU.mult, inc=self.s_a)
            P.wait("pe", self.s_a, self.n_gu)
            self.proj_units(self.aT, FH, [wd_d[fh, dc] for dc in range(KC)], 128,
                            lambda dc, th, tsl, bk, inc: P.op(
                                "dve", "scalar_tensor_tensor", out=self.xT[:, dc, tsl], in0=bk[:, :], scalar=0.5,
                                op0=ALU.mult, in1=self.xT[:, dc, tsl], op1=ALU.add, inc=inc))

    def proj_units(self, src, nk, w_list, ncols, evac, src_ready=None):
        P = self.P
        P.wait("pe", self.s_sqrt, self.s_sqrt.n)
        P.wait("pe", self.s_h, self.s_h.n)
        if src_ready is None:
            P.wait("pe", self.s_x, self.s_x.n)
        else:
            P.wait("pe", src_ready[0], src_ready[1])
        for oc, w_ap in enumerate(w_list):
            c = self.n_wd
            self.n_wd += 1
            b = c % 2
            P.wait("pool", self.s_dn, 2 * (c - 1))
            P.dma("pool", self.wd[b][:, 0:nk, 0:ncols], w_ap, self.s_wd)
            P.wait("pe", self.s_wd, 16 * (c + 1))
            for th in range(2):
                m = self.n_dn
                self.n_dn += 1
                bk = self.bank[m % 8]
                tsl = slice(th * 512, (th + 1) * 512)
                P.wait("pe", self.s_y, m - 7)
                for j in range(nk):
                    last = j == nk - 1
                    P.op("pe", "matmul", bk[0:ncols, :], self.wd[b][:, j, 0:ncols], src[:, j, tsl],
                         start=(j == 0), stop=last, inc=(self.s_dn if last else None))
                P.wait("dve", self.s_dn, m + 1)
                evac(oc, th, tsl, bk, self.s_y)

    def add_proj(self, src, nk, w_list, src_ready=None):
        P = self.P
        self.proj_units(src, nk, w_list, 128,
                        lambda oc, th, tsl, bk, inc: P.op(
                            "dve", "tensor_tensor", out=self.xT[:, oc, tsl], in0=bk[:, :],
                            in1=self.xT[:, oc, tsl], op=ALU.add, inc=inc), src_ready=src_ready)

    def proj_to(self, src, nk, w_list, dst):
        P = self.P
        self.proj_units(src, nk, w_list, 128,
                        lambda oc, th, tsl, bk, inc: P.op(
                            "dve", "tensor_copy", out=dst[:, oc, tsl], in_=bk[:, :], inc=inc))


def tile_wgu(w):
    return np.ascontiguousarray(w.reshape(KC, 128, FC, 128).transpose(2, 1, 0, 3))


def tile_wd(w):
    return np.ascontiguousarray(w.reshape(2, FH, 128, KC, 128).transpose(0, 3, 2, 1, 4))


def gcol_of(g):
    return np.ascontiguousarray(g.reshape(KC, 128).T)


def build_ffn_program():
    from contextlib import ExitStack
    nc = bass.Bass("TRN2", target_bir_lowering=False)
    x_d = nc.dram_tensor("xT_in", [D, T], F32, kind="ExternalInput").ap()
    g_d = nc.dram_tensor("gcol", [128, KC], F32, kind="ExternalInput").ap()
    wg_d = nc.dram_tensor("wg", [FC, 128, KC, 128], F32, kind="ExternalInput").ap()
    wu_d = nc.dram_tensor("wu", [FC, 128, KC, 128], F32, kind="ExternalInput").ap()
    wd_d = nc.dram_tensor("wd", [2, KC, 128, FH, 128], F32, kind="ExternalInput").ap()
    y_d = nc.dram_tensor("xT_out", [D, T], F32, kind="ExternalOutput").ap()
    with ExitStack() as st:
        C = Ctx(nc, st)
        P = Prog(nc)
        tp = TokenPhase(nc, P, C)
        gcol = C.sbuf([128, KC], F32, "gcol_sb")
        tp.epsb = C.sbuf([128, 1], F32, "epsb")
        P.op("dve", "memset", tp.epsb[:, :], EPS, inc=tp.s_init)
        P.wait("act", tp.s_init, 2)
        tp.load_x(x_d)
        tp.load(gcol[:, :], g_d[:, :])
        tp.norm(gcol, tp.hT)
        tp.ffn(wg_d, wu_d, wd_d)
        tp.store_x(y_d)
        tp.finish()
        P.run()
    return nc


DILS = (1, 4, 16)


def ss(start, n, step):
    return slice(start, start + (n - 1) * step + 1, step)

BLK = 2048
NB = S // BLK


def alibi_masks(heads):
    k = np.arange(128)[:, None].astype(np.float64)
    j = np.arange(128)[None, :].astype(np.float64)
    out = np.zeros((3, len(heads), 128, 256), np.float32)
    for di, d in enumerate(DILS):
        for hi, h in enumerate(heads):
            slope = 2.0 ** (-8.0 * (h + 1) / 16)
            lo = np.where(j >= k, np.exp(-slope * d * np.maximum(j - k, 0.0)), 0.0)
            hi_ = np.where(j <= k, np.exp(-slope * d * np.maximum(128 + j - k, 0.0)), 0.0)
            out[di, hi, :, 0:128] = hi_
            out[di, hi, :, 128:256] = lo
    return out


def build_attn_a_program():
    from contextlib import ExitStack
    nc = bass.Bass("TRN2", target_bir_lowering=False)
    NT = S // 512
    h_d = nc.dram_tensor("hT_all", [NT, 128, KC, 512], BF16, kind="ExternalInput").ap()
    wq_d = nc.dram_tensor("wq", [128, KC, 256], F32, kind="ExternalInput").ap()
    wk_d = nc.dram_tensor("wk", [128, KC, 256], F32, kind="ExternalInput").ap()
    wv_d = nc.dram_tensor("wv", [128, KC, 256], F32, kind="ExternalInput").ap()
    em_d = nc.dram_tensor("emask", [3, 2, 128, 256], F32, kind="ExternalInput").ap()
    o_d = nc.dram_tensor("oT", [256, S], BF16, kind="ExternalOutput").ap()
    vd = nc.dram_tensor("v_scratch", [S, 256], BF16).ap()
    scale = 128.0 ** -0.5
    with ExitStack() as st:
        C = Ctx(nc, st)
        P = Prog(nc)
        wq = C.sbuf([128, KC, 256], BF16, "wq_sb")
        wk = C.sbuf([128, KC, 256], BF16, "wk_sb")
        wv = C.sbuf([128, KC, 256], BF16, "wv_sb")
        em = C.sbuf([128, 3, 2, 256], F32, "em_sb")
        QT = C.sbuf([128, 2, S], BF16, "QT")
        KT = C.sbuf([128, 2, S], BF16, "KT")
        hbuf = [C.sbuf([128, KC, 512], BF16, f"hbuf{i}") for i in range(2)]
        vst = [C.sbuf([128, 2, 256], BF16, f"vst{i}") for i in range(2)]
        vn = [C.sbuf([128, 16, 256], BF16, f"vn{i}") for i in range(2)]
        v4 = [C.sbuf([128, 4, 4, 256], BF16, f"v4{i}") for i in range(2)]
        v16 = [C.sbuf([128, 16, 256], BF16, f"v16{i}") for i in range(2)]
        vn.append(hbuf[0][:, 0:8, :].rearrange("p a (b c) -> p (a b) c", c=256))
        v4.append(hbuf[0][:, 8:16, :].rearrange("p (r a) (b c) -> p r (a b) c", r=4, c=256))
        v16.append(hbuf[1][:, 0:8, :].rearrange("p a (b c) -> p (a b) c", c=256))
        NV = 3
        oacc = C.sbuf([128, BLK], F32, "oacc")
        lacc = C.sbuf([128, BLK], F32, "lacc")
        NP = 3
        pbuf = [C.sbuf([128, 512], F32, f"pbuf{i}") for i in range(NP)]
        ptb = [C.sbuf([128, 512], BF16, f"ptb{i}") for i in range(NP)]
        onesb = C.sbuf([128, 128], BF16, "onesb")
        oout = [C.sbuf([128, BLK], BF16, "oout0")]
        bank = [C.psum([128, 512], F32, f"bk{i}") for i in range(8)]
        s_w, s_hb, s_pj, s_evA, s_evD, s_vst, s_vl = (C.sem(n) for n in
                                                      ("s_w", "s_hb", "s_pj", "s_evA", "s_evD", "s_vst", "s_vl"))
        s_init, s_qk, s_ex, s_ptA, s_ptB, s_pv, s_eA, s_eD, s_fin, s_out = (
            C.sem(n) for n in ("s_init", "s_qk", "s_ex", "s_ptA", "s_ptB", "s_pv", "s_eA", "s_eD", "s_fin", "s_out"))
        P.op("dve", "memset", onesb[:, :], 1.0, inc=s_init)
        P.dma("pool", wq[:, :, :], wq_d, s_w)
        P.dma("pool", wk[:, :, :], wk_d, s_w)
        P.dma("pool", wv[:, :, :], wv_d, s_w)
        P.dma("pool", em[:, :, :, :], em_d.rearrange("d h p c -> p d h c"), s_w)
        vdt = vd.rearrange("(n p) c -> p n c", p=128)
        P.wait("pe", s_w, 64)
        P.wait("pe", s_init, 1)
        P.wait("pool", s_w, 64)
        units = []
        n_evA = n_evD = 0
        for tt in range(NT):
            hb = hbuf[tt % 2]
            if tt >= 2:
                P.wait("sp", s_pj, 6 * (tt - 1))
            P.dma("sp", hb[:, 0:KC // 2, :], h_d[tt, :, 0:KC // 2, :], s_hb)
            P.dma("sp", hb[:, KC // 2:KC, :], h_d[tt, :, KC // 2:KC, :], s_hb)
            P.wait("pe", s_hb, 32 * (tt + 1))
            for ui in range(6):
                u = len(units)
                bk = bank[u % 8]
                if u >= 8:
                    P.wait("pe", units[u - 8][0], units[u - 8][1])
                if ui < 4:
                    w = wq if ui < 2 else wk
                    hd = ui % 2
                    for k in range(KC):
                        P.op("pe", "matmul", bk[:, :], w[:, k, hd * 128:(hd + 1) * 128], hb[:, k, :],
                             start=(k == 0), stop=(k == KC - 1), inc=s_pj if k == KC - 1 else None)
                    dst = (QT if ui < 2 else KT)[:, hd, tt * 512:(tt + 1) * 512]
                    P.wait("act", s_pj, u + 1)
                    P.op("act", "activation", out=dst, in_=bk[:, :], func=AF.Copy, inc=s_evA)
                    n_evA += 1
                    units.append((s_evA, n_evA))
                else:
                    sp_ = ui - 4
                    for si in range(2):
                        sub = sp_ * 2 + si
                        for k in range(KC):
                            last = (k == KC - 1) and si == 1
                            P.op("pe", "matmul", bk[:, si * 256:(si + 1) * 256], hb[:, k, sub * 128:(sub + 1) * 128],
                                 wv[:, k, :], start=(k == 0), stop=(k == KC - 1), inc=s_pj if last else None)
                    vb = vst[n_evD % 2]
                    P.wait("dve", s_pj, u + 1)
                    P.wait("dve", s_vst, 16 * (n_evD - 1))
                    P.op("dve", "tensor_copy", out=vb[:, :, :], in_=bk[:, :].rearrange("p (s c) -> p s c", c=256),
                         inc=s_evD)
                    n_evD += 1
                    units.append((s_evD, n_evD))
                    P.wait("pool", s_evD, n_evD)
                    n0 = tt * 4 + sp_ * 2
                    P.dma("pool", vdt[:, n0:n0 + 2, :], vb[:, :, :], s_vst)
        v4d = vd.rearrange("(blk i r) c -> i r blk c", i=128, r=4)
        v16d = vd.rearrange("(blk i r) c -> i r blk c", i=128, r=16)
        P.wait("sp", s_vst, s_vst.n)
        P.wait("pe", s_evA, n_evA)
        groups = []
        for B in range(NB):
            b = B % NV
            pb = (B - 1) % NV
            for hd in range(2):
                hs = slice(hd * 128, (hd + 1) * 128)
                tiles = []
                for ml in range(16):
                    prev = vn[b][:, ml - 1, hs] if ml > 0 else (vn[pb][:, 15, hs] if B > 0 else None)
                    tiles.append((0, 1, B * BLK + ml * 128, vn[b][:, ml, hs], prev, ml * 128))
                for r in range(4):
                    for ml in range(4):
                        prev = v4[b][:, r, ml - 1, hs] if ml > 0 else (v4[pb][:, r, 3, hs] if B > 0 else None)
                        tiles.append((1, 4, B * BLK + r + 4 * 128 * ml, v4[b][:, r, ml, hs], prev, r + 4 * 128 * ml))
                for r in range(16):
                    prev = v16[pb][:, r, hs] if B > 0 else None
                    tiles.append((2, 16, B * BLK + r, v16[b][:, r, hs], prev, r))
                for gi in range(0, len(tiles), 2):
                    groups.append(dict(B=B, hd=hd, tiles=tiles[gi:gi + 2], first=(gi == 0),
                                       last=(gi == len(tiles) - 2)))
        G = len(groups)
        ev_done = []
        st_ = dict(n_eA=0, n_eD=0, n_fin=0, last_d1=(None, 0))
        vl_loaded = set()

        def ensure_v(B):
            if B in vl_loaded or B >= NB:
                return
            vl_loaded.add(B)
            b = B % NV
            if B >= 2:
                P.wait("sp", s_pv, blk_end[B - 2])
                P.wait("sp", s_pj, s_pj.n)
            for q4 in range(4):
                P.dma("sp", vn[b][:, q4 * 4:(q4 + 1) * 4, :], vdt[:, B * 16 + q4 * 4:B * 16 + (q4 + 1) * 4, :], s_vl)
                P.dma("sp", v4[b][:, q4, :, :], v4d[:, q4, 4 * B:4 * B + 4, :], s_vl)
                P.dma("sp", v16[b][:, q4 * 4:(q4 + 1) * 4, :], v16d[:, q4 * 4:(q4 + 1) * 4, B, :], s_vl)

        blk_end = {}
        for B in range(NB):
            blk_end[B] = sum(1 for g_ in groups if g_["B"] <= B)

        def emit_qk(g):
            gr = groups[g]
            hd = gr["hd"]
            di, d = gr["tiles"][0][0], gr["tiles"][0][1]
            sb_ = bank[g % NP]
            P.wait("pe", s_ex, g - NP + 1)
            for ti in range(2):
                _, _, t0, vcur, vprev, _ = gr["tiles"][ti]
                qap = QT[:, hd, ss(t0, 128, d)]
                if vprev is not None:
                    P.op("pe", "matmul", sb_[:, ti * 256:ti * 256 + 128], KT[:, hd, ss(t0 - 128 * d, 128, d)], qap,
                         start=True, stop=True)
                P.op("pe", "matmul", sb_[:, ti * 256 + 128:ti * 256 + 256], KT[:, hd, ss(t0, 128, d)], qap,
                     start=True, stop=True, inc=s_qk if ti == 1 else None)
            P.wait("act", s_qk, g + 1)
            P.wait("act", s_ptA, g - NP + 1)
            P.wait("act", s_ptB, g - NP + 1)
            P.op("act", "activation", out=pbuf[g % NP][:, :], in_=sb_[:, :], func=AF.Exp, scale=scale, inc=s_ex)
            for eng, sem_, c0 in (("dve", s_ptA, 0), ("pool", s_ptB, 256)):
                P.wait(eng, s_ex, g + 1)
                P.wait(eng, s_pv, g - NP + 1)
                P.op(eng, "tensor_tensor", out=ptb[g % NP][:, c0:c0 + 256], in0=pbuf[g % NP][:, c0:c0 + 256],
                     in1=em[:, di, hd, :], op=ALU.mult, inc=sem_)

        def emit_pv(g):
            gr = groups[g]
            hd, B = gr["hd"], gr["B"]
            di, d = gr["tiles"][0][0], gr["tiles"][0][1]
            ob, lb = bank[NP + 2 * (g % 2)], bank[NP + 1 + 2 * (g % 2)]
            pt = ptb[g % NP]
            if gr["first"] and hd == 0:
                P.wait("pe", s_vl, 16 * 12 * (B + 1))
            P.wait("pe", s_ptA, g + 1)
            P.wait("pe", s_ptB, g + 1)
            if g >= 2:
                P.wait("pe", ev_done[g - 2][0], ev_done[g - 2][1])
            for kind in (0, 1):
                dstb = ob if kind == 0 else lb
                for ti in range(2):
                    _, _, t0, vcur, vprev, _ = gr["tiles"][ti]
                    oc = slice(ti * 128, (ti + 1) * 128)
                    last = kind == 1 and ti == 1
                    if vprev is not None:
                        P.op("pe", "matmul", dstb[:, oc], vprev if kind == 0 else onesb[:, :],
                             pt[:, ti * 256:ti * 256 + 128], start=True, stop=False)
                    P.op("pe", "matmul", dstb[:, oc], vcur if kind == 0 else onesb[:, :],
                         pt[:, ti * 256 + 128:ti * 256 + 256], start=(vprev is None), stop=True,
                         inc=s_pv if last else None)
            if di == 0:
                P.wait("act", s_pv, g + 1)
                if gr["first"]:
                    P.wait("act", s_fin, st_["n_fin"])
                for ti in range(2):
                    loc = gr["tiles"][ti][5]
                    oc = slice(ti * 128, (ti + 1) * 128)
                    dsl = ss(loc, 128, d)
                    P.op("act", "activation", out=oacc[:, dsl], in_=ob[:, oc], func=AF.Copy)
                    P.op("act", "activation", out=lacc[:, dsl], in_=lb[:, oc], func=AF.Copy,
                         inc=s_eA if ti == 1 else None)
                st_["n_eA"] += 1
                ev_done.append((s_eA, st_["n_eA"]))
                st_["last_d1"] = (s_eA, st_["n_eA"])
            else:
                P.wait("dve", s_pv, g + 1)
                P.wait("dve", st_["last_d1"][0], st_["last_d1"][1])
                for ti in range(2):
                    loc = gr["tiles"][ti][5]
                    oc = slice(ti * 128, (ti + 1) * 128)
                    dsl = ss(loc, 128, d)
                    P.op("dve", "tensor_tensor", out=oacc[:, dsl], in0=ob[:, oc], in1=oacc[:, dsl], op=ALU.add)
                    P.op("dve", "tensor_tensor", out=lacc[:, dsl], in0=lb[:, oc], in1=lacc[:, dsl], op=ALU.add,
                         inc=s_eD if ti == 1 else None)
                st_["n_eD"] += 1
                ev_done.append((s_eD, st_["n_eD"]))
            if gr["last"]:
                nf = st_["n_fin"]
                ob_ = oout[0]
                P.wait("dve", s_out, 16 * nf)
                P.op("dve", "reciprocal", lacc[:, :], lacc[:, :])
                P.op("dve", "tensor_tensor", out=ob_[:, :], in0=oacc[:, :], in1=lacc[:, :], op=ALU.mult, inc=s_fin)
                st_["n_fin"] += 1
                P.wait("sp", s_fin, st_["n_fin"])
                P.dma("sp", o_d[hd * 128:(hd + 1) * 128, B * BLK:(B + 1) * BLK], ob_[:, :], s_out)
                if hd == 1:
                    ensure_v(B + 2)

        ensure_v(0)
        ensure_v(1)
        emit_qk(0)
        for g in range(G):
            if g + 1 < G:
                emit_qk(g + 1)
            emit_pv(g)
        P.wait("sp", s_out, s_out.n)
        P.run()
    return nc


def tile_hT(hT_all):
    return np.ascontiguousarray(hT_all.reshape(KC, 128, S // 512, 512).transpose(2, 1, 0, 3))


LC = 4


def build_mla_program():
    from contextlib import ExitStack
    nc = bass.Bass("TRN2", target_bir_lowering=False)
    cq_d = nc.dram_tensor("cqT_all", [512, S], BF16, kind="ExternalInput").ap()
    ckv_d = nc.dram_tensor("ckvT_all", [512, S], BF16, kind="ExternalInput").ap()
    kr_d = nc.dram_tensor("krT_all", [64, S], BF16, kind="ExternalInput").ap()
    wqn_d = nc.dram_tensor("wuq_n", [128, LC, 256], F32, kind="ExternalInput").ap()
    wqr_d = nc.dram_tensor("wuq_r", [128, LC, 128], F32, kind="ExternalInput").ap()
    wqr2_d = nc.dram_tensor("wuq_r2", [128, LC, 128], F32, kind="ExternalInput").ap()
    wuk_d = nc.dram_tensor("wuk", [128, LC, 256], F32, kind="ExternalInput").ap()
    wuv_d = nc.dram_tensor("wuv", [128, LC, 256], F32, kind="ExternalInput").ap()
    cs_d = nc.dram_tensor("cs", [2, 128, S], F32, kind="ExternalInput").ap()
    cst_d = nc.dram_tensor("cst", [2, 128, 128], BF16, kind="ExternalInput").ap()
    o_d = nc.dram_tensor("oT", [256, S], BF16, kind="ExternalOutput").ap()
    scale = 192.0 ** -0.5
    with ExitStack() as st:
        C = Ctx(nc, st)
        P = Prog(nc)
        wqn = C.sbuf([128, LC, 256], BF16, "wqn")
        wqr = C.sbuf([128, LC, 128], BF16, "wqr")
        wqr2 = C.sbuf([128, LC, 128], BF16, "wqr2")
        wuk = C.sbuf([128, LC, 256], BF16, "wuk_sb")
        wuv = C.sbuf([128, LC, 256], BF16, "wuv_sb")
        cst = C.sbuf([128, 2, 128], BF16, "cst_sb")
        KnT = C.sbuf([128, 2, S], BF16, "KnT")
        krT = C.sbuf([128, S], BF16, "krT2")
        Vs = C.sbuf([128, S // 128, 256], BF16, "Vs")
        QnT = C.sbuf([128, 2, S], BF16, "QnT")
        QrT = C.sbuf([128, S], BF16, "QrT")
        ckb = [C.sbuf([128, LC, 512], BF16, f"ckb{i}") for i in range(2)]
        cqb = [C.sbuf([128, LC, 512], BF16, f"cqb{i}") for i in range(2)]
        csb = [C.sbuf([128, 2, 512], F32, f"csb{i}") for i in range(2)]
        tmp1 = C.sbuf([128, 512], F32, "tmp1")
        tmp2 = C.sbuf([128, 512], F32, "tmp2")
        ptb = [C.sbuf([128, 512], BF16, f"ptb{i}") for i in range(3)]
        rl = C.sbuf([128, 512], F32, "rl")
        oout = [C.sbuf([128, 512], BF16, f"oout{i}") for i in range(2)]
        onesb = C.sbuf([128, 128], BF16, "onesb")
        bank = [C.psum([128, 512], F32, f"bk{i}") for i in range(8)]
        s_w, s_in, s_pj, s_evA, s_evD, s_init = (C.sem(n) for n in ("s_w", "s_in", "s_pj", "s_evA", "s_evD", "s_init"))
        s_qk, s_ex, s_pv, s_fin, s_out, s_acc = (C.sem(n) for n in ("s_qk", "s_ex", "s_pv", "s_fin", "s_out", "s_acc"))
        P.op("dve", "memset", onesb[:, :], 1.0, inc=s_init)
        for dst, src in ((wqn, wqn_d), (wqr, wqr_d), (wqr2, wqr2_d), (wuk, wuk_d), (wuv, wuv_d)):
            P.dma("pool", dst[:, :, :], src, s_w)
        P.dma("pool", cst[:, :, :], cst_d.rearrange("a p c -> p a c"), s_w)
        P.dma("pool", krT[0:64, :], kr_d, s_w)
        P.dma("pool", krT[64:128, :], kr_d, s_w)
        NW = 8 * 16
        ident, tri = cst[:, 0, :], cst[:, 1, :]
        ckv_v = ckv_d.rearrange("(k p) t -> p k t", p=128)
        cq_v = cq_d.rearrange("(k p) t -> p k t", p=128)
        cs_v = cs_d.rearrange("a p t -> p a t")
        NT = S // 512
        P.wait("pe", s_w, NW)
        P.wait("pe", s_init, 1)
        units = []
        n_evA = n_evD = 0
        for tt in range(NT):
            b = tt % 2
            tsl = slice(tt * 512, (tt + 1) * 512)
            if tt >= 2:
                P.wait("sp", s_pj, 8 * (tt - 1))
                P.wait("sp", s_evD, evd_at[tt - 2])
            P.dma("sp", ckb[b][:, :, :], ckv_v[:, :, tsl], s_in)
            P.dma("sp", cqb[b][:, :, :], cq_v[:, :, tsl], s_in)
            P.dma("sp", csb[b][:, :, :], cs_v[:, :, tsl], s_in)
            P.wait("pe", s_in, 48 * (tt + 1))
            if tt == 0:
                evd_at = {}
            for ui in range(8):
                u = len(units)
                bk = bank[u % 8]
                if u >= 8:
                    P.wait("pe", units[u - 8][0], units[u - 8][1])
                if ui in (0, 1, 4, 5):
                    hd = ui % 2
                    w, src, dst = (wuk, ckb[b], KnT) if ui < 2 else (wqn, cqb[b], QnT)
                    for k in range(LC):
                        P.op("pe", "matmul", bk[:, :], w[:, k, hd * 128:(hd + 1) * 128], src[:, k, :],
                             start=(k == 0), stop=(k == LC - 1), inc=s_pj if k == LC - 1 else None)
                    P.wait("act", s_pj, u + 1)
                    P.op("act", "activation", out=dst[:, hd, tsl], in_=bk[:, :], func=AF.Copy, inc=s_evA)
                    n_evA += 1
                    units.append((s_evA, n_evA))
                elif ui in (2, 3):
                    sp_ = ui - 2
                    for si in range(2):
                        sub = sp_ * 2 + si
                        for k in range(LC):
                            last = (k == LC - 1) and si == 1
                            P.op("pe", "matmul", bk[:, si * 256:(si + 1) * 256], ckb[b][:, k, sub * 128:(sub + 1) * 128],
                                 wuv[:, k, :], start=(k == 0), stop=(k == LC - 1), inc=s_pj if last else None)
                    P.wait("dve", s_pj, u + 1)
                    n0 = tt * 4 + sp_ * 2
                    P.op("dve", "tensor_copy", out=Vs[:, n0:n0 + 2, :], in_=bk[:, :].rearrange("p (s c) -> p s c", c=256),
                         inc=s_evD)
                    n_evD += 1
                    units.append((s_evD, n_evD))
                else:
                    w = wqr if ui == 6 else wqr2
                    for k in range(LC):
                        P.op("pe", "matmul", bk[:, :], w[:, k, :], cqb[b][:, k, :],
                             start=(k == 0), stop=(k == LC - 1), inc=s_pj if k == LC - 1 else None)
                    P.wait("dve", s_pj, u + 1)
                    if ui == 6:
                        P.op("dve", "tensor_tensor", out=tmp1[:, :], in0=bk[:, :], in1=csb[b][:, 0, :], op=ALU.mult,
                             inc=s_evD)
                        n_evD += 1
                    else:
                        P.op("dve", "tensor_tensor", out=tmp2[:, :], in0=bk[:, :], in1=csb[b][:, 1, :], op=ALU.mult,
                             inc=s_evD)
                        n_evD += 1
                        P.wait("dve", s_evD, n_evD)
                        P.op("dve", "tensor_tensor", out=QrT[:, tsl], in0=tmp1[:, :], in1=tmp2[:, :], op=ALU.add,
                             inc=s_evD)
                        n_evD += 1
                    units.append((s_evD, n_evD if ui == 7 else n_evD))
            evd_at[tt] = n_evD
        P.wait("pe", s_evA, n_evA)
        P.wait("pe", s_evD, n_evD)
        steps = []
        for hd in range(2):
            for qt in range(NT):
                nk = 4 * qt + 4
                for kt in range(nk):
                    steps.append((hd, qt, kt, nk))
        n_acc = 0

        def emit_qk(g):
            hd, qt, kt, nk = steps[g]
            i = kt - 4 * qt
            c0 = 128 * i if i > 0 else 0
            sb_ = bank[g % 2]
            q0 = qt * 512
            ksl = slice(kt * 128, (kt + 1) * 128)
            P.wait("pe", s_ex, g - 1)
            P.op("pe", "matmul", sb_[:, c0:512], KnT[:, hd, ksl], QnT[:, hd, q0 + c0:q0 + 512], start=True, stop=False)
            P.op("pe", "matmul", sb_[:, c0:512], krT[hd * 64:(hd + 1) * 64, ksl],
                 QrT[hd * 64:(hd + 1) * 64, q0 + c0:q0 + 512], start=False, stop=(i < 0),
                 inc=s_qk if i < 0 else None)
            if i >= 0:
                P.op("pe", "matmul", sb_[:, c0:c0 + 128], ident, tri, start=False, stop=True, inc=s_qk)
            P.wait("act", s_qk, g + 1)
            P.wait("act", s_pv, g - 2)
            P.op("act", "activation", out=ptb[g % 3][:, c0:512], in_=sb_[:, c0:512], func=AF.Exp, scale=scale, inc=s_ex)

        def emit_pv(g):
            nonlocal n_acc
            hd, qt, kt, nk = steps[g]
            i = kt - 4 * qt
            c0 = 128 * i if i > 0 else 0
            a = n_acc
            ob, lb = bank[2 + 2 * (a % 2)], bank[3 + 2 * (a % 2)]
            P.wait("pe", s_ex, g + 1)
            if kt == 0:
                P.wait("pe", s_fin, a - 1)
            P.op("pe", "matmul", ob[:, c0:512], Vs[:, kt, hd * 128:(hd + 1) * 128], ptb[g % 3][:, c0:512],
                 start=(kt == 0), stop=(kt == nk - 1))
            P.op("pe", "matmul", lb[:, c0:512], onesb[:, :], ptb[g % 3][:, c0:512],
                 start=(kt == 0), stop=(kt == nk - 1), inc=s_pv)
            if kt == nk - 1:
                q0 = qt * 512
                P.wait("dve", s_pv, g + 1)
                P.wait("dve", s_out, 16 * (a - 1))
                P.op("dve", "reciprocal", rl[:, :], lb[:, :], inc=s_acc)
                P.wait("dve", s_acc, a + 1)
                P.op("dve", "tensor_tensor", out=oout[a % 2][:, :], in0=ob[:, :], in1=rl[:, :], op=ALU.mult, inc=s_fin)
                P.wait("sp", s_fin, a + 1)
                P.dma("sp", o_d[hd * 128:(hd + 1) * 128, q0:q0 + 512], oout[a % 2][:, :], s_out)
                n_acc += 1

        G = len(steps)
        emit_qk(0)
        for g in range(G):
            if g + 1 < G:
                emit_qk(g + 1)
            emit_pv(g)
        P.wait("sp", s_out, s_out.n)
        P.run()
    return nc


def rope_tables_ext(pos):
    inv = (1.0 / (10000.0 ** (np.arange(0, 64, 2, dtype=np.float32) / np.float32(64)))).astype(np.float32)
    ang = pos.astype(np.float32)[None, :] * inv[:, None]
    cos, sin = np.cos(ang).astype(np.float32), np.sin(ang).astype(np.float32)
    return np.concatenate([cos, cos], 0), np.concatenate([-sin, sin], 0)


def mla_consts():
    k = np.arange(128)[:, None]
    j = np.arange(128)[None, :]
    tri = np.where(k <= j, 0.0, -30000.0).astype(np.float32)
    return np.stack([np.eye(128, dtype=np.float32), tri]).astype(ml_dtypes.bfloat16)


def til(w, ncols=128):
    din, dout = w.shape
    return np.ascontiguousarray(w.reshape(din // 128, 128, dout // ncols, ncols).transpose(2, 1, 0, 3))


def build_token_program(wo=False, ffn2=False, latent=False, ffn1=False, mix=None, final=False):
    from contextlib import ExitStack
    nc = bass.Bass("TRN2", target_bir_lowering=False)

    def din(name, shape, dt=F32):
        return nc.dram_tensor(name, list(shape), dt, kind="ExternalInput").ap()

    def dout(name, shape, dt=F32):
        return nc.dram_tensor(name, list(shape), dt, kind="ExternalOutput").ap()

    x_d = din("xT_in", [D, T])
    y_d = dout("xT_out", [D, T])
    gnames = []
    if ffn2:
        gnames.append("g_ffn2")
    if latent:
        gnames += ["g_kv"]
    if ffn1:
        gnames.append("g_ffn1")
    if mix:
        gnames.append("g_mix")
    if final:
        gnames.append("g_final")
    g_d = {n: din(n, [128, KC]) for n in gnames}
    if wo:
        o_d = din("oT_in", [D, T], BF16)
        wo_d = din("wo", [KC, 128, KC, 128])
    if ffn2:
        w2 = (din("f2_wg", [FC, 128, KC, 128]), din("f2_wu", [FC, 128, KC, 128]), din("f2_wd", [2, KC, 128, FH, 128]))
    if ffn1:
        w1 = (din("f1_wg", [FC, 128, KC, 128]), din("f1_wu", [FC, 128, KC, 128]), din("f1_wd", [2, KC, 128, FH, 128]))
    if latent:
        wdkv_d = din("wdkv", [LC, 128, KC, 128])
        gckv_d = din("g_ckv", [128, LC])
        wkr_d = din("wkr", [2, 128, KC, 64])
        cs_d = din("cs_tok", [2, 64, T])
        ckv_o = dout("ckvT_out", [512, T], BF16)
        kr_o = dout("krT_out", [64, T], BF16)
    if mix == "cq":
        wdq_d = din("wdq", [LC, 128, KC, 128])
        gcq_d = din("g_cq", [128, LC])
        cq_o = dout("cqT_out", [512, T], BF16)
    if mix == "h":
        h_o = dout("hT_out", [D, T], BF16)

    with ExitStack() as st:
        C = Ctx(nc, st)
        P = Prog(nc)
        tp = TokenPhase(nc, P, C)
        tp.epsb = C.sbuf([128, 1], F32, "epsb")
        P.op("dve", "memset", tp.epsb[:, :], EPS, inc=tp.s_init)
        P.wait("act", tp.s_init, 2)
        g_sb = {n: C.sbuf([128, KC], F32, n + "_sb") for n in gnames}
        if wo:
            s_o = C.sem("s_o")
            ov = o_d.rearrange("(k p) t -> p k t", p=128)
            P.dma("sp", tp.aT[:, 0:KC // 2, :], ov[:, 0:KC // 2, :], s_o)
            P.dma("act", tp.aT[:, KC // 2:KC, :], ov[:, KC // 2:KC, :], s_o)
        tp.load_x(x_d)
        for n in gnames:
            tp.load(g_sb[n][:, :], g_d[n][:, :])
        if latent or mix == "cq":
            lat = C.sbuf([128, LC, T], F32, "lat")
            latb = tp.aT[:, 0:LC, :]
            s_misc = C.sem("s_misc")
        if latent:
            gckv = C.sbuf([128, LC], F32, "gckv_sb")
            cs_sb = C.sbuf([64, 2, T], F32, "cs_sb")
            krb = tp.aT[0:64, LC + 1, :]
            tp.load(gckv[:, :], gckv_d[:, :])
            tp.load(cs_sb[:, :, :], cs_d.rearrange("a p t -> p a t"))
        if mix == "cq":
            gcq = C.sbuf([128, LC], F32, "gcq_sb")
            tp.load(gcq[:, :], gcq_d[:, :])
        n_lat_out = 0
        if wo:
            P.wait("dve", tp.s_x, tp.s_x.n)
            tp.add_proj(tp.aT, KC, [wo_d[dc] for dc in range(KC)], src_ready=(s_o, 32))
        if ffn2:
            tp.norm(g_sb["g_ffn2"], tp.hT)
            tp.ffn(*w2)
        if latent:
            tp.norm(g_sb["g_kv"], tp.hT)
            tp.proj_to(tp.hT, KC, [wdkv_d[oc] for oc in range(LC)], lat)
            tp.norm(gckv, latb, src=lat, nk=LC)
            P.wait("sp", tp.s_h, tp.s_h.n)
            P.dma("sp", ckv_o.rearrange("(k p) t -> p k t", p=128), latb, tp.s_out)
            n_lat_out = tp.s_out.n
            tA = lat[0:64, 0, :]
            tB = lat[0:64, 1, :]
            P.wait("dve", tp.s_h, tp.s_h.n)

            def kr_evac(v, th, tsl, bk, inc):
                P.op("dve", "tensor_tensor", out=(tA if v == 0 else tB)[:, tsl], in0=bk[0:64, :],
                     in1=cs_sb[:, v, tsl], op=ALU.mult, inc=inc)
            tp.proj_units(tp.hT, KC, [wkr_d[0], wkr_d[1]], 64, kr_evac)
            P.wait("dve", tp.s_y, tp.s_y.n)
            P.op("dve", "tensor_tensor", out=krb, in0=tA, in1=tB, op=ALU.add, inc=s_misc)
            P.wait("sp", s_misc, s_misc.n)
            P.dma("sp", kr_o, krb, tp.s_out)
        if ffn1:
            tp.norm(g_sb["g_ffn1"], tp.hT)
            tp.ffn(*w1)
        if mix:
            tp.norm(g_sb["g_mix"], tp.hT)
        if mix == "h":
            P.wait("sp", tp.s_h, tp.s_h.n)
            hv = h_o.rearrange("(k p) t -> p k t", p=128)
            for k in range(0, KC, 4):
                P.dma("sp", hv[:, k:k + 4, :], tp.hT[:, k:k + 4, :], tp.s_out)
        if mix == "cq":
            if latent:
                P.wait("dve", tp.s_out, n_lat_out)
                P.wait("dve", s_misc, s_misc.n)
            tp.proj_to(tp.hT, KC, [wdq_d[oc] for oc in range(LC)], lat)
            tp.norm(gcq, latb, src=lat, nk=LC)
            P.wait("sp", tp.s_h, tp.s_h.n)
            P.dma("sp", cq_o.rearrange("(k p) t -> p k t", p=128), latb, tp.s_out)
        if final:
            tp.norm(g_sb["g_final"], tp.xT)
            P.wait("sp", tp.s_h, tp.s_h.n)
        tp.store_x(y_d)
        tp.finish()
        P.run()
    return nc


_PROGS = {}


def _prog(key, fn, **kw):
    if key not in _PROGS:
        _PROGS[key] = fn(**kw)
    return _PROGS[key]


def _launch(nc, maps):
    res = run_bass_kernel_spmd(nc, maps, core_ids=list(range(NCORES)))
    return res.results


def _f32(a):
    return np.ascontiguousarray(np.asarray(a, dtype=np.float32))


def kernel(x, ffn_norm1, ffn1_wg, ffn1_wu, ffn1_wd, mix_norm, ffn_norm2, ffn2_wg, ffn2_wu, ffn2_wd,
           a_wqkv, a_wo, kv_norm, b_wdkv, b_ckv_norm, b_wkr, b_wuk, b_wuv,
           b_wdq, b_cq_norm, b_wuq, b_wo, final_norm):
    x = _f32(x)
    xT = np.ascontiguousarray(x[0].T)
    xs = [np.ascontiguousarray(xT[:, c * T:(c + 1) * T]) for c in range(NCORES)]
    swap = (np.arange(64) + 32) % 64

    def ffn_w(l, which):
        wg, wu, wd = (ffn1_wg, ffn1_wu, ffn1_wd) if which == 1 else (ffn2_wg, ffn2_wu, ffn2_wd)
        p = "f1_" if which == 1 else "f2_"
        return {p + "wg": tile_wgu(_f32(wg[l])), p + "wu": tile_wgu(_f32(wu[l])), p + "wd": tile_wd(_f32(wd[l]))}

    def gc(g):
        return gcol_of(_f32(g))

    def gath_tok(res, name):
        return np.ascontiguousarray(np.concatenate([np.asarray(r[name]) for r in res], axis=1))

    def tok_shards(full):
        return [np.ascontiguousarray(full[:, c * T:(c + 1) * T]) for c in range(NCORES)]

    com = dict(ffn_w(0, 1), g_ffn1=gc(ffn_norm1[0]), g_mix=gc(mix_norm[0]))
    nc = _prog("T_first", build_token_program, ffn1=True, mix="h")
    res = _launch(nc, [dict(com, xT_in=xs[c]) for c in range(NCORES)])
    xs = [np.asarray(r["xT_out"]) for r in res]
    hT_all = gath_tok(res, "hT_out")
    oT = None
    for l in range(DEPTH):
        if l < 2:
            wqkv = _f32(a_wqkv[l])

            def hw(w, c):
                return np.ascontiguousarray(w[:, c * 256:(c + 1) * 256].reshape(KC, 128, 256).transpose(1, 0, 2))
            nc = _prog("A", build_attn_a_program)
            hT_t = tile_hT(hT_all)
            maps = [{"hT_all": hT_t, "wq": hw(wqkv[:, 0:2048], c), "wk": hw(wqkv[:, 2048:4096], c),
                     "wv": hw(wqkv[:, 4096:6144], c), "emask": alibi_masks([2 * c, 2 * c + 1])}
                    for c in range(NCORES)]
            res = _launch(nc, maps)
            wo_l = _f32(a_wo[l])
        else:
            jb = l - 2
            wuq, wuk, wuv = _f32(b_wuq[jb]), _f32(b_wuk), _f32(b_wuv)
            ce, se = rope_tables_ext(np.arange(S))
            cs = np.stack([np.concatenate([ce, ce], 0), np.concatenate([se, se], 0)])

            def t4(w):
                return np.ascontiguousarray(w.reshape(LC, 128, -1).transpose(1, 0, 2))
            nc = _prog("M", build_mla_program)
            maps = []
            for c in range(NCORES):
                hs = [2 * c, 2 * c + 1]
                maps.append({"cqT_all": cqT_all, "ckvT_all": ckvT_all, "krT_all": krT_all,
                             "wuq_n": t4(np.concatenate([wuq[:, h, :128] for h in hs], 1)),
                             "wuq_r": t4(np.concatenate([wuq[:, h, 128:] for h in hs], 1)),
                             "wuq_r2": t4(np.concatenate([wuq[:, h, 128:][:, swap] for h in hs], 1)),
                             "wuk": t4(np.concatenate([wuk[:, h] for h in hs], 1)),
                             "wuv": t4(np.concatenate([wuv[:, h] for h in hs], 1)),
                             "cs": cs, "cst": mla_consts()})
            res = _launch(nc, maps)
            wo_l = _f32(b_wo[jb])
        oT_full = np.ascontiguousarray(np.concatenate([np.asarray(r["oT"]) for r in res], axis=0))
        oTs = tok_shards(oT_full)
        com = dict(ffn_w(l, 2), wo=til(wo_l), g_ffn2=gc(ffn_norm2[l]))
        if l == DEPTH - 1:
            nc = _prog("T_last", build_token_program, wo=True, ffn2=True, final=True)
            com["g_final"] = gc(final_norm)
            res = _launch(nc, [dict(com, xT_in=xs[c], oT_in=oTs[c]) for c in range(NCORES)])
            outT = np.concatenate([np.asarray(r["xT_out"]) for r in res], axis=1)
            return np.ascontiguousarray(outT.T)[None].astype(np.float32)
        com.update(ffn_w(l + 1, 1))
        com["g_ffn1"] = gc(ffn_norm1[l + 1])
        com["g_mix"] = gc(mix_norm[l + 1])
        per_core = [dict(xT_in=xs[c], oT_in=oTs[c]) for c in range(NCORES)]
        if l + 1 < 2:
            nc = _prog("T_mid_h", build_token_program, wo=True, ffn2=True, ffn1=True, mix="h")
        else:
            jb = l + 1 - 2
            com["wdq"] = til(_f32(b_wdq[jb]))
            com["g_cq"] = np.ascontiguousarray(_f32(b_cq_norm[jb]).reshape(LC, 128).T)
            if l + 1 == 2:
                nc = _prog("T_mid_lat", build_token_program, wo=True, ffn2=True, latent=True, ffn1=True, mix="cq")
                com["g_kv"] = gc(kv_norm)
                com["wdkv"] = til(_f32(b_wdkv))
                com["g_ckv"] = np.ascontiguousarray(_f32(b_ckv_norm).reshape(LC, 128).T)
                wkr = _f32(b_wkr)
                com["wkr"] = np.stack([til(wkr, 64)[0], til(np.ascontiguousarray(wkr[:, swap]), 64)[0]])
                for c in range(NCORES):
                    ce, se = rope_tables_ext(np.arange(c * T, (c + 1) * T))
                    per_core[c]["cs_tok"] = np.stack([ce, se])
            else:
                nc = _prog("T_mid_cq", build_token_program, wo=True, ffn2=True, ffn1=True, mix="cq")
        res = _launch(nc, [dict(com, **per_core[c]) for c in range(NCORES)])
        xs = [np.asarray(r["xT_out"]) for r in res]
        if l + 1 < 2:
            hT_all = gath_tok(res, "hT_out")
        else:
            cqT_all = gath_tok(res, "cqT_out")
            if l + 1 == 2:
                ckvT_all = gath_tok(res, "ckvT_out")
                krT_all = gath_tok(res, "krT_out")
```

```python
import numpy as np
import ml_dtypes
import concourse.bass as bass
import concourse.mybir as mybir
from concourse.bass_utils import run_bass_kernel_spmd

F32 = mybir.dt.float32
BF16 = mybir.dt.bfloat16
AF = mybir.ActivationFunctionType
ALU = mybir.AluOpType

NCORES = 8
D = 2048
S = 8192
T = S // NCORES
KC = D // 128
DFF = 5632
FC = DFF // 128
FH = FC // 2
EPS = 1e-6
DEPTH = 4


class Sem:
    def __init__(self, h):
        self.h = h
        self.n = 0


class Prog:
    def __init__(self, nc):
        self.nc = nc
        self.q = {"pe": [], "act": [], "dve": [], "pool": [], "sp": []}

    def op(self, eng, name, *args, inc=None, incv=None, **kw):
        if inc is not None:
            v = incv if incv is not None else 1
            inc.n += v
            h = inc.h

            def f(e, name=name, args=args, kw=kw, h=h, v=v):
                getattr(e, name)(*args, **kw).then_inc(h, v)
        else:
            def f(e, name=name, args=args, kw=kw):
                getattr(e, name)(*args, **kw)
        self.q[eng].append(f)

    def dma(self, eng, out, in_, sem):
        self.op(eng, "dma_start", out=out, in_=in_, inc=sem, incv=16)

    def wait(self, eng, sem, val):
        if val <= 0:
            return
        h = sem.h
        self.q[eng].append(lambda e, h=h, val=val: e.wait_ge(h, val))

    def run(self):
        with self.nc.Block() as block:
            block.tensor(lambda e: [f(e) for f in self.q["pe"]])
            block.scalar(lambda e: [f(e) for f in self.q["act"]])
            block.vector(lambda e: [f(e) for f in self.q["dve"]])
            block.gpsimd(lambda e: [f(e) for f in self.q["pool"]])
            block.sync(lambda e: [f(e) for f in self.q["sp"]])


class Ctx:
    def __init__(self, nc, stack):
        self.nc = nc
        self.stack = stack
        self.k = 0

    def sbuf(self, shape, dt, name=None):
        self.k += 1
        return self.stack.enter_context(self.nc.sbuf_tensor(name or f"sb{self.k}", list(shape), dt))

    def psum(self, shape, dt=F32, name=None):
        self.k += 1
        return self.stack.enter_context(self.nc.psum_tensor(name or f"ps{self.k}", list(shape), dt))

    def sem(self, name=None):
        self.k += 1
        return Sem(self.stack.enter_context(self.nc.semaphore(name or f"sem{self.k}")))


class TokenPhase:
    def __init__(self, nc, P, C):
        self.nc, self.P, self.C = nc, P, C
        self.xT = C.sbuf([128, KC, T], F32, "xT")
        self.hT = C.sbuf([128, KC, T], BF16, "hT")
        self.aT = C.sbuf([128, FH, T], BF16, "aT")
        self.scr = C.sbuf([128, 4, 512], F32, "scr")
        self.ones = C.sbuf([128, 128], F32, "ones")
        self.rstd = C.sbuf([128, 512], F32, "rstd")
        self.sg = [C.sbuf([128, 512], F32, f"sg{i}") for i in range(2)]
        self.wg = [C.sbuf([128, KC, 128], BF16, f"wg{i}") for i in range(2)]
        self.wu = [C.sbuf([128, KC, 128], BF16, f"wu{i}") for i in range(2)]
        self.wd = [C.sbuf([128, FH, 128], BF16, f"wd{i}") for i in range(2)]
        self.bank = [C.psum([128, 512], F32, f"bk{i}") for i in range(8)]
        self.s_init = C.sem("s_init")
        self.s_x = C.sem("s_x")
        self.s_sq = C.sem("s_sq")
        self.s_st = C.sem("s_st")
        self.s_sqrt = C.sem("s_sqrt")
        self.s_rs = C.sem("s_rs")
        self.s_h = C.sem("s_h")
        self.s_wgu = C.sem("s_wgu")
        self.s_gu = C.sem("s_gu")
        self.s_sl = C.sem("s_sl")
        self.s_a = C.sem("s_a")
        self.s_wd = C.sem("s_wd")
        self.s_dn = C.sem("s_dn")
        self.s_y = C.sem("s_y")
        self.s_out = C.sem("s_out")
        self.n_wgu = self.n_wd = self.n_gu = self.n_dn = self.n_hn = self.n_sq = 0
        P.op("dve", "memset", self.ones[:, :], 1.0, inc=self.s_init)
        P.wait("pe", self.s_init, 1)

    def load(self, dst, src):
        self.P.dma("sp", dst, src, self.s_x)

    def load_x(self, x_dram):
        xv = x_dram.rearrange("(k p) t -> p k t", p=128)
        for k in range(0, KC, 4):
            self.load(self.xT[:, k:k + 4, :], xv[:, k:k + 4, :])

    def store_x(self, y_dram):
        P = self.P
        yv = y_dram.rearrange("(k p) t -> p k t", p=128)
        P.wait("sp", self.s_y, self.s_y.n)
        for k in range(0, KC, 4):
            P.dma("sp", yv[:, k:k + 4, :], self.xT[:, k:k + 4, :], self.s_out)

    def finish(self):
        self.P.wait("sp", self.s_out, self.s_out.n)

    def norm(self, gcol, out_tile, src=None, nk=KC):
        P = self.P
        src = self.xT if src is None else src
        for e in ("act", "dve"):
            P.wait(e, self.s_x, self.s_x.n)
            P.wait(e, self.s_y, self.s_y.n)
        for th in range(2):
            n = self.n_hn
            self.n_hn += 1
            tsl = slice(th * 512, (th + 1) * 512)
            P.wait("pe", self.s_sqrt, n)
            for k in range(nk):
                q = self.n_sq
                self.n_sq += 1
                P.wait("act", self.s_st, q - 3)
                P.op("act", "activation", out=self.scr[:, q % 4, :], in_=src[:, k, tsl],
                     func=AF.Square, inc=self.s_sq)
                P.wait("pe", self.s_sq, q + 1)
                P.op("pe", "matmul", self.bank[7][:, :], self.ones[:, :], self.scr[:, q % 4, :],
                     start=(k == 0), stop=(k == nk - 1), inc=self.s_st)
            P.wait("act", self.s_st, self.n_sq)
            P.wait("act", self.s_h, self.s_h.n)
            P.op("act", "activation", out=self.rstd[:, :], in_=self.bank[7][:, :], func=AF.Sqrt,
                 scale=1.0 / (128 * nk), bias=self.epsb[:, 0:1], inc=self.s_sqrt)
            P.wait("dve", self.s_sqrt, n + 1)
            P.op("dve", "reciprocal", self.rstd[:, :], self.rstd[:, :], inc=self.s_rs)
            P.wait("dve", self.s_rs, n + 1)
            for k in range(nk):
                P.op("dve", "scalar_tensor_tensor", out=out_tile[:, k, tsl], in0=src[:, k, tsl],
                     scalar=gcol[:, k:k + 1], op0=ALU.mult, in1=self.rstd[:, :], op1=ALU.mult,
                     inc=self.s_h)

    def ffn(self, wg_d, wu_d, wd_d):
        P = self.P
        h_ready = self.s_h.n
        P.wait("dve", self.s_out, self.s_out.n)
        for fh in range(2):
            gu0 = self.n_gu
            P.wait("pe", self.s_y, self.s_y.n)
            P.wait("pe", self.s_h, h_ready)
            for j in range(FH):
                fc = fh * FH + j
                c = self.n_wgu
                self.n_wgu += 1
                b = c % 2
                P.wait("pool", self.s_gu, 2 * (c - 1))
                P.dma("pool", self.wg[b][:, :, :], wg_d[fc], self.s_wgu)
                P.dma("pool", self.wu[b][:, :, :], wu_d[fc], self.s_wgu)
                P.wait("pe", self.s_wgu, 32 * (c + 1))
                for th in range(2):
                    n = self.n_gu
                    self.n_gu += 1
                    pr = n % 4
                    gps, ups = self.bank[2 * pr], self.bank[2 * pr + 1]
                    tsl = slice(th * 512, (th + 1) * 512)
                    if n - gu0 >= 4:
                        P.wait("pe", self.s_a, n - 3)
                    for k in range(KC):
                        P.op("pe", "matmul", gps[:, :], self.wg[b][:, k, :], self.hT[:, k, tsl],
                             start=(k == 0), stop=(k == KC - 1))
                    for k in range(KC):
                        last = k == KC - 1
                        P.op("pe", "matmul", ups[:, :], self.wu[b][:, k, :], self.hT[:, k, tsl],
                             start=(k == 0), stop=last, inc=self.s_gu if last else None)
                    P.wait("act", self.s_gu, n + 1)
                    P.wait("act", self.s_a, n - 1)
                    P.op("act", "activation", out=self.sg[n % 2][:, :], in_=gps[:, :], func=AF.Silu,
                         inc=self.s_sl)
                    P.wait("dve", self.s_sl, n + 1)
                    P.op("dve", "tensor_tensor", out=self.aT[:, j, tsl], in0=ups[:, :],
                         in1=self.sg[n % 2][:, :], op=ALU.mult, inc=self.s_a)
            P.wait("pe", self.s_a, self.n_gu)
            self.proj_units(self.aT, FH, [wd_d[fh, dc] for dc in range(KC)], 128,
                            lambda dc, th, tsl, bk, inc: P.op(
                                "dve", "scalar_tensor_tensor", out=self.xT[:, dc, tsl], in0=bk[:, :], scalar=0.5,
                                op0=ALU.mult, in1=self.xT[:, dc, tsl], op1=ALU.add, inc=inc))

    def proj_units(self, src, nk, w_list, ncols, evac):
        P = self.P
        P.wait("pe", self.s_sqrt, self.s_sqrt.n)
        P.wait("pe", self.s_h, self.s_h.n)
        P.wait("pe", self.s_x, self.s_x.n)
        for oc, w_ap in enumerate(w_list):
            c = self.n_wd
            self.n_wd += 1
            b = c % 2
            P.wait("pool", self.s_dn, 2 * (c - 1))
            P.dma("pool", self.wd[b][:, 0:nk, 0:ncols], w_ap, self.s_wd)
            P.wait("pe", self.s_wd, 16 * (c + 1))
            for th in range(2):
                m = self.n_dn
                self.n_dn += 1
                bk = self.bank[m % 8]
                tsl = slice(th * 512, (th + 1) * 512)
                P.wait("pe", self.s_y, m - 7)
                for j in range(nk):
                    last = j == nk - 1
                    P.op("pe", "matmul", bk[0:ncols, :], self.wd[b][:, j, 0:ncols], src[:, j, tsl],
                         start=(j == 0), stop=last, inc=(self.s_dn if last else None))
                P.wait("dve", self.s_dn, m + 1)
                evac(oc, th, tsl, bk, self.s_y)

    def add_proj(self, src, nk, w_list):
        P = self.P
        self.proj_units(src, nk, w_list, 128,
                        lambda oc, th, tsl, bk, inc: P.op(
                            "dve", "tensor_tensor", out=self.xT[:, oc, tsl], in0=bk[:, :],
                            in1=self.xT[:, oc, tsl], op=ALU.add, inc=inc))

    def proj_to(self, src, nk, w_list, dst):
        P = self.P
        self.proj_units(src, nk, w_list, 128,
                        lambda oc, th, tsl, bk, inc: P.op(
                            "dve", "tensor_copy", out=dst[:, oc, tsl], in_=bk[:, :], inc=inc))


def tile_wgu(w):
    return np.ascontiguousarray(w.reshape(KC, 128, FC, 128).transpose(2, 1, 0, 3))


def tile_wd(w):
    return np.ascontiguousarray(w.reshape(2, FH, 128, KC, 128).transpose(0, 3, 2, 1, 4))


def gcol_of(g):
    return np.ascontiguousarray(g.reshape(KC, 128).T)


def build_ffn_program():
    from contextlib import ExitStack
    nc = bass.Bass("TRN2", target_bir_lowering=False)
    x_d = nc.dram_tensor("xT_in", [D, T], F32, kind="ExternalInput").ap()
    g_d = nc.dram_tensor("gcol", [128, KC], F32, kind="ExternalInput").ap()
    wg_d = nc.dram_tensor("wg", [FC, 128, KC, 128], F32, kind="ExternalInput").ap()
    wu_d = nc.dram_tensor("wu", [FC, 128, KC, 128], F32, kind="ExternalInput").ap()
    wd_d = nc.dram_tensor("wd", [2, KC, 128, FH, 128], F32, kind="ExternalInput").ap()
    y_d = nc.dram_tensor("xT_out", [D, T], F32, kind="ExternalOutput").ap()
    with ExitStack() as st:
        C = Ctx(nc, st)
        P = Prog(nc)
        tp = TokenPhase(nc, P, C)
        gcol = C.sbuf([128, KC], F32, "gcol_sb")
        tp.epsb = C.sbuf([128, 1], F32, "epsb")
        P.op("dve", "memset", tp.epsb[:, :], EPS, inc=tp.s_init)
        P.wait("act", tp.s_init, 2)
        tp.load_x(x_d)
        tp.load(gcol[:, :], g_d[:, :])
        tp.norm(gcol, tp.hT)
        tp.ffn(wg_d, wu_d, wd_d)
        tp.store_x(y_d)
        tp.finish()
        P.run()
    return nc


DILS = (1, 4, 16)


def ss(start, n, step):
    return slice(start, start + (n - 1) * step + 1, step)

BLK = 2048
NB = S // BLK


def alibi_masks(heads):
    k = np.arange(128)[:, None].astype(np.float64)
    j = np.arange(128)[None, :].astype(np.float64)
    out = np.zeros((3, len(heads), 128, 256), np.float32)
    for di, d in enumerate(DILS):
        for hi, h in enumerate(heads):
            slope = 2.0 ** (-8.0 * (h + 1) / 16)
            lo = np.where(j >= k, np.exp(-slope * d * np.maximum(j - k, 0.0)), 0.0)
            hi_ = np.where(j <= k, np.exp(-slope * d * np.maximum(128 + j - k, 0.0)), 0.0)
            out[di, hi, :, 0:128] = hi_
            out[di, hi, :, 128:256] = lo
    return out


def build_attn_a_program():
    from contextlib import ExitStack
    nc = bass.Bass("TRN2", target_bir_lowering=False)
    NT = S // 512
    h_d = nc.dram_tensor("hT_all", [NT, 128, KC, 512], BF16, kind="ExternalInput").ap()
    wq_d = nc.dram_tensor("wq", [128, KC, 256], F32, kind="ExternalInput").ap()
    wk_d = nc.dram_tensor("wk", [128, KC, 256], F32, kind="ExternalInput").ap()
    wv_d = nc.dram_tensor("wv", [128, KC, 256], F32, kind="ExternalInput").ap()
    em_d = nc.dram_tensor("emask", [3, 2, 128, 256], F32, kind="ExternalInput").ap()
    o_d = nc.dram_tensor("oT", [256, S], BF16, kind="ExternalOutput").ap()
    vd = nc.dram_tensor("v_scratch", [S, 256], BF16).ap()
    scale = 128.0 ** -0.5
    with ExitStack() as st:
        C = Ctx(nc, st)
        P = Prog(nc)
        wq = C.sbuf([128, KC, 256], BF16, "wq_sb")
        wk = C.sbuf([128, KC, 256], BF16, "wk_sb")
        wv = C.sbuf([128, KC, 256], BF16, "wv_sb")
        em = C.sbuf([128, 3, 2, 256], F32, "em_sb")
        QT = C.sbuf([128, 2, S], BF16, "QT")
        KT = C.sbuf([128, 2, S], BF16, "KT")
        hbuf = [C.sbuf([128, KC, 512], BF16, f"hbuf{i}") for i in range(2)]
        vst = [C.sbuf([128, 2, 256], BF16, f"vst{i}") for i in range(2)]
        vn = [C.sbuf([128, 16, 256], BF16, f"vn{i}") for i in range(2)]
        v4 = [C.sbuf([128, 4, 4, 256], BF16, f"v4{i}") for i in range(2)]
        v16 = [C.sbuf([128, 16, 256], BF16, f"v16{i}") for i in range(2)]
        vn.append(hbuf[0][:, 0:8, :].rearrange("p a (b c) -> p (a b) c", c=256))
        v4.append(hbuf[0][:, 8:16, :].rearrange("p (r a) (b c) -> p r (a b) c", r=4, c=256))
        v16.append(hbuf[1][:, 0:8, :].rearrange("p a (b c) -> p (a b) c", c=256))
        NV = 3
        acc = C.sbuf([128, 2, BLK], F32, "acc")
        oacc, lacc = acc[:, 0, :], acc[:, 1, :]
        NP, NOL, LOOK = 4, 3, 2
        pbuf = [C.sbuf([128, 512], F32, f"pbuf{i}") for i in range(3)]
        ptb = [C.sbuf([128, 512], BF16, f"ptb{i}") for i in range(3)]
        pbuf.append(hbuf[1][:, 8:10, :].rearrange("p a c -> p (a c)").bitcast(F32))
        ptb.append(hbuf[1][:, 10, :])
        onesb = C.sbuf([128, 128], BF16, "onesb")
        oout = [C.sbuf([128, BLK], BF16, "oout0")]
        bank = [C.psum([128, 512], F32, f"bk{i}") for i in range(8)]
        s_w, s_hb, s_pj, s_evA, s_evD, s_vst, s_vl = (C.sem(n) for n in
                                                      ("s_w", "s_hb", "s_pj", "s_evA", "s_evD", "s_vst", "s_vl"))
        s_init, s_qk, s_ex, s_ptA, s_ptB, s_pv, s_eA, s_eD, s_fin, s_out = (
            C.sem(n) for n in ("s_init", "s_qk", "s_ex", "s_ptA", "s_ptB", "s_pv", "s_eA", "s_eD", "s_fin", "s_out"))
        P.op("dve", "memset", onesb[:, :], 1.0, inc=s_init)
        P.dma("pool", wq[:, :, :], wq_d, s_w)
        P.dma("pool", wk[:, :, :], wk_d, s_w)
        P.dma("pool", wv[:, :, :], wv_d, s_w)
        P.dma("pool", em[:, :, :, :], em_d.rearrange("d h p c -> p d h c"), s_w)
        vdt = vd.rearrange("(n p) c -> p n c", p=128)
        P.wait("pe", s_w, 64)
        P.wait("pe", s_init, 1)
        P.wait("pool", s_w, 64)
        units = []
        n_evA = n_evD = 0
        for tt in range(NT):
            hb = hbuf[tt % 2]
            if tt >= 2:
                P.wait("sp", s_pj, 6 * (tt - 1))
            P.dma("sp", hb[:, 0:KC // 2, :], h_d[tt, :, 0:KC // 2, :], s_hb)
            P.dma("sp", hb[:, KC // 2:KC, :], h_d[tt, :, KC // 2:KC, :], s_hb)
            P.wait("pe", s_hb, 32 * (tt + 1))
            for ui in range(6):
                u = len(units)
                bk = bank[u % 8]
                if u >= 8:
                    P.wait("pe", units[u - 8][0], units[u - 8][1])
                if ui < 4:
                    w = wq if ui < 2 else wk
                    hd = ui % 2
                    for k in range(KC):
                        P.op("pe", "matmul", bk[:, :], w[:, k, hd * 128:(hd + 1) * 128], hb[:, k, :],
                             start=(k == 0), stop=(k == KC - 1), inc=s_pj if k == KC - 1 else None)
                    dst = (QT if ui < 2 else KT)[:, hd, tt * 512:(tt + 1) * 512]
                    P.wait("act", s_pj, u + 1)
                    P.op("act", "activation", out=dst, in_=bk[:, :], func=AF.Copy, inc=s_evA)
                    n_evA += 1
                    units.append((s_evA, n_evA))
                else:
                    sp_ = ui - 4
                    for si in range(2):
                        sub = sp_ * 2 + si
                        for k in range(KC):
                            last = (k == KC - 1) and si == 1
                            P.op("pe", "matmul", bk[:, si * 256:(si + 1) * 256], hb[:, k, sub * 128:(sub + 1) * 128],
                                 wv[:, k, :], start=(k == 0), stop=(k == KC - 1), inc=s_pj if last else None)
                    vb = vst[n_evD % 2]
                    P.wait("dve", s_pj, u + 1)
                    P.wait("dve", s_vst, 16 * (n_evD - 1))
                    P.op("dve", "tensor_copy", out=vb[:, :, :], in_=bk[:, :].rearrange("p (s c) -> p s c", c=256),
                         inc=s_evD)
                    n_evD += 1
                    units.append((s_evD, n_evD))
                    P.wait("pool", s_evD, n_evD)
                    n0 = tt * 4 + sp_ * 2
                    P.dma("pool", vdt[:, n0:n0 + 2, :], vb[:, :, :], s_vst)
        v4d = vd.rearrange("(blk i r) c -> i r blk c", i=128, r=4)
        v16d = vd.rearrange("(blk i r) c -> i r blk c", i=128, r=16)
        P.wait("sp", s_vst, s_vst.n)
        P.wait("pe", s_evA, n_evA)
        groups = []
        for B in range(NB):
            b = B % NV
            pb = (B - 1) % NV
            for hd in range(2):
                hs = slice(hd * 128, (hd + 1) * 128)
                tiles = []
                for ml in range(16):
                    prev = vn[b][:, ml - 1, hs] if ml > 0 else (vn[pb][:, 15, hs] if B > 0 else None)
                    tiles.append((0, 1, B * BLK + ml * 128, vn[b][:, ml, hs], prev, ml * 128))
                for r in range(4):
                    for ml in range(4):
                        prev = v4[b][:, r, ml - 1, hs] if ml > 0 else (v4[pb][:, r, 3, hs] if B > 0 else None)
                        tiles.append((1, 4, B * BLK + r + 4 * 128 * ml, v4[b][:, r, ml, hs], prev, r + 4 * 128 * ml))
                for r in range(16):
                    prev = v16[pb][:, r, hs] if B > 0 else None
                    tiles.append((2, 16, B * BLK + r, v16[b][:, r, hs], prev, r))
                for gi in range(0, len(tiles), 2):
                    groups.append(dict(B=B, hd=hd, tiles=tiles[gi:gi + 2], first=(gi == 0),
                                       last=(gi == len(tiles) - 2)))
        G = len(groups)
        ev_done = []
        st_ = dict(n_eA=0, n_eD=0, n_fin=0, last_d1=(None, 0))
        vl_loaded = set()

        def ensure_v(B):
            if B in vl_loaded or B >= NB:
                return
            vl_loaded.add(B)
            b = B % NV
            if B >= 2:
                P.wait("sp", s_pv, blk_end[B - 2])
                P.wait("sp", s_pj, s_pj.n)
            for q4 in range(4):
                P.dma("sp", vn[b][:, q4 * 4:(q4 + 1) * 4, :], vdt[:, B * 16 + q4 * 4:B * 16 + (q4 + 1) * 4, :], s_vl)
                P.dma("sp", v4[b][:, q4, :, :], v4d[:, q4, 4 * B:4 * B + 4, :], s_vl)
                P.dma("sp", v16[b][:, q4 * 4:(q4 + 1) * 4, :], v16d[:, q4 * 4:(q4 + 1) * 4, B, :], s_vl)

        blk_end = {}
        for B in range(NB):
            blk_end[B] = sum(1 for g_ in groups if g_["B"] <= B)

        def emit_qk(g):
            gr = groups[g]
            hd = gr["hd"]
            di, d = gr["tiles"][0][0], gr["tiles"][0][1]
            sb_ = bank[g % NP]
            P.wait("pe", s_ex, g - NP + 1)
            for ti in range(2):
                _, _, t0, vcur, vprev, _ = gr["tiles"][ti]
                qap = QT[:, hd, ss(t0, 128, d)]
                if vprev is not None:
                    P.op("pe", "matmul", sb_[:, ti * 256:ti * 256 + 128], KT[:, hd, ss(t0 - 128 * d, 128, d)], qap,
                         start=True, stop=True)
                P.op("pe", "matmul", sb_[:, ti * 256 + 128:ti * 256 + 256], KT[:, hd, ss(t0, 128, d)], qap,
                     start=True, stop=True, inc=s_qk if ti == 1 else None)
            P.wait("act", s_qk, g + 1)
            P.wait("act", s_ptA, g - NP + 1)
            P.wait("act", s_ptB, g - NP + 1)
            P.op("act", "activation", out=pbuf[g % NP][:, :], in_=sb_[:, :], func=AF.Exp, scale=scale, inc=s_ex)
            for eng, sem_, c0 in (("dve", s_ptA, 0), ("pool", s_ptB, 256)):
                P.wait(eng, s_ex, g + 1)
                P.wait(eng, s_pv, g - NP + 1)
                P.op(eng, "tensor_tensor", out=ptb[g % NP][:, c0:c0 + 256], in0=pbuf[g % NP][:, c0:c0 + 256],
                     in1=em[:, di, hd, :], op=ALU.mult, inc=sem_)

        def emit_pv(g):
            gr = groups[g]
            hd, B = gr["hd"], gr["B"]
            di, d = gr["tiles"][0][0], gr["tiles"][0][1]
            olb = bank[NP + (g % NOL)]
            pt = ptb[g % NP]
            if gr["first"] and hd == 0:
                P.wait("pe", s_vl, 16 * 12 * (B + 1))
            P.wait("pe", s_ptA, g + 1)
            P.wait("pe", s_ptB, g + 1)
            if g >= NOL:
                P.wait("pe", ev_done[g - NOL][0], ev_done[g - NOL][1])
            for kind in (0, 1):
                dstb = olb
                for ti in range(2):
                    _, _, t0, vcur, vprev, _ = gr["tiles"][ti]
                    oc = slice(kind * 256 + ti * 128, kind * 256 + (ti + 1) * 128)
                    last = kind == 1 and ti == 1
                    if vprev is not None:
                        P.op("pe", "matmul", dstb[:, oc], vprev if kind == 0 else onesb[:, :],
                             pt[:, ti * 256:ti * 256 + 128], start=True, stop=False)
                    P.op("pe", "matmul", dstb[:, oc], vcur if kind == 0 else onesb[:, :],
                         pt[:, ti * 256 + 128:ti * 256 + 256], start=(vprev is None), stop=True,
                         inc=s_pv if last else None)
            loc0 = gr["tiles"][0][5]
            if di == 2:
                dst = acc[:, :, :].rearrange("p a (j r) -> p a r j", r=16)[:, :, loc0:loc0 + 2, :]
                srcv = olb[:, :].rearrange("p (a t j) -> p a t j", a=2, t=2)
            else:
                dst = acc[:, :, ss(loc0, 256, d)]
                srcv = olb[:, :].rearrange("p (a c) -> p a c", a=2)
            if di == 0:
                P.wait("act", s_pv, g + 1)
                if gr["first"]:
                    P.wait("act", s_fin, st_["n_fin"])
                P.op("act", "activation", out=dst, in_=srcv, func=AF.Copy, inc=s_eA)
                st_["n_eA"] += 1
                ev_done.append((s_eA, st_["n_eA"]))
                st_["last_d1"] = (s_eA, st_["n_eA"])
            else:
                P.wait("dve", s_pv, g + 1)
                P.wait("dve", st_["last_d1"][0], st_["last_d1"][1])
                P.op("dve", "tensor_tensor", out=dst, in0=srcv, in1=dst, op=ALU.add, inc=s_eD)
                st_["n_eD"] += 1
                ev_done.append((s_eD, st_["n_eD"]))
            if gr["last"]:
                nf = st_["n_fin"]
                ob_ = oout[0]
                P.wait("dve", s_out, 16 * nf)
                P.op("dve", "reciprocal", lacc, lacc)
                P.op("dve", "tensor_tensor", out=ob_[:, :], in0=oacc, in1=lacc, op=ALU.mult, inc=s_fin)
                st_["n_fin"] += 1
                P.wait("sp", s_fin, st_["n_fin"])
                P.dma("sp", o_d[hd * 128:(hd + 1) * 128, B * BLK:(B + 1) * BLK], ob_[:, :], s_out)
                if hd == 1:
                    ensure_v(B + 2)

        ensure_v(0)
        ensure_v(1)
        for g in range(min(LOOK, G)):
            emit_qk(g)
        for g in range(G):
            if g + LOOK < G:
                emit_qk(g + LOOK)
            emit_pv(g)
        P.wait("sp", s_out, s_out.n)
        P.run()
    return nc


def tile_hT(hT_all):
    return np.ascontiguousarray(hT_all.reshape(KC, 128, S // 512, 512).transpose(2, 1, 0, 3))


LC = 4


def build_mla_program():
    from contextlib import ExitStack
    nc = bass.Bass("TRN2", target_bir_lowering=False)
    cq_d = nc.dram_tensor("cqT_all", [512, S], BF16, kind="ExternalInput").ap()
    ckv_d = nc.dram_tensor("ckvT_all", [512, S], BF16, kind="ExternalInput").ap()
    kr_d = nc.dram_tensor("krT_all", [64, S], BF16, kind="ExternalInput").ap()
    wqn_d = nc.dram_tensor("wuq_n", [128, LC, 256], F32, kind="ExternalInput").ap()
    wqr_d = nc.dram_tensor("wuq_r", [128, LC, 128], F32, kind="ExternalInput").ap()
    wqr2_d = nc.dram_tensor("wuq_r2", [128, LC, 128], F32, kind="ExternalInput").ap()
    wuk_d = nc.dram_tensor("wuk", [128, LC, 256], F32, kind="ExternalInput").ap()
    wuv_d = nc.dram_tensor("wuv", [128, LC, 256], F32, kind="ExternalInput").ap()
    cs_d = nc.dram_tensor("cs", [2, 128, S], F32, kind="ExternalInput").ap()
    cst_d = nc.dram_tensor("cst", [2, 128, 128], BF16, kind="ExternalInput").ap()
    o_d = nc.dram_tensor("oT", [256, S], BF16, kind="ExternalOutput").ap()
    scale = 192.0 ** -0.5
    with ExitStack() as st:
        C = Ctx(nc, st)
        P = Prog(nc)
        wqn = C.sbuf([128, LC, 256], BF16, "wqn")
        wqr = C.sbuf([128, LC, 128], BF16, "wqr")
        wqr2 = C.sbuf([128, LC, 128], BF16, "wqr2")
        wuk = C.sbuf([128, LC, 256], BF16, "wuk_sb")
        wuv = C.sbuf([128, LC, 256], BF16, "wuv_sb")
        cst = C.sbuf([128, 2, 128], BF16, "cst_sb")
        KnT = C.sbuf([128, 2, S], BF16, "KnT")
        krT = C.sbuf([128, S], BF16, "krT2")
        Vs = C.sbuf([128, S // 128, 256], BF16, "Vs")
        QnT = C.sbuf([128, 2, S], BF16, "QnT")
        QrT = C.sbuf([128, S], BF16, "QrT")
        ckb = [C.sbuf([128, LC, 512], BF16, f"ckb{i}") for i in range(2)]
        cqb = [C.sbuf([128, LC, 512], BF16, f"cqb{i}") for i in range(2)]
        csb = [C.sbuf([128, 2, 512], F32, f"csb{i}") for i in range(2)]
        tmp1 = C.sbuf([128, 512], F32, "tmp1")
        tmp2 = C.sbuf([128, 512], F32, "tmp2")
        ptb = [C.sbuf([128, 512], BF16, f"ptb{i}") for i in range(3)]
        rl = C.sbuf([128, 512], F32, "rl")
        oout = [C.sbuf([128, 512], BF16, f"oout{i}") for i in range(2)]
        onesb = C.sbuf([128, 128], BF16, "onesb")
        bank = [C.psum([128, 512], F32, f"bk{i}") for i in range(8)]
        s_w, s_in, s_pj, s_evA, s_evD, s_init = (C.sem(n) for n in ("s_w", "s_in", "s_pj", "s_evA", "s_evD", "s_init"))
        s_qk, s_ex, s_pv, s_fin, s_out, s_acc = (C.sem(n) for n in ("s_qk", "s_ex", "s_pv", "s_fin", "s_out", "s_acc"))
        P.op("dve", "memset", onesb[:, :], 1.0, inc=s_init)
        for dst, src in ((wqn, wqn_d), (wqr, wqr_d), (wqr2, wqr2_d), (wuk, wuk_d), (wuv, wuv_d)):
            P.dma("pool", dst[:, :, :], src, s_w)
        P.dma("pool", cst[:, :, :], cst_d.rearrange("a p c -> p a c"), s_w)
        P.dma("pool", krT[0:64, :], kr_d, s_w)
        P.dma("pool", krT[64:128, :], kr_d, s_w)
        NW = 8 * 16
        ident, tri = cst[:, 0, :], cst[:, 1, :]
        ckv_v = ckv_d.rearrange("(k p) t -> p k t", p=128)
        cq_v = cq_d.rearrange("(k p) t -> p k t", p=128)
        cs_v = cs_d.rearrange("a p t -> p a t")
        NT = S // 512
        P.wait("pe", s_w, NW)
        P.wait("pe", s_init, 1)
        units = []
        n_evA = n_evD = 0
        for tt in range(NT):
            b = tt % 2
            tsl = slice(tt * 512, (tt + 1) * 512)
            if tt >= 2:
                P.wait("sp", s_pj, 8 * (tt - 1))
                P.wait("sp", s_evD, evd_at[tt - 2])
            P.dma("sp", ckb[b][:, :, :], ckv_v[:, :, tsl], s_in)
            P.dma("sp", cqb[b][:, :, :], cq_v[:, :, tsl], s_in)
            P.dma("sp", csb[b][:, :, :], cs_v[:, :, tsl], s_in)
            P.wait("pe", s_in, 48 * (tt + 1))
            if tt == 0:
                evd_at = {}
            for ui in range(8):
                u = len(units)
                bk = bank[u % 8]
                if u >= 8:
                    P.wait("pe", units[u - 8][0], units[u - 8][1])
                if ui in (0, 1, 4, 5):
                    hd = ui % 2
                    w, src, dst = (wuk, ckb[b], KnT) if ui < 2 else (wqn, cqb[b], QnT)
                    for k in range(LC):
                        P.op("pe", "matmul", bk[:, :], w[:, k, hd * 128:(hd + 1) * 128], src[:, k, :],
                             start=(k == 0), stop=(k == LC - 1), inc=s_pj if k == LC - 1 else None)
                    P.wait("act", s_pj, u + 1)
                    P.op("act", "activation", out=dst[:, hd, tsl], in_=bk[:, :], func=AF.Copy, inc=s_evA)
                    n_evA += 1
                    units.append((s_evA, n_evA))
                elif ui in (2, 3):
                    sp_ = ui - 2
                    for si in range(2):
                        sub = sp_ * 2 + si
                        for k in range(LC):
                            last = (k == LC - 1) and si == 1
                            P.op("pe", "matmul", bk[:, si * 256:(si + 1) * 256], ckb[b][:, k, sub * 128:(sub + 1) * 128],
                                 wuv[:, k, :], start=(k == 0), stop=(k == LC - 1), inc=s_pj if last else None)
                    P.wait("dve", s_pj, u + 1)
                    n0 = tt * 4 + sp_ * 2
                    P.op("dve", "tensor_copy", out=Vs[:, n0:n0 + 2, :], in_=bk[:, :].rearrange("p (s c) -> p s c", c=256),
                         inc=s_evD)
                    n_evD += 1
                    units.append((s_evD, n_evD))
                else:
                    w = wqr if ui == 6 else wqr2
                    for k in range(LC):
                        P.op("pe", "matmul", bk[:, :], w[:, k, :], cqb[b][:, k, :],
                             start=(k == 0), stop=(k == LC - 1), inc=s_pj if k == LC - 1 else None)
                    P.wait("dve", s_pj, u + 1)
                    if ui == 6:
                        P.op("dve", "tensor_tensor", out=tmp1[:, :], in0=bk[:, :], in1=csb[b][:, 0, :], op=ALU.mult,
                             inc=s_evD)
                        n_evD += 1
                    else:
                        P.op("dve", "tensor_tensor", out=tmp2[:, :], in0=bk[:, :], in1=csb[b][:, 1, :], op=ALU.mult,
                             inc=s_evD)
                        n_evD += 1
                        P.wait("dve", s_evD, n_evD)
                        P.op("dve", "tensor_tensor", out=QrT[:, tsl], in0=tmp1[:, :], in1=tmp2[:, :], op=ALU.add,
                             inc=s_evD)
                        n_evD += 1
                    units.append((s_evD, n_evD if ui == 7 else n_evD))
            evd_at[tt] = n_evD
        P.wait("pe", s_evA, n_evA)
        P.wait("pe", s_evD, n_evD)
        steps = []
        for hd in range(2):
            for qt in range(NT):
                nk = 4 * qt + 4
                for kt in range(nk):
                    steps.append((hd, qt, kt, nk))
        n_acc = 0

        def emit_qk(g):
            hd, qt, kt, nk = steps[g]
            i = kt - 4 * qt
            c0 = 128 * i if i > 0 else 0
            sb_ = bank[g % 2]
            q0 = qt * 512
            ksl = slice(kt * 128, (kt + 1) * 128)
            P.wait("pe", s_ex, g - 1)
            P.op("pe", "matmul", sb_[:, c0:512], KnT[:, hd, ksl], QnT[:, hd, q0 + c0:q0 + 512], start=True, stop=False)
            P.op("pe", "matmul", sb_[:, c0:512], krT[hd * 64:(hd + 1) * 64, ksl],
                 QrT[hd * 64:(hd + 1) * 64, q0 + c0:q0 + 512], start=False, stop=(i < 0),
                 inc=s_qk if i < 0 else None)
            if i >= 0:
                P.op("pe", "matmul", sb_[:, c0:c0 + 128], ident, tri, start=False, stop=True, inc=s_qk)
            P.wait("act", s_qk, g + 1)
            P.wait("act", s_pv, g - 2)
            P.op("act", "activation", out=ptb[g % 3][:, c0:512], in_=sb_[:, c0:512], func=AF.Exp, scale=scale, inc=s_ex)

        def emit_pv(g):
            nonlocal n_acc
            hd, qt, kt, nk = steps[g]
            i = kt - 4 * qt
            c0 = 128 * i if i > 0 else 0
            a = n_acc
            ob, lb = bank[2 + 2 * (a % 2)], bank[3 + 2 * (a % 2)]
            P.wait("pe", s_ex, g + 1)
            if kt == 0:
                P.wait("pe", s_fin, a - 1)
            P.op("pe", "matmul", ob[:, c0:512], Vs[:, kt, hd * 128:(hd + 1) * 128], ptb[g % 3][:, c0:512],
                 start=(kt == 0), stop=(kt == nk - 1))
            P.op("pe", "matmul", lb[:, c0:512], onesb[:, :], ptb[g % 3][:, c0:512],
                 start=(kt == 0), stop=(kt == nk - 1), inc=s_pv)
            if kt == nk - 1:
                q0 = qt * 512
                P.wait("dve", s_pv, g + 1)
                P.wait("dve", s_out, 16 * (a - 1))
                P.op("dve", "reciprocal", rl[:, :], lb[:, :], inc=s_acc)
                P.wait("dve", s_acc, a + 1)
                P.op("dve", "tensor_tensor", out=oout[a % 2][:, :], in0=ob[:, :], in1=rl[:, :], op=ALU.mult, inc=s_fin)
                P.wait("sp", s_fin, a + 1)
                P.dma("sp", o_d[hd * 128:(hd + 1) * 128, q0:q0 + 512], oout[a % 2][:, :], s_out)
                n_acc += 1

        G = len(steps)
        emit_qk(0)
        for g in range(G):
            if g + 1 < G:
                emit_qk(g + 1)
            emit_pv(g)
        P.wait("sp", s_out, s_out.n)
        P.run()
    return nc


def rope_tables_ext(pos):
    inv = (1.0 / (10000.0 ** (np.arange(0, 64, 2, dtype=np.float32) / np.float32(64)))).astype(np.float32)
    ang = pos.astype(np.float32)[None, :] * inv[:, None]
    cos, sin = np.cos(ang).astype(np.float32), np.sin(ang).astype(np.float32)
    return np.concatenate([cos, cos], 0), np.concatenate([-sin, sin], 0)


def mla_consts():
    k = np.arange(128)[:, None]
    j = np.arange(128)[None, :]
    tri = np.where(k <= j, 0.0, -30000.0).astype(np.float32)
    return np.stack([np.eye(128, dtype=np.float32), tri]).astype(ml_dtypes.bfloat16)


def til(w, ncols=128):
    din, dout = w.shape
    return np.ascontiguousarray(w.reshape(din // 128, 128, dout // ncols, ncols).transpose(2, 1, 0, 3))


def build_token_program(wo=False, ffn2=False, latent=False, ffn1=False, mix=None, final=False):
    from contextlib import ExitStack
    nc = bass.Bass("TRN2", target_bir_lowering=False)

    def din(name, shape, dt=F32):
        return nc.dram_tensor(name, list(shape), dt, kind="ExternalInput").ap()

    def dout(name, shape, dt=F32):
        return nc.dram_tensor(name, list(shape), dt, kind="ExternalOutput").ap()

    x_d = din("xT_in", [D, T])
    y_d = dout("xT_out", [D, T])
    gnames = []
    if ffn2:
        gnames.append("g_ffn2")
    if latent:
        gnames += ["g_kv"]
    if ffn1:
        gnames.append("g_ffn1")
    if mix:
        gnames.append("g_mix")
    if final:
        gnames.append("g_final")
    g_d = {n: din(n, [128, KC]) for n in gnames}
    if wo:
        o_d = din("oT_in", [D, T], BF16)
        wo_d = din("wo", [KC, 128, KC, 128])
    if ffn2:
        w2 = (din("f2_wg", [FC, 128, KC, 128]), din("f2_wu", [FC, 128, KC, 128]), din("f2_wd", [2, KC, 128, FH, 128]))
    if ffn1:
        w1 = (din("f1_wg", [FC, 128, KC, 128]), din("f1_wu", [FC, 128, KC, 128]), din("f1_wd", [2, KC, 128, FH, 128]))
    if latent:
        wdkv_d = din("wdkv", [LC, 128, KC, 128])
        gckv_d = din("g_ckv", [128, LC])
        wkr_d = din("wkr", [2, 128, KC, 64])
        cs_d = din("cs_tok", [2, 64, T])
        ckv_o = dout("ckvT_out", [512, T], BF16)
        kr_o = dout("krT_out", [64, T], BF16)
    if mix == "cq":
        wdq_d = din("wdq", [LC, 128, KC, 128])
        gcq_d = din("g_cq", [128, LC])
        cq_o = dout("cqT_out", [512, T], BF16)
    if mix == "h":
        h_o = dout("hT_out", [D, T], BF16)

    with ExitStack() as st:
        C = Ctx(nc, st)
        P = Prog(nc)
        tp = TokenPhase(nc, P, C)
        tp.epsb = C.sbuf([128, 1], F32, "epsb")
        P.op("dve", "memset", tp.epsb[:, :], EPS, inc=tp.s_init)
        P.wait("act", tp.s_init, 2)
        g_sb = {n: C.sbuf([128, KC], F32, n + "_sb") for n in gnames}
        tp.load_x(x_d)
        for n in gnames:
            tp.load(g_sb[n][:, :], g_d[n][:, :])
        if latent or mix == "cq":
            lat = C.sbuf([128, LC, T], F32, "lat")
            latb = tp.aT[:, 0:LC, :]
            s_misc = C.sem("s_misc")
        if latent:
            gckv = C.sbuf([128, LC], F32, "gckv_sb")
            cs_sb = C.sbuf([64, 2, T], F32, "cs_sb")
            krb = tp.aT[0:64, LC + 1, :]
            tp.load(gckv[:, :], gckv_d[:, :])
            tp.load(cs_sb[:, :, :], cs_d.rearrange("a p t -> p a t"))
        if mix == "cq":
            gcq = C.sbuf([128, LC], F32, "gcq_sb")
            tp.load(gcq[:, :], gcq_d[:, :])
        n_lat_out = 0
        if wo:
            tp.load(tp.aT[:, 0:KC, :], o_d.rearrange("(k p) t -> p k t", p=128))
            for e in ("dve",):
                P.wait(e, tp.s_x, tp.s_x.n)
            tp.add_proj(tp.aT, KC, [wo_d[dc] for dc in range(KC)])
        if ffn2:
            tp.norm(g_sb["g_ffn2"], tp.hT)
            tp.ffn(*w2)
        if latent:
            tp.norm(g_sb["g_kv"], tp.hT)
            tp.proj_to(tp.hT, KC, [wdkv_d[oc] for oc in range(LC)], lat)
            tp.norm(gckv, latb, src=lat, nk=LC)
            P.wait("sp", tp.s_h, tp.s_h.n)
            P.dma("sp", ckv_o.rearrange("(k p) t -> p k t", p=128), latb, tp.s_out)
            n_lat_out = tp.s_out.n
            tA = lat[0:64, 0, :]
            tB = lat[0:64, 1, :]
            P.wait("dve", tp.s_h, tp.s_h.n)

            def kr_evac(v, th, tsl, bk, inc):
                P.op("dve", "tensor_tensor", out=(tA if v == 0 else tB)[:, tsl], in0=bk[0:64, :],
                     in1=cs_sb[:, v, tsl], op=ALU.mult, inc=inc)
            tp.proj_units(tp.hT, KC, [wkr_d[0], wkr_d[1]], 64, kr_evac)
            P.wait("dve", tp.s_y, tp.s_y.n)
            P.op("dve", "tensor_tensor", out=krb, in0=tA, in1=tB, op=ALU.add, inc=s_misc)
            P.wait("sp", s_misc, s_misc.n)
            P.dma("sp", kr_o, krb, tp.s_out)
        if ffn1:
            tp.norm(g_sb["g_ffn1"], tp.hT)
            tp.ffn(*w1)
        if mix:
            tp.norm(g_sb["g_mix"], tp.hT)
        if mix == "h":
            P.wait("sp", tp.s_h, tp.s_h.n)
            hv = h_o.rearrange("(k p) t -> p k t", p=128)
            for k in range(0, KC, 4):
                P.dma("sp", hv[:, k:k + 4, :], tp.hT[:, k:k + 4, :], tp.s_out)
        if mix == "cq":
            if latent:
                P.wait("dve", tp.s_out, n_lat_out)
                P.wait("dve", s_misc, s_misc.n)
            tp.proj_to(tp.hT, KC, [wdq_d[oc] for oc in range(LC)], lat)
            tp.norm(gcq, latb, src=lat, nk=LC)
            P.wait("sp", tp.s_h, tp.s_h.n)
            P.dma("sp", cq_o.rearrange("(k p) t -> p k t", p=128), latb, tp.s_out)
        if final:
            tp.norm(g_sb["g_final"], tp.xT)
            P.wait("sp", tp.s_h, tp.s_h.n)
        tp.store_x(y_d)
        tp.finish()
        P.run()
    return nc


_PROGS = {}


def _prog(key, fn, **kw):
    if key not in _PROGS:
        _PROGS[key] = fn(**kw)
    return _PROGS[key]


def _launch(nc, maps):
    res = run_bass_kernel_spmd(nc, maps, core_ids=list(range(NCORES)))
    return res.results


def _f32(a):
    return np.ascontiguousarray(np.asarray(a, dtype=np.float32))


def kernel(x, ffn_norm1, ffn1_wg, ffn1_wu, ffn1_wd, mix_norm, ffn_norm2, ffn2_wg, ffn2_wu, ffn2_wd,
           a_wqkv, a_wo, kv_norm, b_wdkv, b_ckv_norm, b_wkr, b_wuk, b_wuv,
           b_wdq, b_cq_norm, b_wuq, b_wo, final_norm):
    x = _f32(x)
    xT = np.ascontiguousarray(x[0].T)
    xs = [np.ascontiguousarray(xT[:, c * T:(c + 1) * T]) for c in range(NCORES)]
    swap = (np.arange(64) + 32) % 64

    def ffn_w(l, which):
        wg, wu, wd = (ffn1_wg, ffn1_wu, ffn1_wd) if which == 1 else (ffn2_wg, ffn2_wu, ffn2_wd)
        p = "f1_" if which == 1 else "f2_"
        return {p + "wg": tile_wgu(_f32(wg[l])), p + "wu": tile_wgu(_f32(wu[l])), p + "wd": tile_wd(_f32(wd[l]))}

    def gc(g):
        return gcol_of(_f32(g))

    def gath_tok(res, name):
        return np.ascontiguousarray(np.concatenate([np.asarray(r[name]) for r in res], axis=1))

    def tok_shards(full):
        return [np.ascontiguousarray(full[:, c * T:(c + 1) * T]) for c in range(NCORES)]

    com = dict(ffn_w(0, 1), g_ffn1=gc(ffn_norm1[0]), g_mix=gc(mix_norm[0]))
    nc = _prog("T_first", build_token_program, ffn1=True, mix="h")
    res = _launch(nc, [dict(com, xT_in=xs[c]) for c in range(NCORES)])
    xs = [np.asarray(r["xT_out"]) for r in res]
    hT_all = gath_tok(res, "hT_out")
    oT = None
    for l in range(DEPTH):
        if l < 2:
            wqkv = _f32(a_wqkv[l])

            def hw(w, c):
                return np.ascontiguousarray(w[:, c * 256:(c + 1) * 256].reshape(KC, 128, 256).transpose(1, 0, 2))
            nc = _prog("A", build_attn_a_program)
            hT_t = tile_hT(hT_all)
            maps = [{"hT_all": hT_t, "wq": hw(wqkv[:, 0:2048], c), "wk": hw(wqkv[:, 2048:4096], c),
                     "wv": hw(wqkv[:, 4096:6144], c), "emask": alibi_masks([2 * c, 2 * c + 1])}
                    for c in range(NCORES)]
            res = _launch(nc, maps)
            wo_l = _f32(a_wo[l])
        else:
            jb = l - 2
            wuq, wuk, wuv = _f32(b_wuq[jb]), _f32(b_wuk), _f32(b_wuv)
            ce, se = rope_tables_ext(np.arange(S))
            cs = np.stack([np.concatenate([ce, ce], 0), np.concatenate([se, se], 0)])

            def t4(w):
                return np.ascontiguousarray(w.reshape(LC, 128, -1).transpose(1, 0, 2))
            nc = _prog("M", build_mla_program)
            maps = []
            for c in range(NCORES):
                hs = [2 * c, 2 * c + 1]
                maps.append({"cqT_all": cqT_all, "ckvT_all": ckvT_all, "krT_all": krT_all,
                             "wuq_n": t4(np.concatenate([wuq[:, h, :128] for h in hs], 1)),
                             "wuq_r": t4(np.concatenate([wuq[:, h, 128:] for h in hs], 1)),
                             "wuq_r2": t4(np.concatenate([wuq[:, h, 128:][:, swap] for h in hs], 1)),
                             "wuk": t4(np.concatenate([wuk[:, h] for h in hs], 1)),
                             "wuv": t4(np.concatenate([wuv[:, h] for h in hs], 1)),
                             "cs": cs, "cst": mla_consts()})
            res = _launch(nc, maps)
            wo_l = _f32(b_wo[jb])
        oT_full = np.ascontiguousarray(np.concatenate([np.asarray(r["oT"]) for r in res], axis=0))
        oTs = tok_shards(oT_full)
        com = dict(ffn_w(l, 2), wo=til(wo_l), g_ffn2=gc(ffn_norm2[l]))
        if l == DEPTH - 1:
            nc = _prog("T_last", build_token_program, wo=True, ffn2=True, final=True)
            com["g_final"] = gc(final_norm)
            res = _launch(nc, [dict(com, xT_in=xs[c], oT_in=oTs[c]) for c in range(NCORES)])
            outT = np.concatenate([np.asarray(r["xT_out"]) for r in res], axis=1)
            return np.ascontiguousarray(outT.T)[None].astype(np.float32)
        com.update(ffn_w(l + 1, 1))
        com["g_ffn1"] = gc(ffn_norm1[l + 1])
        com["g_mix"] = gc(mix_norm[l + 1])
        per_core = [dict(xT_in=xs[c], oT_in=oTs[c]) for c in range(NCORES)]
        if l + 1 < 2:
            nc = _prog("T_mid_h", build_token_program, wo=True, ffn2=True, ffn1=True, mix="h")
        else:
            jb = l + 1 - 2
            com["wdq"] = til(_f32(b_wdq[jb]))
            com["g_cq"] = np.ascontiguousarray(_f32(b_cq_norm[jb]).reshape(LC, 128).T)
            if l + 1 == 2:
                nc = _prog("T_mid_lat", build_token_program, wo=True, ffn2=True, latent=True, ffn1=True, mix="cq")
                com["g_kv"] = gc(kv_norm)
                com["wdkv"] = til(_f32(b_wdkv))
                com["g_ckv"] = np.ascontiguousarray(_f32(b_ckv_norm).reshape(LC, 128).T)
                wkr = _f32(b_wkr)
                com["wkr"] = np.stack([til(wkr, 64)[0], til(np.ascontiguousarray(wkr[:, swap]), 64)[0]])
                for c in range(NCORES):
                    ce, se = rope_tables_ext(np.arange(c * T, (c + 1) * T))
                    per_core[c]["cs_tok"] = np.stack([ce, se])
            else:
                nc = _prog("T_mid_cq", build_token_program, wo=True, ffn2=True, ffn1=True, mix="cq")
        res = _launch(nc, [dict(com, **per_core[c]) for c in range(NCORES)])
        xs = [np.asarray(r["xT_out"]) for r in res]
        if l + 1 < 2:
            hT_all = gath_tok(res, "hT_out")
        else:
            cqT_all = gath_tok(res, "cqT_out")
            if l + 1 == 2:
                ckvT_all = gath_tok(res, "ckvT_out")
                krT_all = gath_tok(res, "krT_out")
```

```python
import numpy as np
import ml_dtypes
import concourse.bass as bass
import concourse.mybir as mybir
from concourse.bass_utils import run_bass_kernel_spmd

F32 = mybir.dt.float32
BF16 = mybir.dt.bfloat16
AF = mybir.ActivationFunctionType
ALU = mybir.AluOpType

NCORES = 8
D = 2048
S = 8192
T = S // NCORES
KC = D // 128
DFF = 5632
FC = DFF // 128
FH = FC // 2
EPS = 1e-6
DEPTH = 4


class Sem:
    def __init__(self, h):
        self.h = h
        self.n = 0


class Prog:
    def __init__(self, nc):
        self.nc = nc
        self.q = {"pe": [], "act": [], "dve": [], "pool": [], "sp": []}

    def op(self, eng, name, *args, inc=None, incv=None, **kw):
        if inc is not None:
            v = incv if incv is not None else 1
            inc.n += v
            h = inc.h

            def f(e, name=name, args=args, kw=kw, h=h, v=v):
                getattr(e, name)(*args, **kw).then_inc(h, v)
        else:
            def f(e, name=name, args=args, kw=kw):
                getattr(e, name)(*args, **kw)
        self.q[eng].append(f)

    def dma(self, eng, out, in_, sem):
        self.op(eng, "dma_start", out=out, in_=in_, inc=sem, incv=16)

    def wait(self, eng, sem, val):
        if val <= 0:
            return
        h = sem.h
        self.q[eng].append(lambda e, h=h, val=val: e.wait_ge(h, val))

    def run(self):
        with self.nc.Block() as block:
            block.tensor(lambda e: [f(e) for f in self.q["pe"]])
            block.scalar(lambda e: [f(e) for f in self.q["act"]])
            block.vector(lambda e: [f(e) for f in self.q["dve"]])
            block.gpsimd(lambda e: [f(e) for f in self.q["pool"]])
            block.sync(lambda e: [f(e) for f in self.q["sp"]])


class Ctx:
    def __init__(self, nc, stack):
        self.nc = nc
        self.stack = stack
        self.k = 0

    def sbuf(self, shape, dt, name=None):
        self.k += 1
        return self.stack.enter_context(self.nc.sbuf_tensor(name or f"sb{self.k}", list(shape), dt))

    def psum(self, shape, dt=F32, name=None):
        self.k += 1
        return self.stack.enter_context(self.nc.psum_tensor(name or f"ps{self.k}", list(shape), dt))

    def sem(self, name=None):
        self.k += 1
        return Sem(self.stack.enter_context(self.nc.semaphore(name or f"sem{self.k}")))


class TokenPhase:
    def __init__(self, nc, P, C):
        self.nc, self.P, self.C = nc, P, C
        self.xT = C.sbuf([128, KC, T], F32, "xT")
        self.hT = C.sbuf([128, KC, T], BF16, "hT")
        self.aT = C.sbuf([128, FH, T], BF16, "aT")
        self.scr = C.sbuf([128, 4, 512], F32, "scr")
        self.ones = C.sbuf([128, 128], F32, "ones")
        self.rstd = C.sbuf([128, 512], F32, "rstd")
        self.sg = [C.sbuf([128, 512], F32, f"sg{i}") for i in range(2)]
        self.wg = [C.sbuf([128, KC, 128], BF16, f"wg{i}") for i in range(2)]
        self.wu = [C.sbuf([128, KC, 128], BF16, f"wu{i}") for i in range(2)]
        self.wd = [C.sbuf([128, FH, 128], BF16, f"wd{i}") for i in range(2)]
        self.bank = [C.psum([128, 512], F32, f"bk{i}") for i in range(8)]
        self.s_init = C.sem("s_init")
        self.s_x = C.sem("s_x")
        self.s_sq = C.sem("s_sq")
        self.s_st = C.sem("s_st")
        self.s_sqrt = C.sem("s_sqrt")
        self.s_rs = C.sem("s_rs")
        self.s_h = C.sem("s_h")
        self.s_wgu = C.sem("s_wgu")
        self.s_gu = C.sem("s_gu")
        self.s_sl = C.sem("s_sl")
        self.s_a = C.sem("s_a")
        self.s_wd = C.sem("s_wd")
        self.s_dn = C.sem("s_dn")
        self.s_y = C.sem("s_y")
        self.s_out = C.sem("s_out")
        self.n_wgu = self.n_wd = self.n_gu = self.n_dn = self.n_hn = self.n_sq = 0
        P.op("dve", "memset", self.ones[:, :], 1.0, inc=self.s_init)
        P.wait("pe", self.s_init, 1)

    def load(self, dst, src):
        self.P.dma("sp", dst, src, self.s_x)

    def load_x(self, x_dram):
        xv = x_dram.rearrange("(k p) t -> p k t", p=128)
        for k in range(0, KC, 4):
            self.load(self.xT[:, k:k + 4, :], xv[:, k:k + 4, :])

    def store_x(self, y_dram):
        P = self.P
        yv = y_dram.rearrange("(k p) t -> p k t", p=128)
        P.wait("sp", self.s_y, self.s_y.n)
        for k in range(0, KC, 4):
            P.dma("sp", yv[:, k:k + 4, :], self.xT[:, k:k + 4, :], self.s_out)

    def finish(self):
        self.P.wait("sp", self.s_out, self.s_out.n)

    def norm(self, gcol, out_tile, src=None, nk=KC):
        P = self.P
        src = self.xT if src is None else src
        for e in ("act", "dve"):
            P.wait(e, self.s_x, self.s_x.n)
            P.wait(e, self.s_y, self.s_y.n)
        for th in range(2):
            n = self.n_hn
            self.n_hn += 1
            tsl = slice(th * 512, (th + 1) * 512)
            P.wait("pe", self.s_sqrt, n)
            for k in range(nk):
                q = self.n_sq
                self.n_sq += 1
                P.wait("act", self.s_st, q - 3)
                P.op("act", "activation", out=self.scr[:, q % 4, :], in_=src[:, k, tsl],
                     func=AF.Square, inc=self.s_sq)
                P.wait("pe", self.s_sq, q + 1)
                P.op("pe", "matmul", self.bank[7][:, :], self.ones[:, :], self.scr[:, q % 4, :],
                     start=(k == 0), stop=(k == nk - 1), inc=self.s_st)
            P.wait("act", self.s_st, self.n_sq)
            P.wait("act", self.s_h, self.s_h.n)
            P.op("act", "activation", out=self.rstd[:, :], in_=self.bank[7][:, :], func=AF.Sqrt,
                 scale=1.0 / (128 * nk), bias=self.epsb[:, 0:1], inc=self.s_sqrt)
            P.wait("dve", self.s_sqrt, n + 1)
            P.op("dve", "reciprocal", self.rstd[:, :], self.rstd[:, :], inc=self.s_rs)
            P.wait("dve", self.s_rs, n + 1)
            for k in range(nk):
                P.op("dve", "scalar_tensor_tensor", out=out_tile[:, k, tsl], in0=src[:, k, tsl],
                     scalar=gcol[:, k:k + 1], op0=ALU.mult, in1=self.rstd[:, :], op1=ALU.mult,
                     inc=self.s_h)

    def ffn(self, wg_d, wu_d, wd_d):
        P = self.P
        h_ready = self.s_h.n
        P.wait("dve", self.s_out, self.s_out.n)
        for fh in range(2):
            gu0 = self.n_gu
            P.wait("pe", self.s_y, self.s_y.n)
            P.wait("pe", self.s_h, h_ready)
            for j in range(FH):
                fc = fh * FH + j
                c = self.n_wgu
                self.n_wgu += 1
                b = c % 2
                P.wait("pool", self.s_gu, 2 * (c - 1))
                P.dma("pool", self.wg[b][:, :, :], wg_d[fc], self.s_wgu)
                P.dma("pool", self.wu[b][:, :, :], wu_d[fc], self.s_wgu)
                P.wait("pe", self.s_wgu, 32 * (c + 1))
                for th in range(2):
                    n = self.n_gu
                    self.n_gu += 1
                    pr = n % 4
                    gps, ups = self.bank[2 * pr], self.bank[2 * pr + 1]
                    tsl = slice(th * 512, (th + 1) * 512)
                    if n - gu0 >= 4:
                        P.wait("pe", self.s_a, n - 3)
                    for k in range(KC):
                        P.op("pe", "matmul", gps[:, :], self.wg[b][:, k, :], self.hT[:, k, tsl],
                             start=(k == 0), stop=(k == KC - 1))
                    for k in range(KC):
                        last = k == KC - 1
                        P.op("pe", "matmul", ups[:, :], self.wu[b][:, k, :], self.hT[:, k, tsl],
                             start=(k == 0), stop=last, inc=self.s_gu if last else None)
                    P.wait("act", self.s_gu, n + 1)
                    P.wait("act", self.s_a, n - 1)
                    P.op("act", "activation", out=self.sg[n % 2][:, :], in_=gps[:, :], func=AF.Silu,
                         inc=self.s_sl)
                    P.wait("dve", self.s_sl, n + 1)
                    P.op("dve", "tensor_tensor", out=self.aT[:, j, tsl], in0=ups[:, :],
                         in1=self.sg[n % 2][:, :], op=ALU.mult, inc=self.s_a)
            P.wait("pe", self.s_a, self.n_gu)
            self.proj_units(self.aT, FH, [wd_d[fh, dc] for dc in range(KC)], 128,
                            lambda dc, th, tsl, bk, inc: P.op(
                                "dve", "scalar_tensor_tensor", out=self.xT[:, dc, tsl], in0=bk[:, :], scalar=0.5,
                                op0=ALU.mult, in1=self.xT[:, dc, tsl], op1=ALU.add, inc=inc))

    def proj_units(self, src, nk, w_list, ncols, evac):
        P = self.P
        P.wait("pe", self.s_sqrt, self.s_sqrt.n)
        P.wait("pe", self.s_h, self.s_h.n)
        P.wait("pe", self.s_x, self.s_x.n)
        for oc, w_ap in enumerate(w_list):
            c = self.n_wd
            self.n_wd += 1
            b = c % 2
            P.wait("pool", self.s_dn, 2 * (c - 1))
            P.dma("pool", self.wd[b][:, 0:nk, 0:ncols], w_ap, self.s_wd)
            P.wait("pe", self.s_wd, 16 * (c + 1))
            for th in range(2):
                m = self.n_dn
                self.n_dn += 1
                bk = self.bank[m % 8]
                tsl = slice(th * 512, (th + 1) * 512)
                P.wait("pe", self.s_y, m - 7)
                for j in range(nk):
                    last = j == nk - 1
                    P.op("pe", "matmul", bk[0:ncols, :], self.wd[b][:, j, 0:ncols], src[:, j, tsl],
                         start=(j == 0), stop=last, inc=(self.s_dn if last else None))
                P.wait("dve", self.s_dn, m + 1)
                evac(oc, th, tsl, bk, self.s_y)

    def add_proj(self, src, nk, w_list):
        P = self.P
        self.proj_units(src, nk, w_list, 128,
                        lambda oc, th, tsl, bk, inc: P.op(
                            "dve", "tensor_tensor", out=self.xT[:, oc, tsl], in0=bk[:, :],
                            in1=self.xT[:, oc, tsl], op=ALU.add, inc=inc))

    def proj_to(self, src, nk, w_list, dst):
        P = self.P
        self.proj_units(src, nk, w_list, 128,
                        lambda oc, th, tsl, bk, inc: P.op(
                            "dve", "tensor_copy", out=dst[:, oc, tsl], in_=bk[:, :], inc=inc))


def tile_wgu(w):
    return np.ascontiguousarray(w.reshape(KC, 128, FC, 128).transpose(2, 1, 0, 3))


def tile_wd(w):
    return np.ascontiguousarray(w.reshape(2, FH, 128, KC, 128).transpose(0, 3, 2, 1, 4))


def gcol_of(g):
    return np.ascontiguousarray(g.reshape(KC, 128).T)


def build_ffn_program():
    from contextlib import ExitStack
    nc = bass.Bass("TRN2", target_bir_lowering=False)
    x_d = nc.dram_tensor("xT_in", [D, T], F32, kind="ExternalInput").ap()
    g_d = nc.dram_tensor("gcol", [128, KC], F32, kind="ExternalInput").ap()
    wg_d = nc.dram_tensor("wg", [FC, 128, KC, 128], F32, kind="ExternalInput").ap()
    wu_d = nc.dram_tensor("wu", [FC, 128, KC, 128], F32, kind="ExternalInput").ap()
    wd_d = nc.dram_tensor("wd", [2, KC, 128, FH, 128], F32, kind="ExternalInput").ap()
    y_d = nc.dram_tensor("xT_out", [D, T], F32, kind="ExternalOutput").ap()
    with ExitStack() as st:
        C = Ctx(nc, st)
        P = Prog(nc)
        tp = TokenPhase(nc, P, C)
        gcol = C.sbuf([128, KC], F32, "gcol_sb")
        tp.epsb = C.sbuf([128, 1], F32, "epsb")
        P.op("dve", "memset", tp.epsb[:, :], EPS, inc=tp.s_init)
        P.wait("act", tp.s_init, 2)
        tp.load_x(x_d)
        tp.load(gcol[:, :], g_d[:, :])
        tp.norm(gcol, tp.hT)
        tp.ffn(wg_d, wu_d, wd_d)
        tp.store_x(y_d)
        tp.finish()
        P.run()
    return nc


DILS = (1, 4, 16)


def ss(start, n, step):
    return slice(start, start + (n - 1) * step + 1, step)

BLK = 2048
NB = S // BLK


def alibi_masks(heads):
    k = np.arange(128)[:, None].astype(np.float64)
    j = np.arange(128)[None, :].astype(np.float64)
    out = np.zeros((3, len(heads), 128, 256), np.float32)
    for di, d in enumerate(DILS):
        for hi, h in enumerate(heads):
            slope = 2.0 ** (-8.0 * (h + 1) / 16)
            lo = np.where(j >= k, np.exp(-slope * d * np.maximum(j - k, 0.0)), 0.0)
            hi_ = np.where(j <= k, np.exp(-slope * d * np.maximum(128 + j - k, 0.0)), 0.0)
            out[di, hi, :, 0:128] = hi_
            out[di, hi, :, 128:256] = lo
    return out


def build_attn_a_program():
    from contextlib import ExitStack
    nc = bass.Bass("TRN2", target_bir_lowering=False)
    NT = S // 512
    h_d = nc.dram_tensor("hT_all", [NT, 128, KC, 512], BF16, kind="ExternalInput").ap()
    wq_d = nc.dram_tensor("wq", [128, KC, 256], F32, kind="ExternalInput").ap()
    wk_d = nc.dram_tensor("wk", [128, KC, 256], F32, kind="ExternalInput").ap()
    wv_d = nc.dram_tensor("wv", [128, KC, 256], F32, kind="ExternalInput").ap()
    em_d = nc.dram_tensor("emask", [3, 2, 128, 256], F32, kind="ExternalInput").ap()
    o_d = nc.dram_tensor("oT", [256, S], BF16, kind="ExternalOutput").ap()
    vd = nc.dram_tensor("v_scratch", [S, 256], BF16).ap()
    scale = 128.0 ** -0.5
    with ExitStack() as st:
        C = Ctx(nc, st)
        P = Prog(nc)
        wq = C.sbuf([128, KC, 256], BF16, "wq_sb")
        wk = C.sbuf([128, KC, 256], BF16, "wk_sb")
        wv = C.sbuf([128, KC, 256], BF16, "wv_sb")
        em = C.sbuf([128, 3, 2, 256], F32, "em_sb")
        QT = C.sbuf([128, 2, S], BF16, "QT")
        KT = C.sbuf([128, 2, S], BF16, "KT")
        hbuf = [C.sbuf([128, KC, 512], BF16, f"hbuf{i}") for i in range(2)]
        vst = [C.sbuf([128, 2, 256], BF16, f"vst{i}") for i in range(2)]
        vn = [C.sbuf([128, 16, 256], BF16, f"vn{i}") for i in range(2)]
        v4 = [C.sbuf([128, 4, 4, 256], BF16, f"v4{i}") for i in range(2)]
        v16 = [C.sbuf([128, 16, 256], BF16, f"v16{i}") for i in range(2)]
        vn.append(hbuf[0][:, 0:8, :].rearrange("p a (b c) -> p (a b) c", c=256))
        v4.append(hbuf[0][:, 8:16, :].rearrange("p (r a) (b c) -> p r (a b) c", r=4, c=256))
        v16.append(hbuf[1][:, 0:8, :].rearrange("p a (b c) -> p (a b) c", c=256))
        NV = 3
        acc = C.sbuf([128, 2, BLK], F32, "acc")
        oacc, lacc = acc[:, 0, :], acc[:, 1, :]
        NP, NOL, LOOK = 4, 3, 2
        pbuf = [C.sbuf([128, 512], F32, f"pbuf{i}") for i in range(3)]
        ptb = [C.sbuf([128, 512], BF16, f"ptb{i}") for i in range(3)]
        pbuf.append(hbuf[1][:, 8:10, :].rearrange("p a c -> p (a c)").bitcast(F32))
        ptb.append(hbuf[1][:, 10, :])
        onesb = C.sbuf([128, 128], BF16, "onesb")
        oout = [C.sbuf([128, BLK], BF16, "oout0")]
        bank = [C.psum([128, 512], F32, f"bk{i}") for i in range(8)]
        s_w, s_hb, s_pj, s_evA, s_evD, s_vst, s_vl = (C.sem(n) for n in
                                                      ("s_w", "s_hb", "s_pj", "s_evA", "s_evD", "s_vst", "s_vl"))
        s_init, s_qk, s_ex, s_ptA, s_ptB, s_pv, s_eA, s_eD, s_fin, s_out = (
            C.sem(n) for n in ("s_init", "s_qk", "s_ex", "s_ptA", "s_ptB", "s_pv", "s_eA", "s_eD", "s_fin", "s_out"))
        P.op("dve", "memset", onesb[:, :], 1.0, inc=s_init)
        P.dma("pool", wq[:, :, :], wq_d, s_w)
        P.dma("pool", wk[:, :, :], wk_d, s_w)
        P.dma("pool", wv[:, :, :], wv_d, s_w)
        P.dma("pool", em[:, :, :, :], em_d.rearrange("d h p c -> p d h c"), s_w)
        vdt = vd.rearrange("(n p) c -> p n c", p=128)
        P.wait("pe", s_w, 64)
        P.wait("pe", s_init, 1)
        P.wait("pool", s_w, 64)
        units = []
        n_evA = n_evD = 0
        for tt in range(NT):
            hb = hbuf[tt % 2]
            if tt >= 2:
                P.wait("sp", s_pj, 6 * (tt - 1))
            P.dma("sp", hb[:, 0:KC // 2, :], h_d[tt, :, 0:KC // 2, :], s_hb)
            P.dma("sp", hb[:, KC // 2:KC, :], h_d[tt, :, KC // 2:KC, :], s_hb)
            P.wait("pe", s_hb, 32 * (tt + 1))
            for ui in range(6):
                u = len(units)
                bk = bank[u % 8]
                if u >= 8:
                    P.wait("pe", units[u - 8][0], units[u - 8][1])
                if ui < 4:
                    w = wq if ui < 2 else wk
                    hd = ui % 2
                    for k in range(KC):
                        P.op("pe", "matmul", bk[:, :], w[:, k, hd * 128:(hd + 1) * 128], hb[:, k, :],
                             start=(k == 0), stop=(k == KC - 1), inc=s_pj if k == KC - 1 else None)
                    dst = (QT if ui < 2 else KT)[:, hd, tt * 512:(tt + 1) * 512]
                    P.wait("act", s_pj, u + 1)
                    P.op("act", "activation", out=dst, in_=bk[:, :], func=AF.Copy, inc=s_evA)
                    n_evA += 1
                    units.append((s_evA, n_evA))
                else:
                    sp_ = ui - 4
                    for si in range(2):
                        sub = sp_ * 2 + si
                        for k in range(KC):
                            last = (k == KC - 1) and si == 1
                            P.op("pe", "matmul", bk[:, si * 256:(si + 1) * 256], hb[:, k, sub * 128:(sub + 1) * 128],
                                 wv[:, k, :], start=(k == 0), stop=(k == KC - 1), inc=s_pj if last else None)
                    vb = vst[n_evD % 2]
                    P.wait("dve", s_pj, u + 1)
                    P.wait("dve", s_vst, 16 * (n_evD - 1))
                    P.op("dve", "tensor_copy", out=vb[:, :, :], in_=bk[:, :].rearrange("p (s c) -> p s c", c=256),
                         inc=s_evD)
                    n_evD += 1
                    units.append((s_evD, n_evD))
                    P.wait("pool", s_evD, n_evD)
                    n0 = tt * 4 + sp_ * 2
                    P.dma("pool", vdt[:, n0:n0 + 2, :], vb[:, :, :], s_vst)
        v4d = vd.rearrange("(blk i r) c -> i r blk c", i=128, r=4)
        v16d = vd.rearrange("(blk i r) c -> i r blk c", i=128, r=16)
        P.wait("sp", s_vst, s_vst.n)
        P.wait("pe", s_evA, n_evA)
        groups = []
        for B in range(NB):
            b = B % NV
            pb = (B - 1) % NV
            for hd in range(2):
                hs = slice(hd * 128, (hd + 1) * 128)
                tiles = []
                for ml in range(16):
                    prev = vn[b][:, ml - 1, hs] if ml > 0 else (vn[pb][:, 15, hs] if B > 0 else None)
                    tiles.append((0, 1, B * BLK + ml * 128, vn[b][:, ml, hs], prev, ml * 128))
                for r in range(4):
                    for ml in range(4):
                        prev = v4[b][:, r, ml - 1, hs] if ml > 0 else (v4[pb][:, r, 3, hs] if B > 0 else None)
                        tiles.append((1, 4, B * BLK + r + 4 * 128 * ml, v4[b][:, r, ml, hs], prev, r + 4 * 128 * ml))
                for r in range(16):
                    prev = v16[pb][:, r, hs] if B > 0 else None
                    tiles.append((2, 16, B * BLK + r, v16[b][:, r, hs], prev, r))
                for gi in range(0, len(tiles), 2):
                    groups.append(dict(B=B, hd=hd, tiles=tiles[gi:gi + 2], first=(gi == 0),
                                       last=(gi == len(tiles) - 2)))
        G = len(groups)
        ev_done = []
        st_ = dict(n_eA=0, n_eD=0, n_fin=0, last_d1=(None, 0))
        vl_loaded = set()

        def ensure_v(B):
            if B in vl_loaded or B >= NB:
                return
            vl_loaded.add(B)
            b = B % NV
            if B >= 2:
                P.wait("sp", s_pv, blk_end[B - 2])
                P.wait("sp", s_pj, s_pj.n)
            for q4 in range(4):
                P.dma("sp", vn[b][:, q4 * 4:(q4 + 1) * 4, :], vdt[:, B * 16 + q4 * 4:B * 16 + (q4 + 1) * 4, :], s_vl)
                P.dma("sp", v4[b][:, q4, :, :], v4d[:, q4, 4 * B:4 * B + 4, :], s_vl)
                P.dma("sp", v16[b][:, q4 * 4:(q4 + 1) * 4, :], v16d[:, q4 * 4:(q4 + 1) * 4, B, :], s_vl)

        blk_end = {}
        for B in range(NB):
            blk_end[B] = sum(1 for g_ in groups if g_["B"] <= B)

        def emit_qk(g):
            gr = groups[g]
            hd = gr["hd"]
            di, d = gr["tiles"][0][0], gr["tiles"][0][1]
            sb_ = bank[g % NP]
            P.wait("pe", s_ex, g - NP + 1)
            for ti in range(2):
                _, _, t0, vcur, vprev, _ = gr["tiles"][ti]
                qap = QT[:, hd, ss(t0, 128, d)]
                if vprev is not None:
                    P.op("pe", "matmul", sb_[:, ti * 256:ti * 256 + 128], KT[:, hd, ss(t0 - 128 * d, 128, d)], qap,
                         start=True, stop=True)
                P.op("pe", "matmul", sb_[:, ti * 256 + 128:ti * 256 + 256], KT[:, hd, ss(t0, 128, d)], qap,
                     start=True, stop=True, inc=s_qk if ti == 1 else None)
            P.wait("act", s_qk, g + 1)
            P.wait("act", s_ptA, g - NP + 1)
            P.wait("act", s_ptB, g - NP + 1)
            P.op("act", "activation", out=pbuf[g % NP][:, :], in_=sb_[:, :], func=AF.Exp, scale=scale, inc=s_ex)
            for eng, sem_, c0 in (("dve", s_ptA, 0), ("pool", s_ptB, 256)):
                P.wait(eng, s_ex, g + 1)
                P.wait(eng, s_pv, g - NP + 1)
                P.op(eng, "tensor_tensor", out=ptb[g % NP][:, c0:c0 + 256], in0=pbuf[g % NP][:, c0:c0 + 256],
                     in1=em[:, di, hd, :], op=ALU.mult, inc=sem_)

        def emit_pv(g):
            gr = groups[g]
            hd, B = gr["hd"], gr["B"]
            di, d = gr["tiles"][0][0], gr["tiles"][0][1]
            olb = bank[NP + (g % NOL)]
            pt = ptb[g % NP]
            if gr["first"] and hd == 0:
                P.wait("pe", s_vl, 16 * 12 * (B + 1))
            P.wait("pe", s_ptA, g + 1)
            P.wait("pe", s_ptB, g + 1)
            if g >= NOL:
                P.wait("pe", ev_done[g - NOL][0], ev_done[g - NOL][1])
            for kind in (0, 1):
                dstb = olb
                for ti in range(2):
                    _, _, t0, vcur, vprev, _ = gr["tiles"][ti]
                    oc = slice(kind * 256 + ti * 128, kind * 256 + (ti + 1) * 128)
                    last = kind == 1 and ti == 1
                    if vprev is not None:
                        P.op("pe", "matmul", dstb[:, oc], vprev if kind == 0 else onesb[:, :],
                             pt[:, ti * 256:ti * 256 + 128], start=True, stop=False)
                    P.op("pe", "matmul", dstb[:, oc], vcur if kind == 0 else onesb[:, :],
                         pt[:, ti * 256 + 128:ti * 256 + 256], start=(vprev is None), stop=True,
                         inc=s_pv if last else None)
            loc0 = gr["tiles"][0][5]
            if di == 2:
                dst = acc[:, :, :].rearrange("p a (j r) -> p a r j", r=16)[:, :, loc0:loc0 + 2, :]
                srcv = olb[:, :].rearrange("p (a t j) -> p a t j", a=2, t=2)
            else:
                dst = acc[:, :, ss(loc0, 256, d)]
                srcv = olb[:, :].rearrange("p (a c) -> p a c", a=2)
            if di == 0:
                P.wait("act", s_pv, g + 1)
                if gr["first"]:
                    P.wait("act", s_fin, st_["n_fin"])
                P.op("act", "activation", out=dst, in_=srcv, func=AF.Copy, inc=s_eA)
                st_["n_eA"] += 1
                ev_done.append((s_eA, st_["n_eA"]))
                st_["last_d1"] = (s_eA, st_["n_eA"])
            else:
                P.wait("dve", s_pv, g + 1)
                P.wait("dve", st_["last_d1"][0], st_["last_d1"][1])
                P.op("dve", "tensor_tensor", out=dst, in0=srcv, in1=dst, op=ALU.add, inc=s_eD)
                st_["n_eD"] += 1
                ev_done.append((s_eD, st_["n_eD"]))
            if gr["last"]:
                nf = st_["n_fin"]
                ob_ = oout[0]
                P.wait("dve", s_out, 16 * nf)
                P.op("dve", "reciprocal", lacc, lacc)
                P.op("dve", "tensor_tensor", out=ob_[:, :], in0=oacc, in1=lacc, op=ALU.mult, inc=s_fin)
                st_["n_fin"] += 1
                P.wait("sp", s_fin, st_["n_fin"])
                P.dma("sp", o_d[hd * 128:(hd + 1) * 128, B * BLK:(B + 1) * BLK], ob_[:, :], s_out)
                if hd == 1:
                    ensure_v(B + 2)

        ensure_v(0)
        ensure_v(1)
        for g in range(min(LOOK, G)):
            emit_qk(g)
        for g in range(G):
            if g + LOOK < G:
                emit_qk(g + LOOK)
            emit_pv(g)
        P.wait("sp", s_out, s_out.n)
        P.run()
    return nc


def tile_hT(hT_all):
    return np.ascontiguousarray(hT_all.reshape(KC, 128, S // 512, 512).transpose(2, 1, 0, 3))


LC = 4


def build_mla_program():
    from contextlib import ExitStack
    nc = bass.Bass("TRN2", target_bir_lowering=False)
    cq_d = nc.dram_tensor("cqT_all", [512, S], BF16, kind="ExternalInput").ap()
    ckv_d = nc.dram_tensor("ckvT_all", [512, S], BF16, kind="ExternalInput").ap()
    kr_d = nc.dram_tensor("krT_all", [64, S], BF16, kind="ExternalInput").ap()
    wqn_d = nc.dram_tensor("wuq_n", [128, LC, 256], F32, kind="ExternalInput").ap()
    wqr_d = nc.dram_tensor("wuq_r", [128, LC, 256], F32, kind="ExternalInput").ap()
    wqr2_d = nc.dram_tensor("wuq_r2", [128, LC, 256], F32, kind="ExternalInput").ap()
    wuk_d = nc.dram_tensor("wuk", [128, LC, 256], F32, kind="ExternalInput").ap()
    wuv_d = nc.dram_tensor("wuv", [128, LC, 256], F32, kind="ExternalInput").ap()
    cs_d = nc.dram_tensor("cs", [2, 128, S], F32, kind="ExternalInput").ap()
    cst_d = nc.dram_tensor("cst", [2, 128, 128], BF16, kind="ExternalInput").ap()
    o_d = nc.dram_tensor("oT", [256, S], BF16, kind="ExternalOutput").ap()
    scale = 192.0 ** -0.5
    with ExitStack() as st:
        C = Ctx(nc, st)
        P = Prog(nc)
        wqn = C.sbuf([128, LC, 256], BF16, "wqn")
        wqr = C.sbuf([128, LC, 256], BF16, "wqr")
        wqr2 = C.sbuf([128, LC, 256], BF16, "wqr2")
        wuk = C.sbuf([128, LC, 256], BF16, "wuk_sb")
        wuv = C.sbuf([128, LC, 256], BF16, "wuv_sb")
        cst = C.sbuf([128, 2, 128], BF16, "cst_sb")
        KnT = C.sbuf([128, 2, S], BF16, "KnT")
        krT = C.sbuf([128, S], BF16, "krT2")
        Vs = C.sbuf([128, S // 128, 256], BF16, "Vs")
        QnT = C.sbuf([128, 2, S], BF16, "QnT")
        QrT = C.sbuf([128, 2, S], BF16, "QrT")
        ckb = [C.sbuf([128, LC, 512], BF16, f"ckb{i}") for i in range(2)]
        cqb = [C.sbuf([128, LC, 512], BF16, f"cqb{i}") for i in range(2)]
        csb = [C.sbuf([128, 2, 512], F32, f"csb{i}") for i in range(2)]
        tmp1 = C.sbuf([128, 512], F32, "tmp1")
        tmp2 = C.sbuf([128, 512], F32, "tmp2")
        ptb = [C.sbuf([128, 512], BF16, f"ptb{i}") for i in range(4)]
        rl = C.sbuf([128, 512], F32, "rl")
        oout = [C.sbuf([128, 512], BF16, f"oout{i}") for i in range(2)]
        onesb = C.sbuf([128, 128], BF16, "onesb")
        bank = [C.psum([128, 512], F32, f"bk{i}") for i in range(8)]
        s_w, s_in, s_pj, s_evA, s_evD, s_init = (C.sem(n) for n in ("s_w", "s_in", "s_pj", "s_evA", "s_evD", "s_init"))
        s_qk, s_ex, s_pv, s_fin, s_out, s_acc = (C.sem(n) for n in ("s_qk", "s_ex", "s_pv", "s_fin", "s_out", "s_acc"))
        P.op("dve", "memset", onesb[:, :], 1.0, inc=s_init)
        for dst, src in ((wqn, wqn_d), (wqr, wqr_d), (wqr2, wqr2_d), (wuk, wuk_d), (wuv, wuv_d)):
            P.dma("pool", dst[:, :, :], src, s_w)
        P.dma("pool", cst[:, :, :], cst_d.rearrange("a p c -> p a c"), s_w)
        P.dma("pool", krT[0:64, :], kr_d, s_w)
        P.dma("pool", krT[64:128, :], kr_d, s_w)
        NW = 8 * 16
        ident, tri = cst[:, 0, :], cst[:, 1, :]
        ckv_v = ckv_d.rearrange("(k p) t -> p k t", p=128)
        cq_v = cq_d.rearrange("(k p) t -> p k t", p=128)
        cs_v = cs_d.rearrange("a p t -> p a t")
        NT = S // 512
        P.wait("pe", s_w, NW)
        P.wait("pe", s_init, 1)
        units = []
        n_evA = n_evD = 0
        for tt in range(NT):
            b = tt % 2
            tsl = slice(tt * 512, (tt + 1) * 512)
            if tt >= 2:
                P.wait("sp", s_pj, 10 * (tt - 1))
                P.wait("sp", s_evD, evd_at[tt - 2])
            P.dma("sp", ckb[b][:, :, :], ckv_v[:, :, tsl], s_in)
            P.dma("sp", cqb[b][:, :, :], cq_v[:, :, tsl], s_in)
            P.dma("sp", csb[b][:, :, :], cs_v[:, :, tsl], s_in)
            P.wait("pe", s_in, 48 * (tt + 1))
            if tt == 0:
                evd_at = {}
            for ui in range(10):
                u = len(units)
                bk = bank[u % 8]
                if u >= 8:
                    P.wait("pe", units[u - 8][0], units[u - 8][1])
                if ui in (0, 1, 4, 5):
                    hd = ui % 2
                    w, src, dst = (wuk, ckb[b], KnT) if ui < 2 else (wqn, cqb[b], QnT)
                    for k in range(LC):
                        P.op("pe", "matmul", bk[:, :], w[:, k, hd * 128:(hd + 1) * 128], src[:, k, :],
                             start=(k == 0), stop=(k == LC - 1), inc=s_pj if k == LC - 1 else None)
                    P.wait("act", s_pj, u + 1)
                    P.op("act", "activation", out=dst[:, hd, tsl], in_=bk[:, :], func=AF.Copy, inc=s_evA)
                    n_evA += 1
                    units.append((s_evA, n_evA))
                elif ui in (2, 3):
                    sp_ = ui - 2
                    for si in range(2):
                        sub = sp_ * 2 + si
                        for k in range(LC):
                            last = (k == LC - 1) and si == 1
                            P.op("pe", "matmul", bk[:, si * 256:(si + 1) * 256], ckb[b][:, k, sub * 128:(sub + 1) * 128],
                                 wuv[:, k, :], start=(k == 0), stop=(k == LC - 1), inc=s_pj if last else None)
                    P.wait("dve", s_pj, u + 1)
                    n0 = tt * 4 + sp_ * 2
                    P.op("dve", "tensor_copy", out=Vs[:, n0:n0 + 2, :], in_=bk[:, :].rearrange("p (s c) -> p s c", c=256),
                         inc=s_evD)
                    n_evD += 1
                    units.append((s_evD, n_evD))
                else:
                    hq = (ui - 6) // 2
                    var = (ui - 6) % 2
                    w = wqr if var == 0 else wqr2
                    for k in range(LC):
                        P.op("pe", "matmul", bk[:, :], w[:, k, hq * 128:(hq + 1) * 128], cqb[b][:, k, :],
                             start=(k == 0), stop=(k == LC - 1), inc=s_pj if k == LC - 1 else None)
                    P.wait("dve", s_pj, u + 1)
                    if var == 0:
                        P.op("dve", "tensor_tensor", out=tmp1[:, :], in0=bk[:, :], in1=csb[b][:, 0, :], op=ALU.mult,
                             inc=s_evD)
                        n_evD += 1
                    else:
                        P.op("dve", "tensor_tensor", out=tmp2[:, :], in0=bk[:, :], in1=csb[b][:, 1, :], op=ALU.mult,
                             inc=s_evD)
                        n_evD += 1
                        P.wait("dve", s_evD, n_evD)
                        P.op("dve", "tensor_tensor", out=QrT[:, hq, tsl], in0=tmp1[:, :], in1=tmp2[:, :], op=ALU.add,
                             inc=s_evD)
                        n_evD += 1
                    units.append((s_evD, n_evD))
            evd_at[tt] = n_evD
        P.wait("pe", s_evA, n_evA)
        P.wait("pe", s_evD, n_evD)
        steps = []
        for hd in range(2):
            for qt in range(NT):
                nk = 4 * qt + 4
                for kt in range(nk):
                    steps.append((hd, qt, kt, nk))
        n_acc = 0

        def emit_qk_pair(p):
            g0 = 2 * p
            P.wait("pe", s_ex, g0 + 1 - 3)
            info = []
            for j in range(2):
                g = g0 + j
                hd, qt, kt, nk = steps[g]
                i = kt - 4 * qt
                c0 = 128 * i if i > 0 else 0
                info.append((g, hd, qt, kt, i, c0, bank[g % 4]))
            for (g, hd, qt, kt, i, c0, sb_) in info:
                q0 = qt * 512
                P.op("pe", "matmul", sb_[:, c0:512], KnT[:, hd, kt * 128:(kt + 1) * 128],
                     QnT[:, hd, q0 + c0:q0 + 512], start=True, stop=False)
            for j, (g, hd, qt, kt, i, c0, sb_) in enumerate(info):
                q0 = qt * 512
                rows = slice(64 * j, 64 * (j + 1))
                P.op("pe", "matmul", sb_[:, c0:512], krT[rows, kt * 128:(kt + 1) * 128],
                     QrT[rows, hd, q0 + c0:q0 + 512], start=False, stop=(i < 0),
                     inc=s_qk if (i < 0) else None)
            for (g, hd, qt, kt, i, c0, sb_) in info:
                if i >= 0:
                    P.op("pe", "matmul", sb_[:, c0:c0 + 128], ident, tri, start=False, stop=True, inc=s_qk)
            for (g, hd, qt, kt, i, c0, sb_) in info:
                P.wait("act", s_qk, g0 + 2)
                P.wait("act", s_pv, g - 3)
                P.op("act", "activation", out=ptb[g % 4][:, c0:512], in_=sb_[:, c0:512], func=AF.Exp, scale=scale,
                     inc=s_ex)

        def emit_pv(g):
            nonlocal n_acc
            hd, qt, kt, nk = steps[g]
            i = kt - 4 * qt
            c0 = 128 * i if i > 0 else 0
            a = n_acc
            ob, lb = bank[4 + 2 * (a % 2)], bank[5 + 2 * (a % 2)]
            P.wait("pe", s_ex, g + 1)
            if kt == 0:
                P.wait("pe", s_fin, a - 1)
            P.op("pe", "matmul", ob[:, c0:512], Vs[:, kt, hd * 128:(hd + 1) * 128], ptb[g % 4][:, c0:512],
                 start=(kt == 0), stop=(kt == nk - 1))
            P.op("pe", "matmul", lb[:, c0:512], onesb[:, :], ptb[g % 4][:, c0:512],
                 start=(kt == 0), stop=(kt == nk - 1), inc=s_pv)
            if kt == nk - 1:
                q0 = qt * 512
                P.wait("dve", s_pv, g + 1)
                P.wait("dve", s_out, 16 * (a - 1))
                P.op("dve", "reciprocal", rl[:, :], lb[:, :], inc=s_acc)
                P.wait("dve", s_acc, a + 1)
                P.op("dve", "tensor_tensor", out=oout[a % 2][:, :], in0=ob[:, :], in1=rl[:, :], op=ALU.mult, inc=s_fin)
                P.wait("sp", s_fin, a + 1)
                P.dma("sp", o_d[hd * 128:(hd + 1) * 128, q0:q0 + 512], oout[a % 2][:, :], s_out)
                n_acc += 1

        G = len(steps)
        assert G % 2 == 0
        emit_qk_pair(0)
        for p in range(G // 2):
            if p + 1 < G // 2:
                emit_qk_pair(p + 1)
            emit_pv(2 * p)
            emit_pv(2 * p + 1)
        P.wait("sp", s_out, s_out.n)
        P.run()
    return nc


def rope_tables_ext(pos):
    inv = (1.0 / (10000.0 ** (np.arange(0, 64, 2, dtype=np.float32) / np.float32(64)))).astype(np.float32)
    ang = pos.astype(np.float32)[None, :] * inv[:, None]
    cos, sin = np.cos(ang).astype(np.float32), np.sin(ang).astype(np.float32)
    return np.concatenate([cos, cos], 0), np.concatenate([-sin, sin], 0)


def mla_consts():
    k = np.arange(128)[:, None]
    j = np.arange(128)[None, :]
    tri = np.where(k <= j, 0.0, -30000.0).astype(np.float32)
    return np.stack([np.eye(128, dtype=np.float32), tri]).astype(ml_dtypes.bfloat16)


def til(w, ncols=128):
    din, dout = w.shape
    return np.ascontiguousarray(w.reshape(din // 128, 128, dout // ncols, ncols).transpose(2, 1, 0, 3))


def build_token_program(wo=False, ffn2=False, latent=False, ffn1=False, mix=None, final=False):
    from contextlib import ExitStack
    nc = bass.Bass("TRN2", target_bir_lowering=False)

    def din(name, shape, dt=F32):
        return nc.dram_tensor(name, list(shape), dt, kind="ExternalInput").ap()

    def dout(name, shape, dt=F32):
        return nc.dram_tensor(name, list(shape), dt, kind="ExternalOutput").ap()

    x_d = din("xT_in", [D, T])
    y_d = dout("xT_out", [D, T])
    gnames = []
    if ffn2:
        gnames.append("g_ffn2")
    if latent:
        gnames += ["g_kv"]
    if ffn1:
        gnames.append("g_ffn1")
    if mix:
        gnames.append("g_mix")
    if final:
        gnames.append("g_final")
    g_d = {n: din(n, [128, KC]) for n in gnames}
    if wo:
        o_d = din("oT_in", [D, T], BF16)
        wo_d = din("wo", [KC, 128, KC, 128])
    if ffn2:
        w2 = (din("f2_wg", [FC, 128, KC, 128]), din("f2_wu", [FC, 128, KC, 128]), din("f2_wd", [2, KC, 128, FH, 128]))
    if ffn1:
        w1 = (din("f1_wg", [FC, 128, KC, 128]), din("f1_wu", [FC, 128, KC, 128]), din("f1_wd", [2, KC, 128, FH, 128]))
    if latent:
        wdkv_d = din("wdkv", [LC, 128, KC, 128])
        gckv_d = din("g_ckv", [128, LC])
        wkr_d = din("wkr", [2, 128, KC, 64])
        cs_d = din("cs_tok", [2, 64, T])
        ckv_o = dout("ckvT_out", [512, T], BF16)
        kr_o = dout("krT_out", [64, T], BF16)
    if mix == "cq":
        wdq_d = din("wdq", [LC, 128, KC, 128])
        gcq_d = din("g_cq", [128, LC])
        cq_o = dout("cqT_out", [512, T], BF16)
    if mix == "h":
        h_o = dout("hT_out", [D, T], BF16)

    with ExitStack() as st:
        C = Ctx(nc, st)
        P = Prog(nc)
        tp = TokenPhase(nc, P, C)
        tp.epsb = C.sbuf([128, 1], F32, "epsb")
        P.op("dve", "memset", tp.epsb[:, :], EPS, inc=tp.s_init)
        P.wait("act", tp.s_init, 2)
        g_sb = {n: C.sbuf([128, KC], F32, n + "_sb") for n in gnames}
        tp.load_x(x_d)
        for n in gnames:
            tp.load(g_sb[n][:, :], g_d[n][:, :])
        if latent or mix == "cq":
            lat = C.sbuf([128, LC, T], F32, "lat")
            latb = tp.aT[:, 0:LC, :]
            s_misc = C.sem("s_misc")
        if latent:
            gckv = C.sbuf([128, LC], F32, "gckv_sb")
            cs_sb = C.sbuf([64, 2, T], F32, "cs_sb")
            krb = tp.aT[0:64, LC + 1, :]
            tp.load(gckv[:, :], gckv_d[:, :])
            tp.load(cs_sb[:, :, :], cs_d.rearrange("a p t -> p a t"))
        if mix == "cq":
            gcq = C.sbuf([128, LC], F32, "gcq_sb")
            tp.load(gcq[:, :], gcq_d[:, :])
        n_lat_out = 0
        if wo:
            tp.load(tp.aT[:, 0:KC, :], o_d.rearrange("(k p) t -> p k t", p=128))
            for e in ("dve",):
                P.wait(e, tp.s_x, tp.s_x.n)
            tp.add_proj(tp.aT, KC, [wo_d[dc] for dc in range(KC)])
        if ffn2:
            tp.norm(g_sb["g_ffn2"], tp.hT)
            tp.ffn(*w2)
        if latent:
            tp.norm(g_sb["g_kv"], tp.hT)
            tp.proj_to(tp.hT, KC, [wdkv_d[oc] for oc in range(LC)], lat)
            tp.norm(gckv, latb, src=lat, nk=LC)
            P.wait("sp", tp.s_h, tp.s_h.n)
            P.dma("sp", ckv_o.rearrange("(k p) t -> p k t", p=128), latb, tp.s_out)
            n_lat_out = tp.s_out.n
            tA = lat[0:64, 0, :]
            tB = lat[0:64, 1, :]
            P.wait("dve", tp.s_h, tp.s_h.n)

            def kr_evac(v, th, tsl, bk, inc):
                P.op("dve", "tensor_tensor", out=(tA if v == 0 else tB)[:, tsl], in0=bk[0:64, :],
                     in1=cs_sb[:, v, tsl], op=ALU.mult, inc=inc)
            tp.proj_units(tp.hT, KC, [wkr_d[0], wkr_d[1]], 64, kr_evac)
            P.wait("dve", tp.s_y, tp.s_y.n)
            P.op("dve", "tensor_tensor", out=krb, in0=tA, in1=tB, op=ALU.add, inc=s_misc)
            P.wait("sp", s_misc, s_misc.n)
            P.dma("sp", kr_o, krb, tp.s_out)
        if ffn1:
            tp.norm(g_sb["g_ffn1"], tp.hT)
            tp.ffn(*w1)
        if mix:
            tp.norm(g_sb["g_mix"], tp.hT)
        if mix == "h":
            P.wait("sp", tp.s_h, tp.s_h.n)
            hv = h_o.rearrange("(k p) t -> p k t", p=128)
            for k in range(0, KC, 4):
                P.dma("sp", hv[:, k:k + 4, :], tp.hT[:, k:k + 4, :], tp.s_out)
        if mix == "cq":
            if latent:
                P.wait("dve", tp.s_out, n_lat_out)
                P.wait("dve", s_misc, s_misc.n)
            tp.proj_to(tp.hT, KC, [wdq_d[oc] for oc in range(LC)], lat)
            tp.norm(gcq, latb, src=lat, nk=LC)
            P.wait("sp", tp.s_h, tp.s_h.n)
            P.dma("sp", cq_o.rearrange("(k p) t -> p k t", p=128), latb, tp.s_out)
        if final:
            tp.norm(g_sb["g_final"], tp.xT)
            P.wait("sp", tp.s_h, tp.s_h.n)
        tp.store_x(y_d)
        tp.finish()
        P.run()
    return nc


_PROGS = {}


def _prog(key, fn, **kw):
    if key not in _PROGS:
        _PROGS[key] = fn(**kw)
    return _PROGS[key]


def _launch(nc, maps):
    res = run_bass_kernel_spmd(nc, maps, core_ids=list(range(NCORES)))
    return res.results


def _f32(a):
    return np.ascontiguousarray(np.asarray(a, dtype=np.float32))


def kernel(x, ffn_norm1, ffn1_wg, ffn1_wu, ffn1_wd, mix_norm, ffn_norm2, ffn2_wg, ffn2_wu, ffn2_wd,
           a_wqkv, a_wo, kv_norm, b_wdkv, b_ckv_norm, b_wkr, b_wuk, b_wuv,
           b_wdq, b_cq_norm, b_wuq, b_wo, final_norm):
    x = _f32(x)
    xT = np.ascontiguousarray(x[0].T)
    xs = [np.ascontiguousarray(xT[:, c * T:(c + 1) * T]) for c in range(NCORES)]
    swap = (np.arange(64) + 32) % 64

    def ffn_w(l, which):
        wg, wu, wd = (ffn1_wg, ffn1_wu, ffn1_wd) if which == 1 else (ffn2_wg, ffn2_wu, ffn2_wd)
        p = "f1_" if which == 1 else "f2_"
        return {p + "wg": tile_wgu(_f32(wg[l])), p + "wu": tile_wgu(_f32(wu[l])), p + "wd": tile_wd(_f32(wd[l]))}

    def gc(g):
        return gcol_of(_f32(g))

    def gath_tok(res, name):
        return np.ascontiguousarray(np.concatenate([np.asarray(r[name]) for r in res], axis=1))

    def tok_shards(full):
        return [np.ascontiguousarray(full[:, c * T:(c + 1) * T]) for c in range(NCORES)]

    com = dict(ffn_w(0, 1), g_ffn1=gc(ffn_norm1[0]), g_mix=gc(mix_norm[0]))
    nc = _prog("T_first", build_token_program, ffn1=True, mix="h")
    res = _launch(nc, [dict(com, xT_in=xs[c]) for c in range(NCORES)])
    xs = [np.asarray(r["xT_out"]) for r in res]
    hT_all = gath_tok(res, "hT_out")
    oT = None
    for l in range(DEPTH):
        if l < 2:
            wqkv = _f32(a_wqkv[l])

            def hw(w, c):
                return np.ascontiguousarray(w[:, c * 256:(c + 1) * 256].reshape(KC, 128, 256).transpose(1, 0, 2))
            nc = _prog("A", build_attn_a_program)
            hT_t = tile_hT(hT_all)
            maps = [{"hT_all": hT_t, "wq": hw(wqkv[:, 0:2048], c), "wk": hw(wqkv[:, 2048:4096], c),
                     "wv": hw(wqkv[:, 4096:6144], c), "emask": alibi_masks([2 * c, 2 * c + 1])}
                    for c in range(NCORES)]
            res = _launch(nc, maps)
            wo_l = _f32(a_wo[l])
        else:
            jb = l - 2
            wuq, wuk, wuv = _f32(b_wuq[jb]), _f32(b_wuk), _f32(b_wuv)
            ce, se = rope_tables_ext(np.arange(S))
            cs = np.stack([np.concatenate([ce, ce], 0), np.concatenate([se, se], 0)])

            def t4(w):
                return np.ascontiguousarray(w.reshape(LC, 128, -1).transpose(1, 0, 2))
            nc = _prog("M", build_mla_program)
            maps = []
            for c in range(NCORES):
                hs = [2 * c, 2 * c + 1]
                maps.append({"cqT_all": cqT_all, "ckvT_all": ckvT_all, "krT_all": krT_all,
                             "wuq_n": t4(np.concatenate([wuq[:, h, :128] for h in hs], 1)),
                             "wuq_r": t4(np.concatenate([wuq[:, h, 128:] for h in hs for _ in range(2)], 1)),
                             "wuq_r2": t4(np.concatenate([wuq[:, h, 128:][:, swap] for h in hs for _ in range(2)], 1)),
                             "wuk": t4(np.concatenate([wuk[:, h] for h in hs], 1)),
                             "wuv": t4(np.concatenate([wuv[:, h] for h in hs], 1)),
                             "cs": cs, "cst": mla_consts()})
            res = _launch(nc, maps)
            wo_l = _f32(b_wo[jb])
        oT_full = np.ascontiguousarray(np.concatenate([np.asarray(r["oT"]) for r in res], axis=0))
        oTs = tok_shards(oT_full)
        com = dict(ffn_w(l, 2), wo=til(wo_l), g_ffn2=gc(ffn_norm2[l]))
        if l == DEPTH - 1:
            nc = _prog("T_last", build_token_program, wo=True, ffn2=True, final=True)
            com["g_final"] = gc(final_norm)
            res = _launch(nc, [dict(com, xT_in=xs[c], oT_in=oTs[c]) for c in range(NCORES)])
            outT = np.concatenate([np.asarray(r["xT_out"]) for r in res], axis=1)
            return np.ascontiguousarray(outT.T)[None].astype(np.float32)
        com.update(ffn_w(l + 1, 1))
        com["g_ffn1"] = gc(ffn_norm1[l + 1])
        com["g_mix"] = gc(mix_norm[l + 1])
        per_core = [dict(xT_in=xs[c], oT_in=oTs[c]) for c in range(NCORES)]
        if l + 1 < 2:
            nc = _prog("T_mid_h", build_token_program, wo=True, ffn2=True, ffn1=True, mix="h")
        else:
            jb = l + 1 - 2
            com["wdq"] = til(_f32(b_wdq[jb]))
            com["g_cq"] = np.ascontiguousarray(_f32(b_cq_norm[jb]).reshape(LC, 128).T)
            if l + 1 == 2:
                nc = _prog("T_mid_lat", build_token_program, wo=True, ffn2=True, latent=True, ffn1=True, mix="cq")
                com["g_kv"] = gc(kv_norm)
                com["wdkv"] = til(_f32(b_wdkv))
                com["g_ckv"] = np.ascontiguousarray(_f32(b_ckv_norm).reshape(LC, 128).T)
                wkr = _f32(b_wkr)
                com["wkr"] = np.stack([til(wkr, 64)[0], til(np.ascontiguousarray(wkr[:, swap]), 64)[0]])
                for c in range(NCORES):
                    ce, se = rope_tables_ext(np.arange(c * T, (c + 1) * T))
                    per_core[c]["cs_tok"] = np.stack([ce, se])
            else:
                nc = _prog("T_mid_cq", build_token_program, wo=True, ffn2=True, ffn1=True, mix="cq")
        res = _launch(nc, [dict(com, **per_core[c]) for c in range(NCORES)])
        xs = [np.asarray(r["xT_out"]) for r in res]
        if l + 1 < 2:
            hT_all = gath_tok(res, "hT_out")
        else:
            cqT_all = gath_tok(res, "cqT_out")
            if l + 1 == 2:
                ckvT_all = gath_tok(res, "ckvT_out")
                krT_all = gath_tok(res, "krT_out")
```

```python
import numpy as np
import ml_dtypes
import concourse.bass as bass
import concourse.mybir as mybir
from concourse.bass_utils import run_bass_kernel_spmd

F32 = mybir.dt.float32
BF16 = mybir.dt.bfloat16
AF = mybir.ActivationFunctionType
ALU = mybir.AluOpType

NCORES = 8
D = 2048
S = 8192
T = S // NCORES
KC = D // 128
DFF = 5632
FC = DFF // 128
FH = FC // 2
EPS = 1e-6
DEPTH = 4


class Sem:
    def __init__(self, h):
        self.h = h
        self.n = 0


class Prog:
    def __init__(self, nc):
        self.nc = nc
        self.q = {"pe": [], "act": [], "dve": [], "pool": [], "sp": []}

    def op(self, eng, name, *args, inc=None, incv=None, **kw):
        if inc is not None:
            v = incv if incv is not None else 1
            inc.n += v
            h = inc.h

            def f(e, name=name, args=args, kw=kw, h=h, v=v):
                getattr(e, name)(*args, **kw).then_inc(h, v)
        else:
            def f(e, name=name, args=args, kw=kw):
                getattr(e, name)(*args, **kw)
        self.q[eng].append(f)

    def dma(self, eng, out, in_, sem):
        self.op(eng, "dma_start", out=out, in_=in_, inc=sem, incv=16)

    def wait(self, eng, sem, val):
        if val <= 0:
            return
        h = sem.h
        self.q[eng].append(lambda e, h=h, val=val: e.wait_ge(h, val))

    def run(self):
        with self.nc.Block() as block:
            block.tensor(lambda e: [f(e) for f in self.q["pe"]])
            block.scalar(lambda e: [f(e) for f in self.q["act"]])
            block.vector(lambda e: [f(e) for f in self.q["dve"]])
            block.gpsimd(lambda e: [f(e) for f in self.q["pool"]])
            block.sync(lambda e: [f(e) for f in self.q["sp"]])


class Ctx:
    def __init__(self, nc, stack):
        self.nc = nc
        self.stack = stack
        self.k = 0

    def sbuf(self, shape, dt, name=None):
        self.k += 1
        return self.stack.enter_context(self.nc.sbuf_tensor(name or f"sb{self.k}", list(shape), dt))

    def psum(self, shape, dt=F32, name=None):
        self.k += 1
        return self.stack.enter_context(self.nc.psum_tensor(name or f"ps{self.k}", list(shape), dt))

    def sem(self, name=None):
        self.k += 1
        return Sem(self.stack.enter_context(self.nc.semaphore(name or f"sem{self.k}")))


class TokenPhase:
    def __init__(self, nc, P, C):
        self.nc, self.P, self.C = nc, P, C
        self.xT = C.sbuf([128, KC, T], F32, "xT")
        self.hT = C.sbuf([128, KC, T], BF16, "hT")
        self.aT = C.sbuf([128, FH, T], BF16, "aT")
        self.scr = C.sbuf([128, 4, 512], F32, "scr")
        self.ones = C.sbuf([128, 128], F32, "ones")
        self.rstd = C.sbuf([128, 512], F32, "rstd")
        self.sg = [C.sbuf([128, 512], F32, f"sg{i}") for i in range(2)]
        self.wg = [C.sbuf([128, KC, 128], BF16, f"wg{i}") for i in range(2)]
        self.wu = [C.sbuf([128, KC, 128], BF16, f"wu{i}") for i in range(2)]
        self.wd = [C.sbuf([128, FH, 128], BF16, f"wd{i}") for i in range(2)]
        self.bank = [C.psum([128, 512], F32, f"bk{i}") for i in range(8)]
        self.s_init = C.sem("s_init")
        self.s_x = C.sem("s_x")
        self.s_sq = C.sem("s_sq")
        self.s_st = C.sem("s_st")
        self.s_sqrt = C.sem("s_sqrt")
        self.s_rs = C.sem("s_rs")
        self.s_h = C.sem("s_h")
        self.s_wgu = C.sem("s_wgu")
        self.s_gu = C.sem("s_gu")
        self.s_sl = C.sem("s_sl")
        self.s_a = C.sem("s_a")
        self.s_wd = C.sem("s_wd")
        self.s_dn = C.sem("s_dn")
        self.s_y = C.sem("s_y")
        self.s_out = C.sem("s_out")
        self.n_wgu = self.n_wd = self.n_gu = self.n_dn = self.n_hn = self.n_sq = 0
        P.op("dve", "memset", self.ones[:, :], 1.0, inc=self.s_init)
        P.wait("pe", self.s_init, 1)

    def load(self, dst, src):
        self.P.dma("sp", dst, src, self.s_x)

    def load_x(self, x_dram):
        xv = x_dram.rearrange("(k p) t -> p k t", p=128)
        for k in range(0, KC, 4):
            self.load(self.xT[:, k:k + 4, :], xv[:, k:k + 4, :])

    def store_x(self, y_dram):
        P = self.P
        yv = y_dram.rearrange("(k p) t -> p k t", p=128)
        P.wait("sp", self.s_y, self.s_y.n)
        for k in range(0, KC, 4):
            P.dma("sp", yv[:, k:k + 4, :], self.xT[:, k:k + 4, :], self.s_out)

    def finish(self):
        self.P.wait("sp", self.s_out, self.s_out.n)

    def norm(self, gcol, out_tile, src=None, nk=KC):
        P = self.P
        src = self.xT if src is None else src
        for e in ("act", "dve"):
            P.wait(e, self.s_x, self.s_x.n)
            P.wait(e, self.s_y, self.s_y.n)
        for th in range(2):
            n = self.n_hn
            self.n_hn += 1
            tsl = slice(th * 512, (th + 1) * 512)
            P.wait("pe", self.s_sqrt, n)
            for k in range(nk):
                q = self.n_sq
                self.n_sq += 1
                P.wait("act", self.s_st, q - 3)
                P.op("act", "activation", out=self.scr[:, q % 4, :], in_=src[:, k, tsl],
                     func=AF.Square, inc=self.s_sq)
                P.wait("pe", self.s_sq, q + 1)
                P.op("pe", "matmul", self.bank[7][:, :], self.ones[:, :], self.scr[:, q % 4, :],
                     start=(k == 0), stop=(k == nk - 1), inc=self.s_st)
            P.wait("act", self.s_st, self.n_sq)
            P.wait("act", self.s_h, self.s_h.n)
            P.op("act", "activation", out=self.rstd[:, :], in_=self.bank[7][:, :], func=AF.Sqrt,
                 scale=1.0 / (128 * nk), bias=self.epsb[:, 0:1], inc=self.s_sqrt)
            P.wait("dve", self.s_sqrt, n + 1)
            P.op("dve", "reciprocal", self.rstd[:, :], self.rstd[:, :], inc=self.s_rs)
            P.wait("dve", self.s_rs, n + 1)
            for k in range(nk):
                P.op("dve", "scalar_tensor_tensor", out=out_tile[:, k, tsl], in0=src[:, k, tsl],
                     scalar=gcol[:, k:k + 1], op0=ALU.mult, in1=self.rstd[:, :], op1=ALU.mult,
                     inc=self.s_h)

    def ffn(self, wg_d, wu_d, wd_d):
        P = self.P
        h_ready = self.s_h.n
        P.wait("dve", self.s_out, self.s_out.n)
        for fh in range(2):
            gu0 = self.n_gu
            P.wait("pe", self.s_y, self.s_y.n)
            P.wait("pe", self.s_h, h_ready)
            for j in range(FH):
                fc = fh * FH + j
                c = self.n_wgu
                self.n_wgu += 1
                b = c % 2
                P.wait("pool", self.s_gu, 2 * (c - 1))
                P.dma("pool", self.wg[b][:, :, :], wg_d[fc], self.s_wgu)
                P.dma("pool", self.wu[b][:, :, :], wu_d[fc], self.s_wgu)
                P.wait("pe", self.s_wgu, 32 * (c + 1))
                for th in range(2):
                    n = self.n_gu
                    self.n_gu += 1
                    pr = n % 4
                    gps, ups = self.bank[2 * pr], self.bank[2 * pr + 1]
                    tsl = slice(th * 512, (th + 1) * 512)
                    if n - gu0 >= 4:
                        P.wait("pe", self.s_a, n - 3)
                    for k in range(KC):
                        P.op("pe", "matmul", gps[:, :], self.wg[b][:, k, :], self.hT[:, k, tsl],
                             start=(k == 0), stop=(k == KC - 1))
                    for k in range(KC):
                        last = k == KC - 1
                        P.op("pe", "matmul", ups[:, :], self.wu[b][:, k, :], self.hT[:, k, tsl],
                             start=(k == 0), stop=last, inc=self.s_gu if last else None)
                    P.wait("act", self.s_gu, n + 1)
                    P.wait("act", self.s_a, n - 1)
                    P.op("act", "activation", out=self.sg[n % 2][:, :], in_=gps[:, :], func=AF.Silu,
                         inc=self.s_sl)
                    P.wait("dve", self.s_sl, n + 1)
                    P.op("dve", "tensor_tensor", out=self.aT[:, j, tsl], in0=ups[:, :],
                         in1=self.sg[n % 2][:, :], op=ALU.mult, inc=self.s_a)
            P.wait("pe", self.s_a, self.n_gu)
            self.proj_units(self.aT, FH, [wd_d[fh, dc] for dc in range(KC)], 128,
                            lambda dc, th, tsl, bk, inc: P.op(
                                "dve", "scalar_tensor_tensor", out=self.xT[:, dc, tsl], in0=bk[:, :], scalar=0.5,
                                op0=ALU.mult, in1=self.xT[:, dc, tsl], op1=ALU.add, inc=inc))

    def proj_units(self, src, nk, w_list, ncols, evac):
        P = self.P
        P.wait("pe", self.s_sqrt, self.s_sqrt.n)
        P.wait("pe", self.s_h, self.s_h.n)
        P.wait("pe", self.s_x, self.s_x.n)
        for oc, w_ap in enumerate(w_list):
            c = self.n_wd
            self.n_wd += 1
            b = c % 2
            P.wait("pool", self.s_dn, 2 * (c - 1))
            P.dma("pool", self.wd[b][:, 0:nk, 0:ncols], w_ap, self.s_wd)
            P.wait("pe", self.s_wd, 16 * (c + 1))
            for th in range(2):
                m = self.n_dn
                self.n_dn += 1
                bk = self.bank[m % 8]
                tsl = slice(th * 512, (th + 1) * 512)
                P.wait("pe", self.s_y, m - 7)
                for j in range(nk):
                    last = j == nk - 1
                    P.op("pe", "matmul", bk[0:ncols, :], self.wd[b][:, j, 0:ncols], src[:, j, tsl],
                         start=(j == 0), stop=last, inc=(self.s_dn if last else None))
                P.wait("dve", self.s_dn, m + 1)
                evac(oc, th, tsl, bk, self.s_y)

    def add_proj(self, src, nk, w_list):
        P = self.P
        self.proj_units(src, nk, w_list, 128,
                        lambda oc, th, tsl, bk, inc: P.op(
                            "dve", "tensor_tensor", out=self.xT[:, oc, tsl], in0=bk[:, :],
                            in1=self.xT[:, oc, tsl], op=ALU.add, inc=inc))

    def proj_to(self, src, nk, w_list, dst):
        P = self.P
        self.proj_units(src, nk, w_list, 128,
                        lambda oc, th, tsl, bk, inc: P.op(
                            "dve", "tensor_copy", out=dst[:, oc, tsl], in_=bk[:, :], inc=inc))


def tile_wgu(w):
    return np.ascontiguousarray(w.reshape(KC, 128, FC, 128).transpose(2, 1, 0, 3))


def tile_wd(w):
    return np.ascontiguousarray(w.reshape(2, FH, 128, KC, 128).transpose(0, 3, 2, 1, 4))


def gcol_of(g):
    return np.ascontiguousarray(g.reshape(KC, 128).T)


def build_ffn_program():
    from contextlib import ExitStack
    nc = bass.Bass("TRN2", target_bir_lowering=False)
    x_d = nc.dram_tensor("xT_in", [D, T], F32, kind="ExternalInput").ap()
    g_d = nc.dram_tensor("gcol", [128, KC], F32, kind="ExternalInput").ap()
    wg_d = nc.dram_tensor("wg", [FC, 128, KC, 128], F32, kind="ExternalInput").ap()
    wu_d = nc.dram_tensor("wu", [FC, 128, KC, 128], F32, kind="ExternalInput").ap()
    wd_d = nc.dram_tensor("wd", [2, KC, 128, FH, 128], F32, kind="ExternalInput").ap()
    y_d = nc.dram_tensor("xT_out", [D, T], F32, kind="ExternalOutput").ap()
    with ExitStack() as st:
        C = Ctx(nc, st)
        P = Prog(nc)
        tp = TokenPhase(nc, P, C)
        gcol = C.sbuf([128, KC], F32, "gcol_sb")
        tp.epsb = C.sbuf([128, 1], F32, "epsb")
        P.op("dve", "memset", tp.epsb[:, :], EPS, inc=tp.s_init)
        P.wait("act", tp.s_init, 2)
        tp.load_x(x_d)
        tp.load(gcol[:, :], g_d[:, :])
        tp.norm(gcol, tp.hT)
        tp.ffn(wg_d, wu_d, wd_d)
        tp.store_x(y_d)
        tp.finish()
        P.run()
    return nc


DILS = (1, 4, 16)


def ss(start, n, step):
    return slice(start, start + (n - 1) * step + 1, step)

BLK = 2048
NB = S // BLK


def alibi_masks(heads):
    k = np.arange(128)[:, None].astype(np.float64)
    j = np.arange(128)[None, :].astype(np.float64)
    out = np.zeros((3, len(heads), 128, 256), np.float32)
    for di, d in enumerate(DILS):
        for hi, h in enumerate(heads):
            slope = 2.0 ** (-8.0 * (h + 1) / 16)
            lo = np.where(j >= k, np.exp(-slope * d * np.maximum(j - k, 0.0)), 0.0)
            hi_ = np.where(j <= k, np.exp(-slope * d * np.maximum(128 + j - k, 0.0)), 0.0)
            out[di, hi, :, 0:128] = hi_
            out[di, hi, :, 128:256] = lo
    return out


def build_attn_a_program():
    from contextlib import ExitStack
    nc = bass.Bass("TRN2", target_bir_lowering=False)
    NT = S // 512
    h_d = nc.dram_tensor("hT_all", [NT, 128, KC, 512], BF16, kind="ExternalInput").ap()
    wq_d = nc.dram_tensor("wq", [128, KC, 256], F32, kind="ExternalInput").ap()
    wk_d = nc.dram_tensor("wk", [128, KC, 256], F32, kind="ExternalInput").ap()
    wv_d = nc.dram_tensor("wv", [128, KC, 256], F32, kind="ExternalInput").ap()
    em_d = nc.dram_tensor("emask", [3, 2, 128, 256], F32, kind="ExternalInput").ap()
    o_d = nc.dram_tensor("oT", [256, S], BF16, kind="ExternalOutput").ap()
    vd = nc.dram_tensor("v_scratch", [S, 256], BF16).ap()
    scale = 128.0 ** -0.5
    with ExitStack() as st:
        C = Ctx(nc, st)
        P = Prog(nc)
        wq = C.sbuf([128, KC, 256], BF16, "wq_sb")
        wk = C.sbuf([128, KC, 256], BF16, "wk_sb")
        wv = C.sbuf([128, KC, 256], BF16, "wv_sb")
        em = C.sbuf([128, 3, 2, 256], F32, "em_sb")
        QT = C.sbuf([128, 2, S], BF16, "QT")
        KT = C.sbuf([128, 2, S], BF16, "KT")
        hbuf = [C.sbuf([128, KC, 512], BF16, f"hbuf{i}") for i in range(2)]
        vst = [C.sbuf([128, 2, 256], BF16, f"vst{i}") for i in range(2)]
        vn = [C.sbuf([128, 16, 256], BF16, f"vn{i}") for i in range(2)]
        v4 = [C.sbuf([128, 4, 4, 256], BF16, f"v4{i}") for i in range(2)]
        v16 = [C.sbuf([128, 16, 256], BF16, f"v16{i}") for i in range(2)]
        vn.append(hbuf[0][:, 0:8, :].rearrange("p a (b c) -> p (a b) c", c=256))
        v4.append(hbuf[0][:, 8:16, :].rearrange("p (r a) (b c) -> p r (a b) c", r=4, c=256))
        v16.append(hbuf[1][:, 0:8, :].rearrange("p a (b c) -> p (a b) c", c=256))
        NV = 3
        acc = C.sbuf([128, 2, BLK], F32, "acc")
        oacc, lacc = acc[:, 0, :], acc[:, 1, :]
        NP, NOL, LOOK = 4, 3, 2
        pbuf = [C.sbuf([128, 512], F32, f"pbuf{i}") for i in range(3)]
        ptb = [C.sbuf([128, 512], BF16, f"ptb{i}") for i in range(3)]
        pbuf.append(hbuf[1][:, 8:10, :].rearrange("p a c -> p (a c)").bitcast(F32))
        ptb.append(hbuf[1][:, 10, :])
        onesb = C.sbuf([128, 128], BF16, "onesb")
        oout = [C.sbuf([128, BLK], BF16, "oout0")]
        bank = [C.psum([128, 512], F32, f"bk{i}") for i in range(8)]
        s_w, s_hb, s_pj, s_evA, s_evD, s_vst, s_vl = (C.sem(n) for n in
                                                      ("s_w", "s_hb", "s_pj", "s_evA", "s_evD", "s_vst", "s_vl"))
        s_init, s_qk, s_ex, s_ptA, s_ptB, s_pv, s_eA, s_eD, s_fin, s_out = (
            C.sem(n) for n in ("s_init", "s_qk", "s_ex", "s_ptA", "s_ptB", "s_pv", "s_eA", "s_eD", "s_fin", "s_out"))
        P.op("dve", "memset", onesb[:, :], 1.0, inc=s_init)
        P.dma("pool", wq[:, :, :], wq_d, s_w)
        P.dma("pool", wk[:, :, :], wk_d, s_w)
        P.dma("pool", wv[:, :, :], wv_d, s_w)
        P.dma("pool", em[:, :, :, :], em_d.rearrange("d h p c -> p d h c"), s_w)
        vdt = vd.rearrange("(n p) c -> p n c", p=128)
        P.wait("pe", s_w, 64)
        P.wait("pe", s_init, 1)
        P.wait("pool", s_w, 64)
        units = []
        n_evA = n_evD = 0
        for tt in range(NT):
            hb = hbuf[tt % 2]
            if tt >= 2:
                P.wait("sp", s_pj, 6 * (tt - 1))
            P.dma("sp", hb[:, 0:KC // 2, :], h_d[tt, :, 0:KC // 2, :], s_hb)
            P.dma("sp", hb[:, KC // 2:KC, :], h_d[tt, :, KC // 2:KC, :], s_hb)
            P.wait("pe", s_hb, 32 * (tt + 1))
            for ui in range(6):
                u = len(units)
                bk = bank[u % 8]
                if u >= 8:
                    P.wait("pe", units[u - 8][0], units[u - 8][1])
                if ui < 4:
                    w = wq if ui < 2 else wk
                    hd = ui % 2
                    for k in range(KC):
                        P.op("pe", "matmul", bk[:, :], w[:, k, hd * 128:(hd + 1) * 128], hb[:, k, :],
                             start=(k == 0), stop=(k == KC - 1), inc=s_pj if k == KC - 1 else None)
                    dst = (QT if ui < 2 else KT)[:, hd, tt * 512:(tt + 1) * 512]
                    P.wait("act", s_pj, u + 1)
                    P.op("act", "activation", out=dst, in_=bk[:, :], func=AF.Copy, inc=s_evA)
                    n_evA += 1
                    units.append((s_evA, n_evA))
                else:
                    sp_ = ui - 4
                    for si in range(2):
                        sub = sp_ * 2 + si
                        for k in range(KC):
                            last = (k == KC - 1) and si == 1
                            P.op("pe", "matmul", bk[:, si * 256:(si + 1) * 256], hb[:, k, sub * 128:(sub + 1) * 128],
                                 wv[:, k, :], start=(k == 0), stop=(k == KC - 1), inc=s_pj if last else None)
                    vb = vst[n_evD % 2]
                    P.wait("dve", s_pj, u + 1)
                    P.wait("dve", s_vst, 16 * (n_evD - 1))
                    P.op("dve", "tensor_copy", out=vb[:, :, :], in_=bk[:, :].rearrange("p (s c) -> p s c", c=256),
                         inc=s_evD)
                    n_evD += 1
                    units.append((s_evD, n_evD))
                    P.wait("pool", s_evD, n_evD)
                    n0 = tt * 4 + sp_ * 2
                    P.dma("pool", vdt[:, n0:n0 + 2, :], vb[:, :, :], s_vst)
        v4d = vd.rearrange("(blk i r) c -> i r blk c", i=128, r=4)
        v16d = vd.rearrange("(blk i r) c -> i r blk c", i=128, r=16)
        P.wait("sp", s_vst, s_vst.n)
        P.wait("pe", s_evA, n_evA)
        groups = []
        for B in range(NB):
            b = B % NV
            pb = (B - 1) % NV
            for hd in range(2):
                hs = slice(hd * 128, (hd + 1) * 128)
                tiles = []
                for ml in range(16):
                    prev = vn[b][:, ml - 1, hs] if ml > 0 else (vn[pb][:, 15, hs] if B > 0 else None)
                    tiles.append((0, 1, B * BLK + ml * 128, vn[b][:, ml, hs], prev, ml * 128))
                for r in range(4):
                    for ml in range(4):
                        prev = v4[b][:, r, ml - 1, hs] if ml > 0 else (v4[pb][:, r, 3, hs] if B > 0 else None)
                        tiles.append((1, 4, B * BLK + r + 4 * 128 * ml, v4[b][:, r, ml, hs], prev, r + 4 * 128 * ml))
                for r in range(16):
                    prev = v16[pb][:, r, hs] if B > 0 else None
                    tiles.append((2, 16, B * BLK + r, v16[b][:, r, hs], prev, r))
                for gi in range(0, len(tiles), 2):
                    groups.append(dict(B=B, hd=hd, tiles=tiles[gi:gi + 2], first=(gi == 0),
                                       last=(gi == len(tiles) - 2)))
        G = len(groups)
        ev_done = []
        st_ = dict(n_eA=0, n_eD=0, n_fin=0, last_d1=(None, 0))
        vl_loaded = set()

        def ensure_v(B):
            if B in vl_loaded or B >= NB:
                return
            vl_loaded.add(B)
            b = B % NV
            if B >= 2:
                P.wait("sp", s_pv, blk_end[B - 2])
                P.wait("sp", s_pj, s_pj.n)
            for q4 in range(4):
                P.dma("sp", vn[b][:, q4 * 4:(q4 + 1) * 4, :], vdt[:, B * 16 + q4 * 4:B * 16 + (q4 + 1) * 4, :], s_vl)
                P.dma("sp", v4[b][:, q4, :, :], v4d[:, q4, 4 * B:4 * B + 4, :], s_vl)
                P.dma("sp", v16[b][:, q4 * 4:(q4 + 1) * 4, :], v16d[:, q4 * 4:(q4 + 1) * 4, B, :], s_vl)

        blk_end = {}
        for B in range(NB):
            blk_end[B] = sum(1 for g_ in groups if g_["B"] <= B)

        def emit_qk(g):
            gr = groups[g]
            hd = gr["hd"]
            di, d = gr["tiles"][0][0], gr["tiles"][0][1]
            sb_ = bank[g % NP]
            P.wait("pe", s_ex, g - NP + 1)
            for ti in range(2):
                _, _, t0, vcur, vprev, _ = gr["tiles"][ti]
                qap = QT[:, hd, ss(t0, 128, d)]
                if vprev is not None:
                    P.op("pe", "matmul", sb_[:, ti * 256:ti * 256 + 128], KT[:, hd, ss(t0 - 128 * d, 128, d)], qap,
                         start=True, stop=True)
                P.op("pe", "matmul", sb_[:, ti * 256 + 128:ti * 256 + 256], KT[:, hd, ss(t0, 128, d)], qap,
                     start=True, stop=True, inc=s_qk if ti == 1 else None)
            P.wait("act", s_qk, g + 1)
            P.wait("act", s_ptA, g - NP + 1)
            P.wait("act", s_ptB, g - NP + 1)
            P.op("act", "activation", out=pbuf[g % NP][:, :], in_=sb_[:, :], func=AF.Exp, scale=scale, inc=s_ex)
            for eng, sem_, c0 in (("dve", s_ptA, 0), ("pool", s_ptB, 256)):
                P.wait(eng, s_ex, g + 1)
                P.wait(eng, s_pv, g - NP + 1)
                P.op(eng, "tensor_tensor", out=ptb[g % NP][:, c0:c0 + 256], in0=pbuf[g % NP][:, c0:c0 + 256],
                     in1=em[:, di, hd, :], op=ALU.mult, inc=sem_)

        def emit_pv(g):
            gr = groups[g]
            hd, B = gr["hd"], gr["B"]
            di, d = gr["tiles"][0][0], gr["tiles"][0][1]
            olb = bank[NP + (g % NOL)]
            pt = ptb[g % NP]
            if gr["first"] and hd == 0:
                P.wait("pe", s_vl, 16 * 12 * (B + 1))
            P.wait("pe", s_ptA, g + 1)
            P.wait("pe", s_ptB, g + 1)
            if g >= NOL:
                P.wait("pe", ev_done[g - NOL][0], ev_done[g - NOL][1])
            for kind in (0, 1):
                dstb = olb
                for ti in range(2):
                    _, _, t0, vcur, vprev, _ = gr["tiles"][ti]
                    oc = slice(kind * 256 + ti * 128, kind * 256 + (ti + 1) * 128)
                    last = kind == 1 and ti == 1
                    if vprev is not None:
                        P.op("pe", "matmul", dstb[:, oc], vprev if kind == 0 else onesb[:, :],
                             pt[:, ti * 256:ti * 256 + 128], start=True, stop=False)
                    P.op("pe", "matmul", dstb[:, oc], vcur if kind == 0 else onesb[:, :],
                         pt[:, ti * 256 + 128:ti * 256 + 256], start=(vprev is None), stop=True,
                         inc=s_pv if last else None)
            loc0 = gr["tiles"][0][5]
            if di == 2:
                dst = acc[:, :, :].rearrange("p a (j r) -> p a r j", r=16)[:, :, loc0:loc0 + 2, :]
                srcv = olb[:, :].rearrange("p (a t j) -> p a t j", a=2, t=2)
            else:
                dst = acc[:, :, ss(loc0, 256, d)]
                srcv = olb[:, :].rearrange("p (a c) -> p a c", a=2)
            if di == 0:
                P.wait("act", s_pv, g + 1)
                if gr["first"]:
                    P.wait("act", s_fin, st_["n_fin"])
                P.op("act", "activation", out=dst, in_=srcv, func=AF.Copy, inc=s_eA)
                st_["n_eA"] += 1
                ev_done.append((s_eA, st_["n_eA"]))
                st_["last_d1"] = (s_eA, st_["n_eA"])
            else:
                P.wait("dve", s_pv, g + 1)
                P.wait("dve", st_["last_d1"][0], st_["last_d1"][1])
                P.op("dve", "tensor_tensor", out=dst, in0=srcv, in1=dst, op=ALU.add, inc=s_eD)
                st_["n_eD"] += 1
                ev_done.append((s_eD, st_["n_eD"]))
            if gr["last"]:
                nf = st_["n_fin"]
                ob_ = oout[0]
                P.wait("dve", s_out, 16 * nf)
                P.op("dve", "reciprocal", lacc, lacc)
                P.op("dve", "tensor_tensor", out=ob_[:, :], in0=oacc, in1=lacc, op=ALU.mult, inc=s_fin)
                st_["n_fin"] += 1
                P.wait("sp", s_fin, st_["n_fin"])
                P.dma("sp", o_d[hd * 128:(hd + 1) * 128, B * BLK:(B + 1) * BLK], ob_[:, :], s_out)
                if hd == 1:
                    ensure_v(B + 2)

        ensure_v(0)
        ensure_v(1)
        for g in range(min(LOOK, G)):
            emit_qk(g)
        for g in range(G):
            if g + LOOK < G:
                emit_qk(g + LOOK)
            emit_pv(g)
        P.wait("sp", s_out, s_out.n)
        P.run()
    return nc


def tile_hT(hT_all):
    return np.ascontiguousarray(hT_all.reshape(KC, 128, S // 512, 512).transpose(2, 1, 0, 3))


LC = 4


def build_mla_program():
    from contextlib import ExitStack
    nc = bass.Bass("TRN2", target_bir_lowering=False)
    cq_d = nc.dram_tensor("cqT_all", [512, S], BF16, kind="ExternalInput").ap()
    ckv_d = nc.dram_tensor("ckvT_all", [512, S], BF16, kind="ExternalInput").ap()
    kr_d = nc.dram_tensor("krT_all", [64, S], BF16, kind="ExternalInput").ap()
    wqn_d = nc.dram_tensor("wuq_n", [128, LC, 256], F32, kind="ExternalInput").ap()
    wqr_d = nc.dram_tensor("wuq_r", [128, LC, 256], F32, kind="ExternalInput").ap()
    wqr2_d = nc.dram_tensor("wuq_r2", [128, LC, 256], F32, kind="ExternalInput").ap()
    wuk_d = nc.dram_tensor("wuk", [128, LC, 256], F32, kind="ExternalInput").ap()
    wuv_d = nc.dram_tensor("wuv", [128, LC, 256], F32, kind="ExternalInput").ap()
    cs_d = nc.dram_tensor("cs", [2, 128, S], F32, kind="ExternalInput").ap()
    cst_d = nc.dram_tensor("cst", [2, 128, 128], BF16, kind="ExternalInput").ap()
    o_d = nc.dram_tensor("oT", [256, S], BF16, kind="ExternalOutput").ap()
    scale = 192.0 ** -0.5
    with ExitStack() as st:
        C = Ctx(nc, st)
        P = Prog(nc)
        wqn = C.sbuf([128, LC, 256], BF16, "wqn")
        wqr = C.sbuf([128, LC, 256], BF16, "wqr")
        wqr2 = C.sbuf([128, LC, 256], BF16, "wqr2")
        wuk = C.sbuf([128, LC, 256], BF16, "wuk_sb")
        wuv = C.sbuf([128, LC, 256], BF16, "wuv_sb")
        cst = C.sbuf([128, 2, 128], BF16, "cst_sb")
        KnT = C.sbuf([128, 2, S], BF16, "KnT")
        krT = C.sbuf([128, S], BF16, "krT2")
        Vs = C.sbuf([128, S // 128, 256], BF16, "Vs")
        QnT = C.sbuf([128, 2, S], BF16, "QnT")
        QrT = C.sbuf([128, 2, S], BF16, "QrT")
        ckb = [C.sbuf([128, LC, 512], BF16, f"ckb{i}") for i in range(2)]
        cqb = [C.sbuf([128, LC, 512], BF16, f"cqb{i}") for i in range(2)]
        csb = [C.sbuf([128, 2, 512], F32, f"csb{i}") for i in range(2)]
        tmp1 = C.sbuf([128, 512], F32, "tmp1")
        tmp2 = C.sbuf([128, 512], F32, "tmp2")
        ptb = [C.sbuf([128, 512], BF16, f"ptb{i}") for i in range(4)]
        rl = C.sbuf([128, 512], F32, "rl")
        oout = [C.sbuf([128, 512], BF16, f"oout{i}") for i in range(2)]
        onesb = C.sbuf([128, 128], BF16, "onesb")
        bank = [C.psum([128, 512], F32, f"bk{i}") for i in range(8)]
        s_w, s_in, s_pj, s_evA, s_evD, s_init = (C.sem(n) for n in ("s_w", "s_in", "s_pj", "s_evA", "s_evD", "s_init"))
        s_qk, s_ex, s_pv, s_fin, s_out, s_acc = (C.sem(n) for n in ("s_qk", "s_ex", "s_pv", "s_fin", "s_out", "s_acc"))
        P.op("dve", "memset", onesb[:, :], 1.0, inc=s_init)
        for dst, src in ((wqn, wqn_d), (wqr, wqr_d), (wqr2, wqr2_d), (wuk, wuk_d), (wuv, wuv_d)):
            P.dma("pool", dst[:, :, :], src, s_w)
        P.dma("pool", cst[:, :, :], cst_d.rearrange("a p c -> p a c"), s_w)
        P.dma("pool", krT[0:64, :], kr_d, s_w)
        P.dma("pool", krT[64:128, :], kr_d, s_w)
        NW = 8 * 16
        ident, tri = cst[:, 0, :], cst[:, 1, :]
        ckv_v = ckv_d.rearrange("(k p) t -> p k t", p=128)
        cq_v = cq_d.rearrange("(k p) t -> p k t", p=128)
        cs_v = cs_d.rearrange("a p t -> p a t")
        NT = S // 512
        P.wait("pe", s_w, NW)
        P.wait("pe", s_init, 1)
        units = []
        n_evA = n_evD = 0
        for tt in range(NT):
            b = tt % 2
            tsl = slice(tt * 512, (tt + 1) * 512)
            if tt >= 2:
                P.wait("sp", s_pj, 10 * (tt - 1))
                P.wait("sp", s_evD, evd_at[tt - 2])
            P.dma("sp", ckb[b][:, :, :], ckv_v[:, :, tsl], s_in)
            P.dma("sp", cqb[b][:, :, :], cq_v[:, :, tsl], s_in)
            P.dma("sp", csb[b][:, :, :], cs_v[:, :, tsl], s_in)
            P.wait("pe", s_in, 48 * (tt + 1))
            if tt == 0:
                evd_at = {}
            for ui in range(10):
                u = len(units)
                bk = bank[u % 8]
                if u >= 8:
                    P.wait("pe", units[u - 8][0], units[u - 8][1])
                if ui in (0, 1, 4, 5):
                    hd = ui % 2
                    w, src, dst = (wuk, ckb[b], KnT) if ui < 2 else (wqn, cqb[b], QnT)
                    for k in range(LC):
                        P.op("pe", "matmul", bk[:, :], w[:, k, hd * 128:(hd + 1) * 128], src[:, k, :],
                             start=(k == 0), stop=(k == LC - 1), inc=s_pj if k == LC - 1 else None)
                    P.wait("act", s_pj, u + 1)
                    P.op("act", "activation", out=dst[:, hd, tsl], in_=bk[:, :], func=AF.Copy, inc=s_evA)
                    n_evA += 1
                    units.append((s_evA, n_evA))
                elif ui in (2, 3):
                    sp_ = ui - 2
                    for si in range(2):
                        sub = sp_ * 2 + si
                        for k in range(LC):
                            last = (k == LC - 1) and si == 1
                            P.op("pe", "matmul", bk[:, si * 256:(si + 1) * 256], ckb[b][:, k, sub * 128:(sub + 1) * 128],
                                 wuv[:, k, :], start=(k == 0), stop=(k == LC - 1), inc=s_pj if last else None)
                    P.wait("dve", s_pj, u + 1)
                    n0 = tt * 4 + sp_ * 2
                    P.op("dve", "tensor_copy", out=Vs[:, n0:n0 + 2, :], in_=bk[:, :].rearrange("p (s c) -> p s c", c=256),
                         inc=s_evD)
                    n_evD += 1
                    units.append((s_evD, n_evD))
                else:
                    hq = (ui - 6) // 2
                    var = (ui - 6) % 2
                    w = wqr if var == 0 else wqr2
                    for k in range(LC):
                        P.op("pe", "matmul", bk[:, :], w[:, k, hq * 128:(hq + 1) * 128], cqb[b][:, k, :],
                             start=(k == 0), stop=(k == LC - 1), inc=s_pj if k == LC - 1 else None)
                    P.wait("dve", s_pj, u + 1)
                    if var == 0:
                        P.op("dve", "tensor_tensor", out=tmp1[:, :], in0=bk[:, :], in1=csb[b][:, 0, :], op=ALU.mult,
                             inc=s_evD)
                        n_evD += 1
                    else:
                        P.op("dve", "tensor_tensor", out=tmp2[:, :], in0=bk[:, :], in1=csb[b][:, 1, :], op=ALU.mult,
                             inc=s_evD)
                        n_evD += 1
                        P.wait("dve", s_evD, n_evD)
                        P.op("dve", "tensor_tensor", out=QrT[:, hq, tsl], in0=tmp1[:, :], in1=tmp2[:, :], op=ALU.add,
                             inc=s_evD)
                        n_evD += 1
                    units.append((s_evD, n_evD))
            evd_at[tt] = n_evD
        P.wait("pe", s_evA, n_evA)
        P.wait("pe", s_evD, n_evD)
        steps = []
        for hd in range(2):
            for qt in range(NT):
                nk = 4 * qt + 4
                for kt in range(nk):
                    steps.append((hd, qt, kt, nk))
        n_acc = 0

        def emit_qk_pair(p):
            g0 = 2 * p
            P.wait("pe", s_ex, g0 + 1 - 3)
            info = []
            for j in range(2):
                g = g0 + j
                hd, qt, kt, nk = steps[g]
                i = kt - 4 * qt
                c0 = 128 * i if i > 0 else 0
                info.append((g, hd, qt, kt, i, c0, bank[g % 4]))
            for (g, hd, qt, kt, i, c0, sb_) in info:
                q0 = qt * 512
                P.op("pe", "matmul", sb_[:, c0:512], KnT[:, hd, kt * 128:(kt + 1) * 128],
                     QnT[:, hd, q0 + c0:q0 + 512], start=True, stop=False)
            for j, (g, hd, qt, kt, i, c0, sb_) in enumerate(info):
                q0 = qt * 512
                rows = slice(64 * j, 64 * (j + 1))
                P.op("pe", "matmul", sb_[:, c0:512], krT[rows, kt * 128:(kt + 1) * 128],
                     QrT[rows, hd, q0 + c0:q0 + 512], start=False, stop=(i < 0),
                     inc=s_qk if (i < 0) else None)
            for (g, hd, qt, kt, i, c0, sb_) in info:
                if i >= 0:
                    P.op("pe", "matmul", sb_[:, c0:c0 + 128], ident, tri, start=False, stop=True, inc=s_qk)
            for (g, hd, qt, kt, i, c0, sb_) in info:
                P.wait("act", s_qk, g0 + 2)
                P.wait("act", s_pv, g - 3)
                P.op("act", "activation", out=ptb[g % 4][:, c0:512], in_=sb_[:, c0:512], func=AF.Exp, scale=scale,
                     inc=s_ex)

        def emit_pv(g):
            nonlocal n_acc
            hd, qt, kt, nk = steps[g]
            i = kt - 4 * qt
            c0 = 128 * i if i > 0 else 0
            a = n_acc
            ob, lb = bank[4 + 2 * (a % 2)], bank[5 + 2 * (a % 2)]
            P.wait("pe", s_ex, g + 1)
            if kt == 0:
                P.wait("pe", s_fin, a - 1)
            P.op("pe", "matmul", ob[:, c0:512], Vs[:, kt, hd * 128:(hd + 1) * 128], ptb[g % 4][:, c0:512],
                 start=(kt == 0), stop=(kt == nk - 1))
            P.op("pe", "matmul", lb[:, c0:512], onesb[:, :], ptb[g % 4][:, c0:512],
                 start=(kt == 0), stop=(kt == nk - 1), inc=s_pv)
            if kt == nk - 1:
                q0 = qt * 512
                P.wait("dve", s_pv, g + 1)
                P.wait("dve", s_out, 16 * (a - 1))
                P.op("dve", "reciprocal", rl[:, :], lb[:, :], inc=s_acc)
                P.wait("dve", s_acc, a + 1)
                P.op("dve", "tensor_tensor", out=oout[a % 2][:, :], in0=ob[:, :], in1=rl[:, :], op=ALU.mult, inc=s_fin)
                P.wait("sp", s_fin, a + 1)
                P.dma("sp", o_d[hd * 128:(hd + 1) * 128, q0:q0 + 512], oout[a % 2][:, :], s_out)
                n_acc += 1

        G = len(steps)
        assert G % 2 == 0
        emit_qk_pair(0)
        for p in range(G // 2):
            if p + 1 < G // 2:
                emit_qk_pair(p + 1)
            emit_pv(2 * p)
            emit_pv(2 * p + 1)
        P.wait("sp", s_out, s_out.n)
        P.run()
    return nc


def rope_tables_ext(pos):
    inv = (1.0 / (10000.0 ** (np.arange(0, 64, 2, dtype=np.float32) / np.float32(64)))).astype(np.float32)
    ang = pos.astype(np.float32)[None, :] * inv[:, None]
    cos, sin = np.cos(ang).astype(np.float32), np.sin(ang).astype(np.float32)
    return np.concatenate([cos, cos], 0), np.concatenate([-sin, sin], 0)


def mla_consts():
    k = np.arange(128)[:, None]
    j = np.arange(128)[None, :]
    tri = np.where(k <= j, 0.0, -30000.0).astype(np.float32)
    return np.stack([np.eye(128, dtype=np.float32), tri]).astype(ml_dtypes.bfloat16)


def til(w, ncols=128):
    din, dout = w.shape
    return np.ascontiguousarray(w.reshape(din // 128, 128, dout // ncols, ncols).transpose(2, 1, 0, 3))


def build_token_program(wo=False, ffn2=False, latent=False, ffn1=False, mix=None, final=False):
    from contextlib import ExitStack
    nc = bass.Bass("TRN2", target_bir_lowering=False)

    def din(name, shape, dt=F32):
        return nc.dram_tensor(name, list(shape), dt, kind="ExternalInput").ap()

    def dout(name, shape, dt=F32):
        return nc.dram_tensor(name, list(shape), dt, kind="ExternalOutput").ap()

    x_d = din("xT_in", [D, T])
    y_d = dout("xT_out", [D, T])
    gnames = []
    if ffn2:
        gnames.append("g_ffn2")
    if latent:
        gnames += ["g_kv"]
    if ffn1:
        gnames.append("g_ffn1")
    if mix:
        gnames.append("g_mix")
    if final:
        gnames.append("g_final")
    g_d = {n: din(n, [128, KC]) for n in gnames}
    if wo:
        o_d = din("oT_in", [D, T], BF16)
        wo_d = din("wo", [KC, 128, KC, 128])
    if ffn2:
        w2 = (din("f2_wg", [FC, 128, KC, 128]), din("f2_wu", [FC, 128, KC, 128]), din("f2_wd", [2, KC, 128, FH, 128]))
    if ffn1:
        w1 = (din("f1_wg", [FC, 128, KC, 128]), din("f1_wu", [FC, 128, KC, 128]), din("f1_wd", [2, KC, 128, FH, 128]))
    if latent:
        wdkv_d = din("wdkv", [LC, 128, KC, 128])
        gckv_d = din("g_ckv", [128, LC])
        wkr_d = din("wkr", [2, 128, KC, 64])
        cs_d = din("cs_tok", [2, 64, T])
        ckv_o = dout("ckvT_out", [512, T], BF16)
        kr_o = dout("krT_out", [64, T], BF16)
    if mix == "cq":
        wdq_d = din("wdq", [LC, 128, KC, 128])
        gcq_d = din("g_cq", [128, LC])
        cq_o = dout("cqT_out", [512, T], BF16)
    if mix == "h":
        h_o = dout("hT_out", [D, T], BF16)

    with ExitStack() as st:
        C = Ctx(nc, st)
        P = Prog(nc)
        tp = TokenPhase(nc, P, C)
        tp.epsb = C.sbuf([128, 1], F32, "epsb")
        P.op("dve", "memset", tp.epsb[:, :], EPS, inc=tp.s_init)
        P.wait("act", tp.s_init, 2)
        g_sb = {n: C.sbuf([128, KC], F32, n + "_sb") for n in gnames}
        tp.load_x(x_d)
        for n in gnames:
            tp.load(g_sb[n][:, :], g_d[n][:, :])
        if latent or mix == "cq":
            lat = C.sbuf([128, LC, T], F32, "lat")
            latb = tp.aT[:, 0:LC, :]
            s_misc = C.sem("s_misc")
        if latent:
            gckv = C.sbuf([128, LC], F32, "gckv_sb")
            cs_sb = C.sbuf([64, 2, T], F32, "cs_sb")
            krb = tp.aT[0:64, LC + 1, :]
            tp.load(gckv[:, :], gckv_d[:, :])
            tp.load(cs_sb[:, :, :], cs_d.rearrange("a p t -> p a t"))
        if mix == "cq":
            gcq = C.sbuf([128, LC], F32, "gcq_sb")
            tp.load(gcq[:, :], gcq_d[:, :])
        n_lat_out = 0
        if wo:
            tp.load(tp.aT[:, 0:KC, :], o_d.rearrange("(k p) t -> p k t", p=128))
            for e in ("dve",):
                P.wait(e, tp.s_x, tp.s_x.n)
            tp.add_proj(tp.aT, KC, [wo_d[dc] for dc in range(KC)])
        if ffn2:
            tp.norm(g_sb["g_ffn2"], tp.hT)
            tp.ffn(*w2)
        if latent:
            tp.norm(g_sb["g_kv"], tp.hT)
            tp.proj_to(tp.hT, KC, [wdkv_d[oc] for oc in range(LC)], lat)
            tp.norm(gckv, latb, src=lat, nk=LC)
            P.wait("sp", tp.s_h, tp.s_h.n)
            P.dma("sp", ckv_o.rearrange("(k p) t -> p k t", p=128), latb, tp.s_out)
            n_lat_out = tp.s_out.n
            tA = lat[0:64, 0, :]
            tB = lat[0:64, 1, :]
            P.wait("dve", tp.s_h, tp.s_h.n)

            def kr_evac(v, th, tsl, bk, inc):
                P.op("dve", "tensor_tensor", out=(tA if v == 0 else tB)[:, tsl], in0=bk[0:64, :],
                     in1=cs_sb[:, v, tsl], op=ALU.mult, inc=inc)
            tp.proj_units(tp.hT, KC, [wkr_d[0], wkr_d[1]], 64, kr_evac)
            P.wait("dve", tp.s_y, tp.s_y.n)
            P.op("dve", "tensor_tensor", out=krb, in0=tA, in1=tB, op=ALU.add, inc=s_misc)
            P.wait("sp", s_misc, s_misc.n)
            P.dma("sp", kr_o, krb, tp.s_out)
        if ffn1:
            tp.norm(g_sb["g_ffn1"], tp.hT)
            tp.ffn(*w1)
        if not final:
            tp.store_x(y_d)
        if mix:
            tp.norm(g_sb["g_mix"], tp.hT)
        if mix == "h":
            P.wait("sp", tp.s_h, tp.s_h.n)
            hv = h_o.rearrange("(k p) t -> p k t", p=128)
            for k in range(0, KC, 4):
                P.dma("sp", hv[:, k:k + 4, :], tp.hT[:, k:k + 4, :], tp.s_out)
        if mix == "cq":
            if latent:
                P.wait("dve", tp.s_out, n_lat_out)
                P.wait("dve", s_misc, s_misc.n)
            tp.proj_to(tp.hT, KC, [wdq_d[oc] for oc in range(LC)], lat)
            tp.norm(gcq, latb, src=lat, nk=LC)
            P.wait("sp", tp.s_h, tp.s_h.n)
            P.dma("sp", cq_o.rearrange("(k p) t -> p k t", p=128), latb, tp.s_out)
        if final:
            tp.norm(g_sb["g_final"], tp.xT)
            P.wait("sp", tp.s_h, tp.s_h.n)
            tp.store_x(y_d)
        tp.finish()
        P.run()
    return nc


_PROGS = {}


def _prog(key, fn, **kw):
    if key not in _PROGS:
        _PROGS[key] = fn(**kw)
    return _PROGS[key]


def _launch(nc, maps):
    res = run_bass_kernel_spmd(nc, maps, core_ids=list(range(NCORES)))
    return res.results


def _f32(a):
    return np.ascontiguousarray(np.asarray(a, dtype=np.float32))


def kernel(x, ffn_norm1, ffn1_wg, ffn1_wu, ffn1_wd, mix_norm, ffn_norm2, ffn2_wg, ffn2_wu, ffn2_wd,
           a_wqkv, a_wo, kv_norm, b_wdkv, b_ckv_norm, b_wkr, b_wuk, b_wuv,
           b_wdq, b_cq_norm, b_wuq, b_wo, final_norm):
    x = _f32(x)
    xT = np.ascontiguousarray(x[0].T)
    xs = [np.ascontiguousarray(xT[:, c * T:(c + 1) * T]) for c in range(NCORES)]
    swap = (np.arange(64) + 32) % 64

    def ffn_w(l, which):
        wg, wu, wd = (ffn1_wg, ffn1_wu, ffn1_wd) if which == 1 else (ffn2_wg, ffn2_wu, ffn2_wd)
        p = "f1_" if which == 1 else "f2_"
        return {p + "wg": tile_wgu(_f32(wg[l])), p + "wu": tile_wgu(_f32(wu[l])), p + "wd": tile_wd(_f32(wd[l]))}

    def gc(g):
        return gcol_of(_f32(g))

    def gath_tok(res, name):
        return np.ascontiguousarray(np.concatenate([np.asarray(r[name]) for r in res], axis=1))

    def tok_shards(full):
        return [np.ascontiguousarray(full[:, c * T:(c + 1) * T]) for c in range(NCORES)]

    com = dict(ffn_w(0, 1), g_ffn1=gc(ffn_norm1[0]), g_mix=gc(mix_norm[0]))
    nc = _prog("T_first", build_token_program, ffn1=True, mix="h")
    res = _launch(nc, [dict(com, xT_in=xs[c]) for c in range(NCORES)])
    xs = [np.asarray(r["xT_out"]) for r in res]
    hT_all = gath_tok(res, "hT_out")
    oT = None
    for l in range(DEPTH):
        if l < 2:
            wqkv = _f32(a_wqkv[l])

            def hw(w, c):
                return np.ascontiguousarray(w[:, c * 256:(c + 1) * 256].reshape(KC, 128, 256).transpose(1, 0, 2))
            nc = _prog("A", build_attn_a_program)
            hT_t = tile_hT(hT_all)
            maps = [{"hT_all": hT_t, "wq": hw(wqkv[:, 0:2048], c), "wk": hw(wqkv[:, 2048:4096], c),
                     "wv": hw(wqkv[:, 4096:6144], c), "emask": alibi_masks([2 * c, 2 * c + 1])}
                    for c in range(NCORES)]
            res = _launch(nc, maps)
            wo_l = _f32(a_wo[l])
        else:
            jb = l - 2
            wuq, wuk, wuv = _f32(b_wuq[jb]), _f32(b_wuk), _f32(b_wuv)
            ce, se = rope_tables_ext(np.arange(S))
            cs = np.stack([np.concatenate([ce, ce], 0), np.concatenate([se, se], 0)])

            def t4(w):
                return np.ascontiguousarray(w.reshape(LC, 128, -1).transpose(1, 0, 2))
            nc = _prog("M", build_mla_program)
            maps = []
            for c in range(NCORES):
                hs = [2 * c, 2 * c + 1]
                maps.append({"cqT_all": cqT_all, "ckvT_all": ckvT_all, "krT_all": krT_all,
                             "wuq_n": t4(np.concatenate([wuq[:, h, :128] for h in hs], 1)),
                             "wuq_r": t4(np.concatenate([wuq[:, h, 128:] for h in hs for _ in range(2)], 1)),
                             "wuq_r2": t4(np.concatenate([wuq[:, h, 128:][:, swap] for h in hs for _ in range(2)], 1)),
                             "wuk": t4(np.concatenate([wuk[:, h] for h in hs], 1)),
                             "wuv": t4(np.concatenate([wuv[:, h] for h in hs], 1)),
                             "cs": cs, "cst": mla_consts()})
            res = _launch(nc, maps)
            wo_l = _f32(b_wo[jb])
        oT_full = np.ascontiguousarray(np.concatenate([np.asarray(r["oT"]) for r in res], axis=0))
        oTs = tok_shards(oT_full)
        com = dict(ffn_w(l, 2), wo=til(wo_l), g_ffn2=gc(ffn_norm2[l]))
        if l == DEPTH - 1:
            nc = _prog("T_last", build_token_program, wo=True, ffn2=True, final=True)
            com["g_final"] = gc(final_norm)
            res = _launch(nc, [dict(com, xT_in=xs[c], oT_in=oTs[c]) for c in range(NCORES)])
            outT = np.concatenate([np.asarray(r["xT_out"]) for r in res], axis=1)
            return np.ascontiguousarray(outT.T)[None].astype(np.float32)
        com.update(ffn_w(l + 1, 1))
        com["g_ffn1"] = gc(ffn_norm1[l + 1])
        com["g_mix"] = gc(mix_norm[l + 1])
        per_core = [dict(xT_in=xs[c], oT_in=oTs[c]) for c in range(NCORES)]
        if l + 1 < 2:
            nc = _prog("T_mid_h", build_token_program, wo=True, ffn2=True, ffn1=True, mix="h")
        else:
            jb = l + 1 - 2
            com["wdq"] = til(_f32(b_wdq[jb]))
            com["g_cq"] = np.ascontiguousarray(_f32(b_cq_norm[jb]).reshape(LC, 128).T)
            if l + 1 == 2:
                nc = _prog("T_mid_lat", build_token_program, wo=True, ffn2=True, latent=True, ffn1=True, mix="cq")
                com["g_kv"] = gc(kv_norm)
                com["wdkv"] = til(_f32(b_wdkv))
                com["g_ckv"] = np.ascontiguousarray(_f32(b_ckv_norm).reshape(LC, 128).T)
                wkr = _f32(b_wkr)
                com["wkr"] = np.stack([til(wkr, 64)[0], til(np.ascontiguousarray(wkr[:, swap]), 64)[0]])
                for c in range(NCORES):
                    ce, se = rope_tables_ext(np.arange(c * T, (c + 1) * T))
                    per_core[c]["cs_tok"] = np.stack([ce, se])
            else:
                nc = _prog("T_mid_cq", build_token_program, wo=True, ffn2=True, ffn1=True, mix="cq")
        res = _launch(nc, [dict(com, **per_core[c]) for c in range(NCORES)])
        xs = [np.asarray(r["xT_out"]) for r in res]
        if l + 1 < 2:
            hT_all = gath_tok(res, "hT_out")
        else:
            cqT_all = gath_tok(res, "cqT_out")
            if l + 1 == 2:
                ckvT_all = gath_tok(res, "ckvT_out")
                krT_all = gath_tok(res, "krT_out")
```
